# Optimizing a Trainium2 kernel written in Bass

```python
import math
import jax, jax.numpy as jnp
from jax import lax
import numpy as np

D_MODEL = 1024
BATCH = 2
SEQ = 8192
DEPTH = 4
DEC_BATCH = 128
DEC_SEQ = 1
PAST_LEN = 8192
PAGE_SIZE = 128

D_MIX = D_MODEL
HEAD_DIM = 64
A_WIDTH = D_MIX // 4
A_HEADS = A_WIDTH // HEAD_DIM
CHUNK = 128
B_WIDTH = D_MIX // 4
ATT_HEADS = B_WIDTH // HEAD_DIM
KV_HEADS = 2
WINDOW = 128
C_WIDTH = D_MIX // 2
SSM_HEADS = C_WIDTH // HEAD_DIM
SSM_HEAD_DIM = HEAD_DIM
SSM_GROUPS = 2
D_STATE = 128
CONV_W = 4
SSD_CHUNK = 128
CONV_DIM = C_WIDTH + 2 * SSM_GROUPS * D_STATE
D_FF = 4 * D_MODEL
D_IN = 2 * A_WIDTH + B_WIDTH + 2 * KV_HEADS * HEAD_DIM + C_WIDTH + CONV_DIM + SSM_HEADS
ALPHA = (2 * DEPTH) ** 0.25
BETA = (8 * DEPTH) ** -0.25
LN_EPS = 1e-5
RMS_EPS = 1e-6

kernel_name = 'hybrid_sgu_swa_ssd_decoder_step'


def _layer_norm(x, g, b):
    xf = x.astype(jnp.float32)
    mu = jnp.mean(xf, axis=-1, keepdims=True)
    var = jnp.mean(jnp.square(xf - mu), axis=-1, keepdims=True)
    return ((xf - mu) * lax.rsqrt(var + LN_EPS) * g.astype(jnp.float32) + b.astype(jnp.float32)).astype(x.dtype)


def _split_proj(proj):
    sizes = (A_WIDTH, A_WIDTH, B_WIDTH, KV_HEADS * HEAD_DIM, KV_HEADS * HEAD_DIM, C_WIDTH, CONV_DIM, SSM_HEADS)
    bounds, acc = [], 0
    for s in sizes[:-1]:
        acc += s
        bounds.append(acc)
    return jnp.split(proj, bounds, axis=-1)


def _spatial_gate(u, vn, w_s, b_s):
    bn, L, _ = u.shape
    lc = min(L, CHUNK)
    nc = L // lc
    causal = jnp.tril(jnp.ones((lc, lc), dtype=bool))
    w = jnp.where(causal, w_s[:, :lc, :lc], 0.0)
    vc = vn.reshape(bn, nc, lc, A_HEADS, HEAD_DIM)
    mix = jnp.einsum('hts,bcshd->bcthd', w, vc) + b_s[:, :lc].T[:, :, None]
    return u * mix.reshape(bn, L, A_WIDTH)


def _sink_attend(qb, kb, vb, mask, sinks):
    g = ATT_HEADS // KV_HEADS
    scores = jnp.einsum('bntkgd,bnskd->bnkgts', qb, kb).astype(jnp.float32) * (HEAD_DIM ** -0.5)
    scores = jnp.where(mask[None, :, None, None], scores, -jnp.inf)
    sink = jnp.broadcast_to(sinks.astype(jnp.float32).reshape(1, 1, KV_HEADS, g, 1, 1), scores.shape[:-1] + (1,))
    probs = jax.nn.softmax(jnp.concatenate([scores, sink], axis=-1), axis=-1)[..., :-1]
    return jnp.einsum('bnkgts,bnskd->bntkgd', probs.astype(vb.dtype), vb)


def _window_attn_prompt(q, k, v, sinks):
    bn, L = q.shape[:2]
    nb = L // WINDOW
    g = ATT_HEADS // KV_HEADS
    qb = q.reshape(bn, nb, WINDOW, KV_HEADS, g, HEAD_DIM)

    def band(t):
        cur = t.reshape(bn, nb, WINDOW, KV_HEADS, HEAD_DIM)
        prev = jnp.concatenate([jnp.zeros_like(cur[:, :1]), cur[:, :-1]], axis=1)
        return jnp.concatenate([prev, cur], axis=2)

    t = jnp.arange(WINDOW)[:, None]
    s = jnp.arange(2 * WINDOW)[None, :]
    blk = jnp.arange(nb)[:, None, None]
    mask = (s > t) & (s <= t + WINDOW) & ((blk > 0) | (s >= WINDOW))
    out = _sink_attend(qb, band(k), band(v), mask, sinks)
    return out.reshape(bn, L, B_WIDTH)


def _window_attn_sample(q, k, v, k_buf, v_buf, sinks):
    bn, L = q.shape[:2]
    g = ATT_HEADS // KV_HEADS
    qb = q.reshape(bn, 1, L, KV_HEADS, g, HEAD_DIM)
    kb = jnp.concatenate([k_buf.astype(k.dtype), k], axis=1)[:, None]
    vb = jnp.concatenate([v_buf.astype(v.dtype), v], axis=1)[:, None]
    t = jnp.arange(L)[:, None]
    s = jnp.arange(WINDOW + L)[None, :]
    mask = ((s > t) & (s <= t + WINDOW))[None]
    out = _sink_attend(qb, kb, vb, mask, sinks)
    return out.reshape(bn, L, B_WIDTH)


def _causal_conv(xbc, buf, conv_w, conv_b):
    L = xbc.shape[1]
    full = jnp.concatenate([buf.astype(xbc.dtype), xbc], axis=1)
    acc = conv_b
    for i in range(CONV_W):
        acc = acc + full[:, i:i + L] * conv_w[i]
    return jax.nn.silu(acc), full[:, L:]


def _ssd(x, dt, a, bm, cm, h0):
    bn, L = x.shape[:2]
    hg = SSM_HEADS // SSM_GROUPS
    lc = min(L, SSD_CHUNK)
    nc = L // lc

    def chunks(t):
        return jnp.moveaxis(t.reshape((bn, nc, lc) + t.shape[2:]), 1, 0)

    xs = chunks(x.reshape(bn, L, SSM_GROUPS, hg, SSM_HEAD_DIM))
    dts = chunks(dt.reshape(bn, L, SSM_GROUPS, hg))
    bs = chunks(bm)
    cs = chunks(cm)
    a_g = a.reshape(SSM_GROUPS, hg)
    causal = jnp.tril(jnp.ones((lc, lc), dtype=bool))[None, :, :, None, None]

    def step(h, inp):
        xc, dtc, bc, cc = inp
        cum = jnp.cumsum(dtc * a_g, axis=1)
        diff = cum[:, :, None] - cum[:, None, :]
        decay = jnp.where(causal, jnp.exp(jnp.where(causal, diff, 0.0)), 0.0)
        cb = jnp.einsum('btgn,bsgn->btsg', cc, bc)
        y = jnp.einsum('btsgh,bsghp->btghp', cb[..., None] * decay * dtc[:, None], xc)
        y = y + jnp.einsum('btgn,bghpn->btghp', cc, h) * jnp.exp(cum)[..., None]
        to_end = jnp.exp(cum[:, -1:] - cum) * dtc
        h = h * jnp.exp(cum[:, -1])[..., None, None] + jnp.einsum('bsgh,bsghp,bsgn->bghpn', to_end, xc, bc)
        return h, y

    h, ys = lax.scan(step, h0.reshape(bn, SSM_GROUPS, hg, SSM_HEAD_DIM, D_STATE), (xs, dts, bs, cs))
    y = jnp.moveaxis(ys, 0, 1).reshape(bn, L, SSM_HEADS, SSM_HEAD_DIM)
    return y, h.reshape(bn, SSM_HEADS, SSM_HEAD_DIM, D_STATE)


def _hybrid_layer(x, k_buf, v_buf, conv_buf, ssm_h, w_in, ln_v_g, ln_v_b, w_s, b_s, sinks,
                  conv_w, conv_b, dt_bias, a_log, d_skip, gn_w, w_out,
                  ln1_g, ln1_b, w1, w2, ln2_g, ln2_b):
    bn, L, _ = x.shape
    f32 = jnp.float32
    proj = jnp.einsum('bld,de->ble', x, w_in)
    a_u, a_v, q, k, v, z, xbc, dt_raw = _split_proj(proj)

    vn = _layer_norm(jax.nn.gelu(a_v), ln_v_g, ln_v_b)
    out_a = _spatial_gate(jax.nn.gelu(a_u), vn, w_s, b_s)

    q = q.reshape(bn, L, ATT_HEADS, HEAD_DIM)
    k = k.reshape(bn, L, KV_HEADS, HEAD_DIM)
    v = v.reshape(bn, L, KV_HEADS, HEAD_DIM)
    if k_buf is None:
        out_b = _window_attn_prompt(q, k, v, sinks)
        k_new, v_new = k[:, -WINDOW:], v[:, -WINDOW:]
    else:
        out_b = _window_attn_sample(q, k, v, k_buf, v_buf, sinks)
        k_new, v_new = k, v

    xbc_act, conv_new = _causal_conv(xbc, conv_buf, conv_w, conv_b)
    xs, bm, cm = jnp.split(xbc_act, [C_WIDTH, C_WIDTH + SSM_GROUPS * D_STATE], axis=-1)
    xs_h = xs.reshape(bn, L, SSM_HEADS, SSM_HEAD_DIM).astype(f32)
    dt = jax.nn.softplus(dt_raw.astype(f32) + dt_bias.astype(f32))
    a = -jnp.exp(a_log.astype(f32))
    y, ssm_new = _ssd(xs_h, dt, a,
                      bm.reshape(bn, L, SSM_GROUPS, D_STATE).astype(f32),
                      cm.reshape(bn, L, SSM_GROUPS, D_STATE).astype(f32),
                      ssm_h.astype(f32))
    y = (y + d_skip.astype(f32)[:, None] * xs_h).reshape(bn, L, C_WIDTH) * jax.nn.silu(z.astype(f32))
    out_c = (y * lax.rsqrt(jnp.mean(jnp.square(y), axis=-1, keepdims=True) + RMS_EPS) * gn_w.astype(f32)).astype(x.dtype)

    mix = jnp.einsum('ble,ed->bld', jnp.concatenate([out_a, out_b, out_c], axis=-1), w_out)
    x = _layer_norm(ALPHA * x + mix, ln1_g, ln1_b)
    hid = jnp.square(jax.nn.relu(jnp.einsum('bld,df->blf', x, w1)))
    x = _layer_norm(ALPHA * x + jnp.einsum('blf,fd->bld', hid, w2), ln2_g, ln2_b)
    return x, k_new, v_new, conv_new, ssm_new.astype(conv_buf.dtype), vn


def setup_inputs(seed: int = 0) -> dict:
    key = jax.random.key(seed)
    ks = jax.random.split(key, 32)
    nrm = jax.random.normal
    dt0 = jnp.exp(jax.random.uniform(ks[13], (DEPTH, SSM_HEADS), minval=math.log(1e-3), maxval=math.log(1e-1)))
    return {
        'x_prompt': nrm(ks[0], (BATCH, SEQ, D_MODEL), jnp.float32),
        'x_sample': nrm(ks[1], (DEC_BATCH, DEC_SEQ, D_MODEL), jnp.float32),
        'state_attn_k': nrm(ks[2], (DEPTH, DEC_BATCH, WINDOW, KV_HEADS, HEAD_DIM), jnp.float32),
        'state_attn_v': nrm(ks[3], (DEPTH, DEC_BATCH, WINDOW, KV_HEADS, HEAD_DIM), jnp.float32),
        'state_conv': nrm(ks[4], (DEPTH, DEC_BATCH, CONV_W - 1, CONV_DIM), jnp.float32),
        'state_ssm': 0.3 * nrm(ks[5], (DEPTH, DEC_BATCH, SSM_HEADS, SSM_HEAD_DIM, D_STATE), jnp.float32),
        'w_in': nrm(ks[6], (DEPTH, D_MODEL, D_IN), jnp.float32) * D_MODEL ** -0.5,
        'ln_v_g': 1.0 + 0.05 * nrm(ks[7], (DEPTH, A_WIDTH), jnp.float32),
        'ln_v_b': 0.02 * nrm(ks[8], (DEPTH, A_WIDTH), jnp.float32),
        'w_s': nrm(ks[9], (DEPTH, A_HEADS, CHUNK, CHUNK), jnp.float32) * CHUNK ** -0.5,
        'b_s': 1.0 + 0.05 * nrm(ks[10], (DEPTH, A_HEADS, CHUNK), jnp.float32),
        'sinks': 0.5 * nrm(ks[11], (DEPTH, ATT_HEADS), jnp.float32),
        'conv_w': nrm(ks[12], (DEPTH, CONV_W, CONV_DIM), jnp.float32) * CONV_W ** -0.5,
        'conv_b': 0.02 * nrm(ks[14], (DEPTH, CONV_DIM), jnp.float32),
        'dt_bias': dt0 + jnp.log(-jnp.expm1(-dt0)),
        'a_log': jnp.log(jax.random.uniform(ks[15], (DEPTH, SSM_HEADS), minval=1.0, maxval=16.0)),
        'd_skip': 1.0 + 0.05 * nrm(ks[16], (DEPTH, SSM_HEADS), jnp.float32),
        'gn_w': 1.0 + 0.05 * nrm(ks[17], (DEPTH, C_WIDTH), jnp.float32),
        'w_out': nrm(ks[18], (DEPTH, D_MIX, D_MODEL), jnp.float32) * (D_MIX ** -0.5) * BETA,
        'ln1_g': 1.0 + 0.05 * nrm(ks[19], (DEPTH, D_MODEL), jnp.float32),
        'ln1_b': 0.02 * nrm(ks[20], (DEPTH, D_MODEL), jnp.float32),
        'w1': nrm(ks[21], (DEPTH, D_MODEL, D_FF), jnp.float32) * D_MODEL ** -0.5,
        'w2': nrm(ks[22], (DEPTH, D_FF, D_MODEL), jnp.float32) * (D_FF ** -0.5) * BETA,
        'ln2_g': 1.0 + 0.05 * nrm(ks[23], (DEPTH, D_MODEL), jnp.float32),
        'ln2_b': 0.02 * nrm(ks[24], (DEPTH, D_MODEL), jnp.float32),
    }


def reference(x_prompt, x_sample, state_attn_k, state_attn_v, state_conv, state_ssm,
              w_in, ln_v_g, ln_v_b, w_s, b_s, sinks, conv_w, conv_b, dt_bias, a_log, d_skip,
              gn_w, w_out, ln1_g, ln1_b, w1, w2, ln2_g, ln2_b):
    yp, ys = x_prompt, x_sample
    kp, vp, cp, hp, ksm, vsm, csm, hsm, vns = [], [], [], [], [], [], [], [], []
    conv0 = jnp.zeros((x_prompt.shape[0], CONV_W - 1, CONV_DIM), x_prompt.dtype)
    ssm0 = jnp.zeros((x_prompt.shape[0], SSM_HEADS, SSM_HEAD_DIM, D_STATE), state_ssm.dtype)
    for l in range(DEPTH):
        wts = (w_in[l], ln_v_g[l], ln_v_b[l], w_s[l], b_s[l], sinks[l], conv_w[l], conv_b[l],
               dt_bias[l], a_log[l], d_skip[l], gn_w[l], w_out[l],
               ln1_g[l], ln1_b[l], w1[l], w2[l], ln2_g[l], ln2_b[l])
        yp, k_new, v_new, c_new, h_new, _ = _hybrid_layer(yp, None, None, conv0, ssm0, *wts)
        kp.append(k_new); vp.append(v_new); cp.append(c_new); hp.append(h_new)
        ys, k_new, v_new, c_new, h_new, vn_new = _hybrid_layer(
            ys, state_attn_k[l], state_attn_v[l], state_conv[l], state_ssm[l], *wts)
        ksm.append(k_new); vsm.append(v_new); csm.append(c_new); hsm.append(h_new); vns.append(vn_new)
    return (yp, ys,
            jnp.stack(kp), jnp.stack(vp), jnp.stack(cp), jnp.stack(hp),
            jnp.stack(ksm), jnp.stack(vsm), jnp.stack(csm), jnp.stack(hsm), jnp.stack(vns))
```

```python
import math
import numpy as np
from contextlib import ExitStack
import concourse.bass as bass
import concourse.mybir as mybir
from concourse.bass_utils import run_bass_kernel_spmd

F32 = mybir.dt.float32
BF16 = mybir.dt.bfloat16
ALU = mybir.AluOpType
AF = mybir.ActivationFunctionType
AX = mybir.AxisListType

D = 1024
DEPTH = 4
SEQ = 8192
NCORES = 8
NS = 16
DIN = 2568
DFF = 4096
ALPHA = (2 * DEPTH) ** 0.25
LN_EPS = 1e-5
RMS_EPS = 1e-6
NEG = -30000.0
NROW = 1068
NCOLP = 80

ENGS = ("pe", "act", "dve", "pool", "sp")


class Buf:
    __slots__ = ("name", "last_w", "rd_eng", "rd_dma")

    def __init__(self, name):
        self.name = name
        self.last_w = None
        self.rd_eng = {}
        self.rd_dma = []


class Op:
    __slots__ = ("eng", "fn", "deps", "is_dma", "sig", "tok", "idx", "vc", "inc")

    def __init__(self, eng, fn, is_dma, inc):
        self.eng = eng
        self.fn = fn
        self.is_dma = is_dma
        self.deps = set()
        self.sig = is_dma
        self.tok = None
        self.vc = None
        self.inc = inc


class Prog:
    def __init__(self, nc, n_dma_slots=10):
        self.nc = nc
        self.ops = []
        self.n_dma_slots = n_dma_slots
        self.live_dma = []

    def op(self, eng, fn, r=(), w=(), dma=False, inc=None):
        o = Op(eng, fn, dma, inc if inc is not None else (16 if dma else 1))
        o.idx = len(self.ops)
        deps = o.deps
        for b in r:
            lw = b.last_w
            if lw is not None:
                if lw.is_dma or dma or lw.eng != eng or eng != "pe":
                    deps.add(lw)
        for b in w:
            lw = b.last_w
            if lw is not None and (lw.is_dma or dma or lw.eng != eng or eng != "pe"):
                deps.add(lw)
            for e, ro in b.rd_eng.items():
                if dma or e != eng or eng != "pe":
                    deps.add(ro)
            for ro in b.rd_dma:
                deps.add(ro)
        for b in w:
            b.last_w = o
            b.rd_eng = {}
            b.rd_dma = []
        for b in r:
            if dma:
                b.rd_dma.append(o)
            else:
                b.rd_eng[eng] = o
        deps.discard(o)
        self.ops.append(o)
        if dma:
            self.live_dma.append(o)
        return o

    def barrier(self):
        self.barrier_fn(self)

    def emit(self, stack):
        nc = self.nc
        ops = self.ops
        for o in ops:
            for d in o.deps:
                d.sig = True
        esem = {e: stack.enter_context(nc.semaphore("s_" + e)) for e in ENGS}
        dsem = {}
        for q in ("sp", "act", "pool"):
            dsem[q] = [stack.enter_context(nc.semaphore("d_%s%d" % (q, i))) for i in range(self.n_dma_slots)]
        ecnt = {e: 0 for e in ENGS}
        dcnt = {q: 0 for q in dsem}
        duse = {q: [0] * self.n_dma_slots for q in dsem}
        dlast = {q: [None] * self.n_dma_slots for q in dsem}
        ccsem = stack.enter_context(nc.semaphore("s_cc"))
        cccnt = 0
        for o in ops:
            if o.is_dma and o.inc != 16:
                cccnt += o.inc
                o.tok = (ccsem, cccnt)
            elif o.is_dma:
                q = o.eng
                slot = dcnt[q] % self.n_dma_slots
                dcnt[q] += 1
                duse[q][slot] += o.inc
                if dlast[q][slot] is not None:
                    o.deps.add(dlast[q][slot])
                dlast[q][slot] = o
                o.tok = (dsem[q][slot], duse[q][slot])
            elif o.sig:
                ecnt[o.eng] += 1
                o.tok = (esem[o.eng], ecnt[o.eng])
        know = {e: {} for e in ENGS}
        streams = {e: [] for e in ENGS}
        nwaits = 0
        for o in ops:
            k = know[o.eng]
            st = streams[o.eng]
            for d in sorted(o.deps, key=lambda x: x.idx):
                sem, val = d.tok
                if k.get(sem, 0) < val:
                    st.append((0, sem, val))
                    nwaits += 1
                    for s2, v2 in d.vc.items():
                        if k.get(s2, 0) < v2:
                            k[s2] = v2
            st.append((1, o))
            if o.sig:
                vc = dict(k)
                vc[o.tok[0]] = o.tok[1]
                o.vc = vc
        self.stats = dict(n_ops=len(ops), n_waits=nwaits, per_eng={e: len(s) for e, s in streams.items()})

        def run(engine, st):
            for it in st:
                if it[0] == 0:
                    engine.wait_ge(it[1], it[2])
                else:
                    o = it[1]
                    if o.fn is None:
                        continue
                    ins = o.fn(engine)
                    if o.sig:
                        ins.then_inc(o.tok[0], o.inc)

        with nc.Block() as block:
            @block.tensor
            def _(e):
                run(e, streams["pe"])

            @block.scalar
            def _(e):
                run(e, streams["act"])

            @block.vector
            def _(e):
                run(e, streams["dve"])

            @block.gpsimd
            def _(e):
                run(e, streams["pool"])

            @block.sync
            def _(e):
                run(e, streams["sp"])


class _Stop(Exception):
    pass


def build(L, NT, use_cc=True, stage=None, debug=False):
    TP = NT * 128
    TX = TP + NS
    TA = 256
    NTA = TP // TA
    FT = min(1024, TP)
    NFT = TP // FT
    FTX = FT + NS
    nc = bass.Bass("TRN2", target_bir_lowering=False)
    P = Prog(nc)

    def din(name, shape, dt=F32):
        return nc.dram_tensor(name, list(shape), dt, kind="ExternalInput").ap()

    def dout(name, shape, dt=F32):
        return nc.dram_tensor(name, list(shape), dt, kind="ExternalOutput").ap()

    def dint(name, shape, dt=F32):
        return nc.dram_tensor(name, list(shape), dt, kind="Internal").ap()

    xT0 = din("xT0", [128, 8, TX])
    xh0 = din("xh0", [128, 8, 128])
    win = din("win", [L, 128, 8, DIN])
    wout = din("wout", [L, 128, 8, D])
    w1 = din("w1", [L, 8, 128, 8, 512])
    w2 = din("w2", [L, 8, 128, 32, 128])
    rowp = din("rowp", [L, 1, NROW])
    colp = din("colp", [L, 128, NCOLP])
    wsT = din("wsT", [L, 128, 4, 128])
    bsd = din("bsd", [L, 4, 128])
    cst_f = din("cst_f", [128, 3, 128])
    cst_m = din("cst_m", [128, 3, 512])
    selp = din("selp", [128, 8])
    st_k = din("st_k", [L, NS, 128, 128])
    st_v = din("st_v", [L, NS, 128, 128])
    st_c = din("st_c", [L, NS * 3, 1024])
    st_h = din("st_h", [L, NS * 8, 64 * 128])

    yT = dout("yT", [128, 8, TX])
    kP = dout("kP", [L, 128, 128])
    vP = dout("vP", [L, 128, 128])
    cP = dout("cP", [L, 3, 1024])
    hP = dout("hP", [L, 128, 512])
    kS = dout("kS", [L, NS, 128])
    vS = dout("vS", [L, NS, 128])
    cS = dout("cS", [L, NS, 3, 1024])
    hS = dout("hS", [L, NS * 8, 64 * 128])
    vnS = dout("vnS", [L, NS, 256])

    cc_st_in = dint("cc_st_in", [128, 520])
    cc_st_out = dint("cc_st_out", [4 * 128, 520])
    cc_x_in = dint("cc_x_in", [128, 1024])
    cc_x_out = dint("cc_x_out", [4 * 128, 1024])
    sbn = dint("sbn", [L, NS, 1048])
    sbx = dint("sbx", [L, NS, 512])
    sbB = dint("sbB", [L, NS, 2, 4, 128])
    sbC = dint("sbC", [L, NS, 2, 4, 128])
    sby = dint("sby", [L, NS * 8, 64])
    sbd = dint("sbd", [L, NS, 8])
    xsc = dint("xsc", [128, 6, NT * 128], BF16)
    B_ccst = Buf("ccst"); B_ccst_o = Buf("ccsto"); B_ccx = Buf("ccx"); B_ccx_o = Buf("ccxo")
    B_sbn = Buf("sbn"); B_sbx = Buf("sbx"); B_sbB = Buf("sbB"); B_sbC = Buf("sbC"); B_sby = Buf("sby"); B_sbd = Buf("sbd")
    B_kS = Buf("kS"); B_vS = Buf("vS")
    B_xsc = [Buf("xsc%d" % i) for i in range(NT // 2)]
    outbufs = []

    with ExitStack() as st:
        def sb(name, shape, dt=F32):
            return st.enter_context(nc.sbuf_tensor(name, list(shape), dt))

        def ps(name, shape, dt=F32):
            return st.enter_context(nc.psum_tensor(name, list(shape), dt))

        X = sb("X", [128, 8, TX]); B_X = [Buf("X%d" % i) for i in range(NTA + 1)]
        identf = sb("identf", [128, 128]); Uf = sb("Uf", [128, 128]); onesf = sb("onesf", [128, 128])
        identb = sb("identb", [128, 128], BF16); onesb = sb("onesb", [128, 128], BF16)
        negA = sb("negA", [128, 512], BF16); negA1 = sb("negA1", [128, 512], BF16); negS = sb("negS", [128, 512], BF16)
        sel = sb("sel", [128, 8]); epsc = sb("epsc", [128, 4]); zerosb = sb("zerosb", [128, 512], BF16)
        B_c = Buf("consts")
        rowt = sb("rowt", [128, NROW]); colt = sb("colt", [128, NCOLP]); B_par = Buf("par")
        wsTb = sb("wsTb", [128, 4, 128], BF16); bsb = sb("bsb", [128, 2, 128])
        Dg = sb("Dg", [128, 4, 128], BF16)
        a_bc = sb("a_bc", [128, 8]); esink = sb("esink", [128, 4]); B_der = Buf("der")
        hst = sb("hst", [128, 512]); hb = sb("hb", [128, 512], BF16); B_h = Buf("h"); B_hb = Buf("hb")
        halo3 = sb("halo3", [128, 8, 3]); B_halo3 = Buf("halo3")
        kT = sb("kT", [128, 128 + TA], BF16); B_kT = Buf("kT")
        vaug = sb("vaug", [128, 3, 2, 65], BF16); B_va = [Buf("va%d" % i) for i in range(3)]
        xh = sb("xh", [128, 8, 128]); B_xh = Buf("xh")
        ostage = sb("ostage", [128, 256]); B_os = Buf("ostage")
        sm = sb("sm", [128, 16, 16]); B_sm = Buf("sm")

        RA_BYTES = 116 * 1024
        RA = sb("RA", [128, RA_BYTES // 4])
        ra_off = [0]

        def carve(shape, dt=F32, reset=None):
            if reset is not None:
                ra_off[0] = reset
            n = 1
            for s_ in shape[1:]:
                n *= s_
            esz = 4 if dt == F32 else 2
            nb = (n * esz + 31) // 32 * 32
            lo = ra_off[0] // 4
            ap = RA[0:shape[0], lo:lo + nb // 4]
            ra_off[0] += nb
            assert ra_off[0] <= RA_BYTES, ("RA overflow", ra_off[0])
            if dt != F32:
                ap = ap.bitcast(dt)
            ap = ap[:, 0:n]
            if len(shape) == 3:
                ap = ap.rearrange("p (a b) -> p a b", a=shape[1])
            elif len(shape) == 4:
                ap = ap.rearrange("p (a b c) -> p a b c", a=shape[1], b=shape[2])
            return ap

        winb = carve([128, 8, DIN], BF16, reset=0); B_win = Buf("win")
        woutb = carve([128, 8, D], BF16); B_wout = Buf("wout")
        xb = carve([128, 8, TA], BF16); B_xb = Buf("xb")
        xhb = xb[:, :, 0:128]; B_xhb = B_xb
        junk = carve([128, 512], BF16); B_junk = Buf("junk")
        workA = ra_off[0]
        G16 = ra_off[0]; B_g16 = Buf("g16")
        gx = carve([128, 4, 1024]); B_gx = B_g16
        gst = carve([128, 4, 520], reset=G16); B_gst = B_g16
        wsTf = carve([128, 4, 128]);
        cstage = carve([128, 3, 512], reset=G16)
        xbB = carve([128, 8, TA], BF16, reset=G16); gauB = carve([128, 2, TA]); qTB = carve([128, 2, TA], BF16); xactB = carve([128, 8, TA], BF16)
        assert ra_off[0] <= G16 + 16384
        Dcv = carve([128, 6, 4, 128], BF16, reset=G16)
        rawb = [carve([128, TA + 4], BF16) for _ in range(2)]; B_rawb = [Buf("rawb0"), Buf("rawb1")]
        assert ra_off[0] <= G16 + 8320
        ra_off[0] = G16 + 16384
        gau = carve([128, 2, TA]); B_gau = Buf("gau")
        qT = carve([128, 2, TA], BF16); B_qT = Buf("qT")
        raw = [carve([128, TA + 3]) for _ in range(2)]; B_raw = [Buf("raw0"), Buf("raw1")]
        acc = carve([128, TA]); B_acc = Buf("acc")
        xact = carve([128, 8, TA], BF16); B_xact = Buf("xact")
        sqb = xact; B_sqb = B_xact
        gav = carve([128, 256]); B_gav = Buf("gav")
        vn = carve([128, 256], BF16); B_vn = Buf("vn")
        vnf = carve([128, 256]); B_vnf = Buf("vnf")
        sz = carve([128, 2, 512]); B_sz = [Buf("sz0"), Buf("sz1")]
        Eb = [carve([128, 512], BF16) for _ in range(2)]; B_E = [Buf("E0"), Buf("E1")]
        ob = carve([128, 256], BF16); B_ob = Buf("ob")
        Rt = carve([128, 1024]); B_R = Buf("R")
        LT = carve([128, 8, 128], BF16); B_LT = Buf("LT")
        Gt = carve([128, 8, 128], BF16); B_G = Buf("G")
        xstok = carve([128, 512], BF16); B_xstok = Buf("xstok")
        Btok = carve([128, 256], BF16); B_Btok = Buf("Btok")
        xw = carve([128, 512], BF16); B_xw = Buf("xw")
        yt1 = carve([128, 512]); B_yt1 = Buf("yt1")
        yt2 = carve([128, 512]); B_yt2 = Buf("yt2")
        nmean = yt1; B_nm = B_yt1
        lvar = yt2; B_lv = B_yt2
        ocb = carve([128, 512], BF16); B_oc = Buf("oc")
        catT = carve([128, 8, TA], BF16); B_cat = Buf("cat")
        xb2 = [xb, xbB]; B_xb2 = [B_xb, Buf("xbB")]
        gau2 = [gau, gauB]; B_gau2 = [B_gau, Buf("gauB")]
        qT2 = [qT, qTB]; B_qT2 = [B_qT, Buf("qTB")]
        xact2 = [xact, xactB]; B_xact2 = [B_xact, Buf("xactB")]
        cpst = sz.rearrange("p a b -> p (a b)")
        endA = ra_off[0]
        s_tok = carve([NS, DIN], reset=workA); B_stok = Buf("stok")
        s_pk = carve([NS, 1048]); B_spk = Buf("spk")
        s_misc = carve([NS, 1024]); B_smisc = Buf("smisc")
        s_act = carve([NS, 1024]); B_sact = Buf("sact")
        s_cat = carve([NS, 1024], BF16); B_scat = Buf("scat")
        s_catT = carve([128, 8, NS], BF16); B_scatT = Buf("scatT")
        s_q = carve([128, 2, NS], BF16); B_sq = Buf("s_q")
        s_E = carve([128, 64], BF16); B_sE = Buf("sE")
        s_rd = carve([128, 64]); B_srd = Buf("srd")
        s_sq = carve([128, 8, NS], BF16); B_ssq = Buf("ssq")
        s_nm = carve([128, 16]); B_snm = Buf("snm")
        s_lv = carve([128, 16]); B_slv = Buf("slv")
        sampB = ra_off[0]
        s_kb = carve([128, NS, 128], BF16); B_skb = Buf("skb")
        s_vb = carve([128, NS, 128], BF16); B_svb = Buf("svb")
        s_kT = carve([128, NS, 128], BF16); B_skT = Buf("skT")
        s_hc = carve([NS * 3, 1024]); B_shc = Buf("shc")
        s_hist = carve([128, 8, NS * 3]); B_shist = Buf("shist")
        s_xfm = carve([128, 8, NS]); B_sxfm = Buf("sxfm")
        s_cv = carve([128, 8, NS]); B_scv = Buf("scv")
        s_cv2 = carve([128, 8, NS]); B_scv2 = Buf("scv2")
        endS1 = ra_off[0]
        s_h = carve([128, 32, 128], reset=sampB); B_sh = Buf("sh")
        s_tmp = carve([128, 16, 128]); B_stmp = Buf("stmp")
        s_x8 = carve([128, 328]); B_sx8 = Buf("sx8")
        s_y8 = carve([128, 64]); B_sy8 = Buf("sy8")
        endS2 = ra_off[0]
        hT = carve([128, 32, FTX], BF16, reset=0); B_hT = Buf("hT")
        w1_off = ra_off[0]
        w1b = [carve([128, 8, 256], BF16) for _ in range(2)]; B_w1b = [Buf("w1b0"), Buf("w1b1")]
        assert ra_off[0] - w1_off == 8192
        sqbB = RA[:, w1_off // 4:w1_off // 4 + 2048].bitcast(BF16).rearrange("p (a b) -> p a b", a=8)
        w2_off = ra_off[0]
        w2b = [carve([128, 32, 128], BF16) for _ in range(2)]; B_w2b = [Buf("w2b0"), Buf("w2b1")]
        assert ra_off[0] - w2_off == 16384
        lnq = RA[:, w2_off // 4:w2_off // 4 + 2048].bitcast(BF16).rearrange("p (a b) -> p a b", a=8)
        lnx = RA[:, w2_off // 4 + 2048:w2_off // 4 + 4096].bitcast(BF16).rearrange("p (a b) -> p a b", a=8)
        x1b = carve([128, 8, FTX], BF16); B_x1b = Buf("x1b")
        rt = [carve([128, 512], BF16) for _ in range(2)]; B_rt = [Buf("rt0"), Buf("rt1")]
        nmeanB = carve([128, 512]); lvarB = carve([128, 512]); B_nmB = Buf('nmB'); B_lvB = Buf('lvB')
        endB = ra_off[0]
        carve_stats = dict(endA=endA, endS1=endS1, endS2=endS2, endB=endB)

        pD = [ps("pD%d" % i, [128, 512]) for i in range(2)]; B_pD = [Buf("pD0"), Buf("pD1")]
        pT0 = ps("pT0", [128, 512]); B_pT0 = Buf("pT0")
        pT1 = ps("pT1", [128, 512]); B_pT1 = Buf("pT1")
        pS = [ps("pS%d" % i, [128, 512]) for i in range(2)]; B_pS = [Buf("pS0"), Buf("pS1")]
        pY = ps("pY", [128, 512]); B_pY = Buf("pY")
        pMT = ps("pMT", [128, 512])
        pM = pMT[:, 0:256]; B_pM = Buf("pM")
        pTP = pMT[:, 256:512].bitcast(BF16); B_pTP = Buf("pTP")

        def mm(out, lhsT, rhs, start, r, w, stop=True):
            P.op("pe", lambda e: e.matmul(out, lhsT=lhsT, rhs=rhs, start=start, stop=stop, skip_group_check=True), r=r, w=w)

        def tr(out, in_, ident, r, w):
            P.op("pe", lambda e: e.transpose(out, in_, ident), r=r, w=w)

        def act(out, in_, func, r, w, **kw):
            P.op("act", lambda e: e.activation(out=out, in_=in_, func=func, **kw), r=r, w=w)

        def vop(eng, name, r, w, **kw):
            P.op(eng, lambda e: getattr(e, name)(**kw), r=r, w=w)

        def dma(q, out, in_, r, w, **kw):
            P.op(q, lambda e: e.dma_start(out=out, in_=in_, **kw), r=r, w=w, dma=True)

        def odma(out, in_, r):
            b = Buf("o%d" % len(outbufs))
            outbufs.append(b)
            dma("sp", out, in_, r, [b])
            return b

        def dbg(name, ap, r, dt=F32):
            if not debug:
                return
            t_ = nc.dram_tensor("dbg_" + name, list(ap.shape), dt, kind="ExternalOutput").ap()
            odma(t_, ap, r)

        def smt(slot, n=16):
            return sm[:, slot, 0:n]

        bar_sb = sb("bar_sb", [128, 8]); bar_d = dint("bar_d", [128, 2])

        def _barrier(P_):
            b0, b1, b2, b3, b4 = Buf("b0"), Buf("b1"), Buf("b2"), Buf("b3"), Buf("b4")
            mm(pM[:, 0:2], zerosb[:, 0:128], zerosb[:, 0:2], True, [B_c], [B_pM, b0])
            act(bar_sb[:, 0:2], pM[:, 0:2], AF.Copy, [B_pM, b0], [b1])
            vop("dve", "tensor_copy", [b1], [b2], out=bar_sb[:, 2:4], in_=bar_sb[:, 0:2])
            vop("pool", "tensor_copy", [b2], [b3], out=bar_sb[:, 4:6], in_=bar_sb[:, 2:4])
            o = P_.op("sp", lambda e: e.dma_start(out=bar_d, in_=bar_sb[:, 4:6]), r=[b3], w=[b4], dma=True)
            o.deps.update(x for x in P_.live_dma if x is not o)
            P_.live_dma = []
            for e in ENGS:
                P_.op(e, None, r=[b4])
        P.barrier_fn = _barrier

        dq = [0]

        def pDn():
            dq[0] ^= 1
            return pD[dq[0]], B_pD[dq[0]]

        def v8(ap):
            return ap.rearrange("p (h d) -> p h d", h=8)

        dma("sp", cstage[:, :, 0:128], cst_f, [], [B_c, B_g16])
        for dst, j in ((identf, 0), (Uf, 1), (onesf, 2), (identb, 0), (onesb, 2)):
            vop("dve", "tensor_copy", [B_c], [B_c], out=dst[:], in_=cstage[:, j, 0:128])
        dma("sp", cstage, cst_m, [B_c], [B_c, B_g16])
        for dst, j in ((negA, 0), (negA1, 1), (negS, 2)):
            vop("dve", "tensor_copy", [B_c], [B_c], out=dst[:], in_=cstage[:, j, :])
        dma("sp", sel[:], selp, [], [B_c])
        vop("dve", "memset", [], [B_c], ap=epsc[:, 0:1], constant=LN_EPS)
        vop("dve", "memset", [], [B_c], ap=zerosb[:], constant=0.0)
        vop("dve", "memset", [], [B_c], ap=epsc[:, 1:2], constant=RMS_EPS)
        vop("dve", "memset", [], [B_c], ap=epsc[:, 2:3], constant=1.0)
        for i in range(NTA):
            dma("sp", X[:, :, i * TA:(i + 1) * TA], xT0[:, :, i * TA:(i + 1) * TA], [], [B_X[i]])
        dma("sp", X[:, :, TP:TX], xT0[:, :, TP:TX], [], [B_X[NTA]])
        dma("sp", xh[:], xh0, [], [B_xh])
        P.barrier()

        def layer_norm_pieces(cols, n, Bxs, g_off, b_off, xnb_ap, sqb_ap, Bxnb, Bsq, nm_ap, lv_ap, B_nm, B_lv, bk0=None, bk1=None):
            Xs = X[:, :, cols]
            q0, Bq0 = bk0 if bk0 is not None else (pT0, B_pT0)
            q1, Bq1 = bk1 if bk1 is not None else (pT1, B_pT1)
            nm = nm_ap[:, 0:n]; lv = lv_ap[:, 0:n]

            def s1():
                act(xnb_ap, Xs, AF.Copy, Bxs, Bxnb)
                act(sqb_ap, Xs, AF.Square, Bxs, Bsq)

            def s2():
                for kc in range(8):
                    mm(q0[:, 0:n], onesb[:], xnb_ap[:, kc, :], kc == 0, Bxnb + [B_c], [Bq0], stop=(kc == 7))
                for kc in range(8):
                    mm(q1[:, 0:n], onesb[:], sqb_ap[:, kc, :], kc == 0, Bsq + [B_c], [Bq1], stop=(kc == 7))
                vop("dve", "tensor_scalar", [Bq0], [B_nm], out=nm, in0=q0[:, 0:n], scalar1=-1.0 / D, scalar2=None, op0=ALU.mult)
                vop("dve", "tensor_tensor", [B_nm], [B_lv], out=lv, in0=nm, in1=nm, op=ALU.mult)
                vop("dve", "scalar_tensor_tensor", [Bq1, B_lv], [B_lv], out=lv, in0=q1[:, 0:n], scalar=1.0 / D, in1=lv, op0=ALU.mult, op1=ALU.subtract)
                act(lv, lv, AF.Ln, [B_lv, B_c], [B_lv], bias=epsc[:, 0:1])
                act(lv, lv, AF.Exp, [B_lv], [B_lv], scale=-0.5)

            def s3():
                vop("dve", "tensor_tensor", Bxs + [B_nm], Bxs, out=Xs, in0=Xs, in1=nm.unsqueeze(1).broadcast_to([128, 8, n]), op=ALU.add)
                vop("dve", "tensor_tensor", Bxs + [B_lv], Bxs, out=Xs, in0=Xs, in1=lv.unsqueeze(1).broadcast_to([128, 8, n]), op=ALU.mult)

            def s4():
                for kc in range(8):
                    vop("dve", "tensor_scalar", Bxs + [B_par], Bxs, out=X[:, kc, cols], in0=X[:, kc, cols],
                        scalar1=colt[:, g_off + kc:g_off + kc + 1], scalar2=colt[:, b_off + kc:b_off + kc + 1], op0=ALU.mult, op1=ALU.add)
            return [s1, s2, s3, s4]

        def layer_norm_fm(*a, **k):
            for pc in layer_norm_pieces(*a, **k):
                pc()

        def ck(k):
            if stage is not None and stage == k:
                raise _Stop()

        for l in range(L):
          try:
            last = (l == L - 1)
            can_prefetch = (32 * FTX * 2 >= 8 * DIN * 2 + 8 * D * 2)
            if l == 0 or not can_prefetch:
                for kc in range(8):
                    dma("pool", winb[:, kc, :], win[l, :, kc, :], [], [B_win])
                dma("pool", woutb, wout[l], [], [B_wout])
            dma("sp", rowt[:], rowp[l].partition_broadcast(128), [], [B_par])
            dma("sp", colt[:], colp[l], [], [B_par])
            dma("sp", wsTf, wsT[l], [], [B_par, B_g16])
            for h in range(4):
                dma("sp", bsb[(h % 2) * 64:(h % 2) * 64 + 64, h // 2, :], bsd[l, h:h + 1, :].partition_broadcast(64), [], [B_par])
            vop("dve", "tensor_tensor", [B_par, B_c, B_g16], [B_der], out=wsTb[:], in0=wsTf, in1=Uf[:].unsqueeze(1).broadcast_to([128, 4, 128]), op=ALU.mult)
            act(a_bc[:], rowt[:, 1032:1040], AF.Exp, [B_par], [B_der])
            vop("dve", "tensor_scalar", [B_der], [B_der], out=a_bc[:], in0=a_bc[:], scalar1=-1.0, scalar2=None, op0=ALU.mult)
            act(esink[:], rowt[:, 1040:1044], AF.Exp, [B_par], [B_der])
            for j in range(4):
                vop("dve", "tensor_scalar", [B_par, B_c], [B_der], out=Dg[:, j, :], in0=identf[:], scalar1=colt[:, 72 + j:73 + j], scalar2=None, op0=ALU.mult)
            for s_ in range(3):
                vop("pool", "memset", [], [B_va[s_]], ap=vaug[:, s_, :, 64:65], constant=1.0)
            if use_cc:
                for cc in range(6):
                    for t in range(4):
                        vop("dve", "tensor_scalar", [B_par, B_c, B_g16], [B_g16], out=Dcv[:, cc, t, :], in0=identf[:],
                            scalar1=colt[:, 32 + cc * 4 + t:33 + cc * 4 + t], scalar2=None, op0=ALU.mult)
            g_bc = rowt[:, 0:256]; bv_bc = rowt[:, 256:512]; gn_bc = rowt[:, 512:1024]; dtb_bc = rowt[:, 1024:1032]

            def halo_project():
                act(xhb, xh[:], AF.Copy, [B_xh], [B_xhb])
                p, Bp = pDn()
                for kc in range(8):
                    mm(p[:, 0:128], winb[:, kc, 512:640], xhb[:, kc, :], kc == 0, [B_win, B_xhb], [Bp], stop=(kc == 7))
                act(kT[:, 0:128], p[:, 0:128], AF.Copy, [Bp], [B_kT])
                p, Bp = pDn()
                for kc in range(8):
                    mm(p[:, 0:128], xhb[:, kc, :], winb[:, kc, 1920:2048], kc == 0, [B_win, B_xhb], [Bp], stop=(kc == 7))
                vop("dve", "tensor_copy", [Bp], [B_va[0]], out=vaug[:, 0, :, 0:64], in_=p[:, 0:128].rearrange("p (a b) -> p a b", a=2))
                for cc in range(8):
                    p, Bp = pDn()
                    for kc in range(8):
                        mm(p[:, 0:4], winb[:, kc, 640 + cc * 128:768 + cc * 128], xhb[:, kc, 124:128], kc == 0, [B_win, B_xhb], [Bp], stop=(kc == 7))
                    vop("dve", "tensor_copy", [Bp], [B_halo3], out=halo3[:, cc, :], in_=p[:, 1:4])

            def cast_tile(i, par=0):
                act(xb2[par], X[:, :, i * TA:(i + 1) * TA], AF.Copy, [B_X[i]], [B_xb2[par]])

            def conv_chunk(cc, ri, par=0):
                p, Bp = pDn()
                j = 5 + cc
                for kc in range(8):
                    mm(p[:, 0:TA], winb[:, kc, j * 128:(j + 1) * 128], xb2[par][:, kc, :], kc == 0, [B_win, B_xb2[par]], [Bp], stop=(kc == 7))
                rw = raw[ri]; Br = B_raw[ri]
                act(rw[:, 3:3 + TA], p[:, 0:TA], AF.Copy, [Bp], [Br])
                vop("dve", "tensor_copy", [B_halo3], [Br], out=rw[:, 0:3], in_=halo3[:, cc, :])
                vop("dve", "tensor_copy", [Br], [B_halo3], out=halo3[:, cc, :], in_=rw[:, TA:TA + 3])

                def cw(t):
                    return colt[:, 32 + cc * 4 + t:33 + cc * 4 + t]
                vop("dve", "tensor_scalar", [Br, B_par], [B_acc], out=acc, in0=rw[:, 0:TA], scalar1=cw(0), scalar2=None, op0=ALU.mult)
                for t in range(1, 4):
                    vop("dve", "scalar_tensor_tensor", [Br, B_par, B_acc], [B_acc], out=acc, in0=rw[:, t:t + TA], scalar=cw(t), in1=acc, op0=ALU.mult, op1=ALU.add)
                act(xact2[par][:, cc, :], acc, AF.Silu, [B_acc, B_par], [B_xact2[par]], bias=colt[:, 64 + cc:65 + cc])

            pp_banks = [(pD[0], B_pD[0]), (pD[1], B_pD[1]), (pT0, B_pT0), (pT1, B_pT1)]
            pp_i = [0]

            def pp_bank():
                pp_i[0] = (pp_i[0] + 1) % 4
                return pp_banks[pp_i[0]]

            def conv_pe_a(cc, ri):
                p, Bp = pp_bank()
                j = 5 + cc
                for kc in range(8):
                    mm(p[:, 0:TA], winb[:, kc, j * 128:(j + 1) * 128], xb[:, kc, :], kc == 0, [B_win, B_xb], [Bp], stop=(kc == 7))
                rw = rawb[ri]; Br = B_rawb[ri]
                act(rw[:, 3:3 + TA], p[:, 0:TA], AF.Copy, [Bp, B_g16], [Br])
                vop("pool", "tensor_copy", [B_halo3], [Br], out=rw[:, 0:3], in_=halo3[:, cc, :])
                vop("pool", "tensor_copy", [Br], [B_halo3], out=halo3[:, cc, :], in_=rw[:, TA:TA + 3])

            def conv_pe_b(cc, ri):
                rw = rawb[ri]; Br = B_rawb[ri]
                p2, Bp2 = pp_bank()
                for t in range(4):
                    mm(p2[:, 0:TA], Dcv[:, cc, t, :], rw[:, t:t + TA], t == 0, [B_g16, Br], [Bp2], stop=(t == 3))
                act(xact[:, cc, :], p2[:, 0:TA], AF.Silu, [Bp2, B_par], [B_xact], bias=colt[:, 64 + cc:65 + cc])

            def dt_chain(nch):
                n = nch * 8

                def v3(s_):
                    return sm[:, s_, 0:n].rearrange("p (c h) -> p c h", c=nch)
                vop("dve", "tensor_tensor", [B_sm, B_par], [B_sm], out=v3(0), in0=v3(0), in1=dtb_bc.unsqueeze(1).broadcast_to([128, nch, 8]), op=ALU.add)
                vop("dve", "scalar_tensor_tensor", [B_sm], [B_sm], out=smt(1, n), in0=smt(0, n), scalar=-1.0, in1=smt(0, n), op0=ALU.mult, op1=ALU.min)
                act(smt(1, n), smt(1, n), AF.Exp, [B_sm], [B_sm])
                act(smt(1, n), smt(1, n), AF.Ln, [B_sm, B_c], [B_sm], bias=epsc[:, 2:3])
                vop("dve", "scalar_tensor_tensor", [B_sm], [B_sm], out=smt(1, n), in0=smt(0, n), scalar=0.0, in1=smt(1, n), op0=ALU.max, op1=ALU.add)
                vop("dve", "tensor_tensor", [B_sm, B_der], [B_sm], out=v3(2), in0=v3(1), in1=a_bc[:].unsqueeze(1).broadcast_to([128, nch, 8]), op=ALU.mult)
                act(smt(3, n), smt(1, n), AF.Ln, [B_sm], [B_sm])
                mm(pY[:, 0:n], Uf[:], smt(2, n), True, [B_sm, B_c], [B_pY])
                mm(pY[:, 16:16 + n], onesf[:], smt(2, n), False, [B_sm, B_c], [B_pY])
                mm(pY[:, 0:32], zerosb[:, 0:128], zerosb[:, 0:32], False, [B_c], [B_pY])
                act(smt(13, n), pY[:, 0:n], AF.Copy, [B_pY], [B_sm])
                act(smt(14, n), pY[:, 16:16 + n], AF.Copy, [B_pY], [B_sm])
                act(smt(4, n), smt(13, n), AF.Exp, [B_sm], [B_sm])
                vop("dve", "tensor_tensor", [B_sm], [B_sm], out=smt(5, n), in0=smt(3, n), in1=smt(13, n), op=ALU.subtract)
                vop("dve", "tensor_tensor", [B_sm], [B_sm], out=smt(6, n), in0=smt(5, n), in1=smt(14, n), op=ALU.add)
                act(smt(6, n), smt(6, n), AF.Exp, [B_sm], [B_sm])
                act(smt(7, n), smt(14, n), AF.Exp, [B_sm], [B_sm])

            def tok_transposes(c, par=0):
                cs = slice(c * 128, (c + 1) * 128)
                for cc in range(4):
                    tr(pTP[:, cc * 128:(cc + 1) * 128], xact2[par][:, cc, cs], identb[:], [B_xact2[par], B_c], [B_pTP])
                act(xstok, pTP[:, 0:512], AF.Copy, [B_pTP], [B_xstok])
                for g in range(2):
                    tr(pTP[:, g * 128:(g + 1) * 128], xact2[par][:, 4 + g, cs], identb[:], [B_xact2[par], B_c], [B_pTP])
                vop("dve", "tensor_copy", [B_pTP], [B_Btok], out=Btok, in_=pTP[:, 0:256])

            def state_update(c, want_hb=True):
                k8 = slice(c * 8, c * 8 + 8)
                vop("pool", "tensor_tensor", [B_xstok, B_sm], [B_xw], out=v8(xw), in0=v8(xstok), in1=sm[:, 6, k8].unsqueeze(2).broadcast_to([128, 8, 64]), op=ALU.mult)
                for g in range(2):
                    mm(pY[:, g * 256:(g + 1) * 256], Btok[:, g * 128:(g + 1) * 128], xw[:, g * 256:(g + 1) * 256], g == 0, [B_Btok, B_xw], [B_pY], stop=(g == 1))
                vop("dve", "tensor_tensor", [B_h, B_sm], [B_h], out=v8(hst[:]), in0=v8(hst[:]), in1=sm[:, 7, k8].unsqueeze(2).broadcast_to([128, 8, 64]), op=ALU.mult)
                vop("dve", "tensor_tensor", [B_h, B_pY], [B_h], out=hst[:], in0=hst[:], in1=pY[:], op=ALU.add)
                if want_hb:
                    act(hb[:], hst[:], AF.Copy, [B_h], [B_hb])

            ck(1)
            halo_project()
            vop("dve", "memset", [], [B_h], ap=hst[:], constant=0.0)
            if use_cc:
                for i in range(NTA):
                    cast_tile(i)
                    conv_pe_a(0, 0)
                    for cc in range(6):
                        if cc + 1 < 6:
                            conv_pe_a(cc + 1, (cc + 1) % 2)
                        conv_pe_b(cc, cc % 2)
                    nl_ = len(P.live_dma)
                    dma("sp", xsc[:, :, i * TA:(i + 1) * TA], xact[:, 0:6, :], [B_xact], [B_xsc[i]])
                    del P.live_dma[nl_:]
                    for c in range(2):
                        p, Bp = pDn()
                        for kc in range(8):
                            mm(p[:, 0:8], xb[:, kc, c * 128:(c + 1) * 128], winb[:, kc, 2048:2056], kc == 0, [B_win, B_xb], [Bp], stop=(kc == 7))
                        vop("dve", "tensor_copy", [Bp], [B_sm], out=sm[:, 0, c * 8:c * 8 + 8], in_=p[:, 0:8])
                    dt_chain(2)
                    for c in range(2):
                        tok_transposes(c)
                        state_update(c, want_hb=False)
                    if i == 0:
                        vop("dve", "tensor_tensor", [B_sm], [B_sm], out=sm[:, 8, 0:8], in0=sm[:, 7, 0:8], in1=sm[:, 7, 8:16], op=ALU.mult)
                    else:
                        vop("dve", "tensor_tensor", [B_sm], [B_sm], out=sm[:, 8, 0:8], in0=sm[:, 8, 0:8], in1=sm[:, 7, 0:8], op=ALU.mult)
                        vop("dve", "tensor_tensor", [B_sm], [B_sm], out=sm[:, 8, 0:8], in0=sm[:, 8, 0:8], in1=sm[:, 7, 8:16], op=ALU.mult)
                ck(2)
                dma("sp", cc_st_in[:, 0:512], hst[:], [B_h], [B_ccst])
                dma("sp", cc_st_in[:, 512:520], sm[:, 8, 0:8], [B_sm], [B_ccst])
                P.op("pool", lambda e: e.collective_compute("AllGather", ALU.bypass, replica_groups=[[0, 1, 2, 3], [4, 5, 6, 7]],
                                                            ins=[cc_st_in], outs=[cc_st_out]), r=[B_ccst], w=[B_ccst_o], dma=True, inc=1)
                dma("sp", gst, cc_st_out.rearrange("(r p) f -> p r f", p=128), [B_ccst_o], [B_gst])
                vop("dve", "memset", [], [B_h], ap=hst[:], constant=0.0)
                for r_ in range(3):
                    vop("dve", "tensor_tensor", [B_h, B_gst], [B_yt1], out=v8(yt1), in0=v8(hst[:]), in1=gst[:, r_, 512:520].unsqueeze(2).broadcast_to([128, 8, 64]), op=ALU.mult)
                    vop("dve", "tensor_tensor", [B_yt1, B_gst], [B_yt1], out=yt1, in0=yt1, in1=gst[:, r_, 0:512], op=ALU.add)
                    vop("dve", "tensor_tensor", [B_yt1, B_h], [B_yt1], out=yt1, in0=yt1, in1=hst[:], op=ALU.subtract)
                    vop("dve", "scalar_tensor_tensor", [B_yt1, B_h, B_c], [B_h], out=hst[:], in0=yt1, scalar=sel[:, r_:r_ + 1], in1=hst[:], op0=ALU.mult, op1=ALU.add)
                halo_project()
            act(hb[:], hst[:], AF.Copy, [B_h], [B_hb])

            ck(3)
            P.barrier()
            vslot = [0]

            def stage_proj(i, par):
                pieces = []
                pieces.append(lambda: cast_tile(i, par))

                def p_gau(j):
                    p, Bp = pDn()
                    for kc in range(8):
                        mm(p[:, 0:TA], winb[:, kc, j * 128:(j + 1) * 128], xb2[par][:, kc, :], kc == 0, [B_win, B_xb2[par]], [Bp], stop=(kc == 7))
                    act(gau2[par][:, j, :], p[:, 0:TA], AF.Gelu_apprx_tanh, [Bp], [B_gau2[par]])

                def p_q(j):
                    p, Bp = pDn()
                    for kc in range(8):
                        mm(p[:, 0:TA], winb[:, kc, (2 + j) * 128:(3 + j) * 128], xb2[par][:, kc, :], kc == 0, [B_win, B_xb2[par]], [Bp], stop=(kc == 7))
                    act(qT2[par][:, j, :], p[:, 0:TA], AF.Copy, [Bp], [B_qT2[par]], scale=0.125)

                def p_k():
                    p, Bp = pDn()
                    for kc in range(8):
                        mm(p[:, 0:TA], winb[:, kc, 512:640], xb2[par][:, kc, :], kc == 0, [B_win, B_xb2[par]], [Bp], stop=(kc == 7))
                    if i > 0:
                        vop("pool", "tensor_copy", [B_kT], [B_kT], out=kT[:, 0:128], in_=kT[:, TA:TA + 128])
                    act(kT[:, 128:128 + TA], p[:, 0:TA], AF.Copy, [Bp], [B_kT])
                for j in range(2):
                    pieces.append(lambda j=j: p_gau(j))
                for j in range(2):
                    pieces.append(lambda j=j: p_q(j))
                pieces.append(p_k)
                if use_cc:
                    def p_load():
                        nl_ = len(P.live_dma)
                        dma("sp", xact2[par][:, 0:6, :], xsc[:, :, i * TA:(i + 1) * TA], [B_xsc[i]], [B_xact2[par]])
                        del P.live_dma[nl_:]
                    pieces.insert(1, p_load)
                    for cc in range(6, 8):
                        pieces.append(lambda cc=cc: conv_chunk(cc, cc % 2, par))
                else:
                    for cc in range(8):
                        pieces.append(lambda cc=cc: conv_chunk(cc, cc % 2, par))
                return pieces

            pending = []

            def fill(n=1):
                for _ in range(n):
                    if pending:
                        pending.pop(0)()

            def stage_attn(i, par):
                for c in range(2):
                    cs = slice(c * 128, (c + 1) * 128)
                    gtok = i * 2 + c
                    for kc in range(8):
                        mm(pT0[:, 0:392], xb2[par][:, kc, cs], winb[:, kc, 1664:2056], kc == 0, [B_win, B_xb2[par]], [B_pT0], stop=(kc == 7))
                    for kc in range(8):
                        mm(pT1[:], xb2[par][:, kc, cs], winb[:, kc, 2056:2568], kc == 0, [B_win, B_xb2[par]], [B_pT1], stop=(kc == 7))
                    act(gav, pT0[:, 0:256], AF.Gelu_apprx_tanh, [B_pT0], [B_gav, B_sm], accum_out=sm[:, 9, 0:1])
                    prv = vslot[0]
                    cur = (vslot[0] + 1) % 3
                    vslot[0] = cur
                    vop("dve", "tensor_copy", [B_pT0], [B_va[cur]], out=vaug[:, cur, :, 0:64], in_=pT0[:, 256:384].rearrange("p (a b) -> p a b", a=2))
                    vop("dve", "tensor_copy", [B_pT0], [B_sm], out=sm[:, 0, c * 8:c * 8 + 8], in_=pT0[:, 384:392])
                    act(sz[:, c, :], pT1[:], AF.Silu, [B_pT1], [B_sz[c]])
                    if gtok == NT - 1:
                        vop("dve", "tensor_copy", [B_pT0], [B_os], out=ostage[:, 0:128], in_=pT0[:, 256:384])
                        odma(vP[l], ostage[:, 0:128], [B_os])
                        p, Bp = pDn()
                        for kc in range(8):
                            mm(p[:, 0:128], xb2[par][:, kc, cs], winb[:, kc, 512:640], kc == 0, [B_win, B_xb2[par]], [Bp], stop=(kc == 7))
                        vop("dve", "tensor_copy", [Bp], [B_os], out=ostage[:, 128:256], in_=p[:, 0:128])
                        odma(kP[l], ostage[:, 128:256], [B_os])
                    act(junk[:, 0:256], gav, AF.Square, [B_gav], [B_junk, B_sm], accum_out=sm[:, 9, 1:2])
                    vop("dve", "tensor_scalar", [B_sm], [B_sm], out=sm[:, 9, 2:3], in0=sm[:, 9, 0:1], scalar1=-1.0 / 256, scalar2=None, op0=ALU.mult)
                    vop("dve", "tensor_tensor", [B_sm], [B_sm], out=sm[:, 9, 3:4], in0=sm[:, 9, 2:3], in1=sm[:, 9, 2:3], op=ALU.mult)
                    vop("dve", "scalar_tensor_tensor", [B_sm], [B_sm], out=sm[:, 9, 3:4], in0=sm[:, 9, 1:2], scalar=1.0 / 256, in1=sm[:, 9, 3:4], op0=ALU.mult, op1=ALU.subtract)
                    act(sm[:, 9, 3:4], sm[:, 9, 3:4], AF.Ln, [B_sm, B_c], [B_sm], bias=epsc[:, 0:1])
                    act(sm[:, 9, 3:4], sm[:, 9, 3:4], AF.Exp, [B_sm], [B_sm], scale=-0.5)
                    vop("dve", "tensor_scalar", [B_gav, B_sm], [B_vnf], out=vnf, in0=gav, scalar1=sm[:, 9, 2:3], scalar2=sm[:, 9, 3:4], op0=ALU.add, op1=ALU.mult)
                    vop("dve", "tensor_tensor", [B_vnf, B_par], [B_vnf], out=vnf, in0=vnf, in1=g_bc, op=ALU.mult)
                    vop("dve", "tensor_tensor", [B_vnf, B_par], [B_vn], out=vn, in0=vnf, in1=bv_bc, op=ALU.add)
                    for h in range(4):
                        mm(pM[(h % 2) * 64:(h % 2) * 64 + 64, (h // 2) * 128:(h // 2) * 128 + 128], vn[:, h * 64:(h + 1) * 64], wsTb[:, h, :], True, [B_vn, B_der], [B_pM])
                    yv = yt2[:, 0:256].rearrange("p (a b) -> p a b", a=2)
                    vop("dve", "tensor_tensor", [B_pM, B_par], [B_yt2], out=yv, in0=pM.rearrange("p (a b) -> p a b", a=2), in1=bsb[:], op=ALU.add)
                    vop("dve", "tensor_tensor", [B_yt2, B_gau2[par]], [B_cat], out=catT[:, 0:2, cs], in0=yv, in1=gau2[par][:, :, cs], op=ALU.mult)
                    ngA = negA1 if gtok == 0 else negA
                    for kv in range(2):
                        S4 = pS[kv][:].rearrange("p (g b t) -> p g b t", g=2, b=2)
                        mm(pS[kv][:], identb[:], ngA[:], True, [B_c], [B_pS[kv]], stop=False)
                        for gi in range(2):
                            for blk in range(2):
                                kc0 = c * 128 + blk * 128
                                mm(S4[:, gi, blk, :], kT[kv * 64:kv * 64 + 64, kc0:kc0 + 128], qT2[par][kv * 64:kv * 64 + 64, gi, cs], False,
                                   [B_kT, B_qT2[par]], [B_pS[kv]], stop=(gi == 1 and blk == 1))
                        act(Eb[kv], pS[kv][:], AF.Exp, [B_pS[kv]], [B_E[kv]])
                    AO = pY[:, 0:260].rearrange("p (h d) -> p h d", h=4)
                    first = True
                    for kv in range(2):
                        E4 = Eb[kv].rearrange("p (g b t) -> p g b t", g=2, b=2)
                        for gi in range(2):
                            hidx = gi * 2 + kv
                            for blk in range(2):
                                vs = prv if blk == 0 else cur
                                mm(AO[:, hidx, :], E4[:, gi, blk, :], vaug[:, vs, kv, :], first, [B_E[kv], B_va[vs]], [B_pY], stop=(kv == 1 and gi == 1 and blk == 1))
                                first = False
                    vop("dve", "tensor_tensor", [B_pY, B_der], [B_sm], out=sm[:, 10, 0:4], in0=AO[:, :, 64], in1=esink[:], op=ALU.add)
                    vop("dve", "reciprocal", [B_sm], [B_sm], out=sm[:, 10, 4:8], in_=sm[:, 10, 0:4])
                    vop("dve", "tensor_tensor", [B_pY, B_sm], [B_ob], out=ob.rearrange("p (h d) -> p h d", h=4), in0=AO[:, :, 0:64],
                        in1=sm[:, 10, 4:8].unsqueeze(2).broadcast_to([128, 4, 64]), op=ALU.mult)
                    for j in range(2):
                        tr(pTP[:, j * 128:(j + 1) * 128], ob[:, j * 128:(j + 1) * 128], identb[:], [B_ob, B_c], [B_pTP])
                    act(catT[:, 2:4, cs], pTP[:, 0:256].rearrange("p (a b) -> p a b", a=2), AF.Copy, [B_pTP], [B_cat])

            def stage_ssd(i, par):
                dt_chain(2)
                for c in range(2):
                    cs = slice(c * 128, (c + 1) * 128)
                    k8 = slice(c * 8, c * 8 + 8)
                    tok_transposes(c, par)
                    for g in range(2):
                        mm(pT1[:, g * 256:(g + 1) * 256], xact2[par][:, 6 + g, cs], hb[:, g * 256:(g + 1) * 256], g == 0, [B_xact2[par], B_hb], [B_pT1], stop=(g == 1))
                    state_update(c)
                    vop("pool", "tensor_tensor", [B_sm, B_c], [B_R], out=v8(Rt), in0=sm[:, 2, k8].unsqueeze(2).broadcast_to([128, 8, 128]),
                        in1=Uf[:].unsqueeze(1).broadcast_to([128, 8, 128]), op=ALU.mult)
                    for hf in range(2):
                        mm(pS[hf][:], onesf[:], Rt[:, hf * 512:(hf + 1) * 512], True, [B_R, B_c], [B_pS[hf]], stop=False)
                        mm(pS[hf][:], identb[:], negS[:], False, [B_c], [B_pS[hf]])
                        for hh in range(4):
                            h = hf * 4 + hh
                            act(LT[:, h, :], pS[hf][:, hh * 128:(hh + 1) * 128], AF.Exp, [B_pS[hf], B_sm], [B_LT], bias=sm[:, 5, c * 8 + h:c * 8 + h + 1])
                    fill(1)
                    for g in range(2):
                        mm(pY[:, 64 + g * 128:192 + g * 128], xact2[par][:, 4 + g, cs], xact2[par][:, 6 + g, cs], True, [B_xact2[par]], [B_pY])
                    for g in range(2):
                        vop("dve", "tensor_tensor", [B_LT, B_pY], [B_G], out=Gt[:, g * 4:g * 4 + 4, :], in0=LT[:, g * 4:g * 4 + 4, :],
                            in1=pY[:, 64 + g * 128:192 + g * 128].unsqueeze(1).broadcast_to([128, 4, 128]), op=ALU.mult)
                    for h in range(8):
                        mm(pT0[:, h * 64:(h + 1) * 64], Gt[:, h, :], xstok[:, h * 64:(h + 1) * 64], h == 0, [B_G, B_xstok], [B_pT0], stop=False)
                    for cc in range(4):
                        mm(pT0[:, cc * 128:(cc + 1) * 128], xact2[par][:, cc, cs], Dg[:, cc, :], False, [B_xact2[par], B_der], [B_pT0], stop=(cc == 3))
                    fill(1)
                    vop("dve", "tensor_tensor", [B_pT1, B_sm], [B_yt1], out=v8(yt1), in0=v8(pT1[:]), in1=sm[:, 4, k8].unsqueeze(2).broadcast_to([128, 8, 64]), op=ALU.mult)
                    vop("dve", "tensor_tensor", [B_yt1, B_pT0], [B_yt1], out=yt1, in0=yt1, in1=pT0[:], op=ALU.add)
                    vop("dve", "tensor_tensor", [B_yt1, B_sz[c]], [B_yt2], out=yt2, in0=yt1, in1=sz[:, c, :], op=ALU.mult)
                    act(junk, yt2, AF.Square, [B_yt2], [B_junk, B_sm], accum_out=sm[:, 11, 0:1])
                    act(sm[:, 11, 1:2], sm[:, 11, 0:1], AF.Ln, [B_sm, B_c], [B_sm], bias=epsc[:, 1:2], scale=1.0 / 512)
                    act(sm[:, 11, 1:2], sm[:, 11, 1:2], AF.Exp, [B_sm], [B_sm], scale=-0.5)
                    vop("dve", "scalar_tensor_tensor", [B_yt2, B_sm, B_par], [B_oc], out=ocb, in0=yt2, scalar=sm[:, 11, 1:2], in1=gn_bc, op0=ALU.mult, op1=ALU.mult)
                    fill(1)
                    for cc in range(4):
                        tr(pTP[:, cc * 128:(cc + 1) * 128], ocb[:, cc * 128:(cc + 1) * 128], identb[:], [B_oc, B_c], [B_pTP])
                    act(catT[:, 4:8, cs], pTP[:, 0:512].rearrange("p (a b) -> p a b", a=4), AF.Copy, [B_pTP], [B_cat])
                    fill(1)

            def stage_out(i, par):
                cols = slice(i * TA, (i + 1) * TA)
                for dc in range(8):
                    p, Bp = pDn()
                    for ec in range(8):
                        mm(p[:, 0:TA], woutb[:, ec, dc * 128:(dc + 1) * 128], catT[:, ec, :], ec == 0, [B_wout, B_cat], [Bp], stop=(ec == 7))
                    vop("dve", "scalar_tensor_tensor", [B_X[i], Bp], [B_X[i]], out=X[:, dc, cols], in0=X[:, dc, cols], scalar=ALPHA, in1=p[:, 0:TA], op0=ALU.mult, op1=ALU.add)
                if i == NTA - 1:
                    for half in range(2):
                        p, Bp = pDn()
                        for kc in range(8):
                            mm(p[0:3, :], xb2[par][:, kc, TA - 3:TA], winb[:, kc, 640 + half * 512:1152 + half * 512], kc == 0, [B_win, B_xb2[par]], [Bp], stop=(kc == 7))
                        vop("dve", "tensor_copy", [Bp], B_sz, out=cpst[0:3, half * 512:(half + 1) * 512], in_=p[0:3, :])
                    odma(cP[l], cpst[0:3, :], B_sz)
                layer_norm_fm(cols, TA, [B_X[i]], 0, 8, xb2[par], xact2[par], [B_xb2[par]], [B_xact2[par]], nmean, lvar, B_nm, B_lv)

            for pc in stage_proj(0, 0):
                pc()
            for i in range(NTA):
                par = i % 2
                stage_attn(i, par)
                if i + 1 < NTA:
                    pending.extend(stage_proj(i + 1, 1 - par))
                    fill(1)
                stage_ssd(i, par)
                stage_out(i, par)
                fill(len(pending))
            vop("dve", "tensor_copy", [B_h], [B_yt1], out=yt1, in_=hst[:])
            odma(hP[l], yt1, [B_yt1])

            dbg("sm", sm[:], [B_sm]); dbg("hst", hst[:], [B_h]); dbg("gst", gst, [B_gst]); dbg("yt1", yt1, [B_yt1]); dbg("yt2", yt2, [B_yt2])
            dbg("catT", catT, [B_cat], BF16); dbg("xact", xact, [B_xact], BF16); dbg("LT", LT, [B_LT], BF16); dbg("Gt", Gt, [B_G], BF16)
            dbg("xstok", xstok, [B_xstok], BF16); dbg("sz", sz, B_sz); dbg("Rt", Rt, [B_R]); dbg("gau", gau, [B_gau]); dbg("kT", kT[:], [B_kT], BF16)
            dbg("vaug", vaug[:], B_va, BF16); dbg("Eb0", Eb[0], [B_E[0]], BF16); dbg("ob", ob, [B_ob], BF16); dbg("qT", qT, [B_qT], BF16)
            ck(4)
            P.barrier()
            SI = NTA
            scols = slice(TP, TX)
            xbs = xb[:, :, 0:NS]
            act(xbs, X[:, :, scols], AF.Copy, [B_X[SI]], [B_xb])
            ck(401)
            for (c0, c1) in ((0, 512), (512, 1024), (1024, 1536), (1536, 2048), (2048, 2560), (2560, 2568)):
                p, Bp = pDn()
                for kc in range(8):
                    mm(p[0:NS, 0:c1 - c0], xb[:, kc, 0:NS], winb[:, kc, c0:c1], kc == 0, [B_win, B_xb], [Bp], stop=(kc == 7))
                vop("dve", "tensor_copy", [Bp], [B_stok], out=s_tok[:, c0:c1], in_=p[0:NS, 0:c1 - c0])
            ck(402)
            dma("sp", kS[l], s_tok[:, 512:640], [B_stok], [B_kS])
            dma("sp", vS[l], s_tok[:, 1920:2048], [B_stok], [B_vS])
            ck(403)
            for b in range(NS):
                dma("pool", s_kb[:, b, :], st_k[l, b], [], [B_skb])
                dma("pool", s_vb[:, b, :], st_v[l, b], [], [B_svb])
            ck(41)
            dma("pool", s_kb[0:1, :, :], kS[l:l + 1], [B_kS], [B_skb])
            dma("pool", s_vb[0:1, :, :], vS[l:l + 1], [B_vS], [B_svb])
            for b0 in range(0, NS, 4):
                for b in range(b0, b0 + 4):
                    tr(pTP[:, (b - b0) * 128:(b - b0 + 1) * 128], s_kb[:, b, :], identb[:], [B_skb, B_c], [B_pTP])
                act(s_kT[:, b0:b0 + 4, :], pTP[:, 0:512].rearrange("p (a b) -> p a b", a=4), AF.Copy, [B_pTP], [B_skT])
            for j in range(2):
                p, Bp = pDn()
                for kc in range(8):
                    mm(p[:, 0:NS], winb[:, kc, (2 + j) * 128:(3 + j) * 128], xb[:, kc, 0:NS], kc == 0, [B_win, B_xb], [Bp], stop=(kc == 7))
                act(s_q[:, j, :], p[:, 0:NS], AF.Copy, [Bp], [B_sq], scale=0.125)
            for kv in range(2):
                for gi in range(2):
                    for b in range(NS):
                        col = gi * NS + b
                        mm(pS[kv][:, col:col + 1], s_kT[kv * 64:kv * 64 + 64, b, :], s_q[kv * 64:kv * 64 + 64, gi, b:b + 1], True, [B_skT, B_sq], [B_pS[kv]])
                act(s_E[:, kv * 32:(kv + 1) * 32], pS[kv][:, 0:32], AF.Exp, [B_pS[kv]], [B_sE])
            ck(42)
            mm(pY[:, 0:64], onesb[:], s_E, True, [B_sE, B_c], [B_pY])
            for kv in range(2):
                for gi in range(2):
                    hidx = gi * 2 + kv
                    c0 = kv * 32 + gi * NS
                    vop("dve", "tensor_scalar", [B_pY, B_der], [B_srd], out=s_rd[:, c0:c0 + NS], in0=pY[:, c0:c0 + NS], scalar1=esink[:, hidx:hidx + 1], scalar2=None, op0=ALU.add)
            vop("dve", "reciprocal", [B_srd], [B_srd], out=s_rd, in_=s_rd)
            pO = pT0[:, 0:2 * NS].rearrange("p (g b) -> p g b", g=2)
            for kv in range(2):
                for gi in range(2):
                    for b in range(NS):
                        col = kv * 32 + gi * NS + b
                        mm(pO[kv * 64:kv * 64 + 64, gi, b:b + 1], s_vb[:, b, kv * 64:kv * 64 + 64], s_E[:, col:col + 1], True, [B_svb, B_sE], [B_pT0])
            for kv in range(2):
                vop("dve", "tensor_tensor", [B_pT0, B_srd], [B_scatT], out=s_catT[kv * 64:kv * 64 + 64, 2:4, :], in0=pO[kv * 64:kv * 64 + 64, :, :],
                    in1=s_rd[kv * 64:kv * 64 + 64, kv * 32:kv * 32 + 32].rearrange("p (g b) -> p g b", g=2), op=ALU.mult)
            ck(43)
            sN = lambda a, b: sm[0:NS, 12, a:b]
            act(s_misc[:, 0:256], s_tok[:, 0:256], AF.Gelu_apprx_tanh, [B_stok], [B_smisc])
            act(s_misc[:, 256:512], s_tok[:, 1664:1920], AF.Gelu_apprx_tanh, [B_stok], [B_smisc, B_sm], accum_out=sN(0, 1))
            act(junk[0:NS, 0:256], s_misc[:, 256:512], AF.Square, [B_smisc], [B_junk, B_sm], accum_out=sN(1, 2))
            vop("dve", "tensor_scalar", [B_sm], [B_sm], out=sN(2, 3), in0=sN(0, 1), scalar1=-1.0 / 256, scalar2=None, op0=ALU.mult)
            vop("dve", "tensor_tensor", [B_sm], [B_sm], out=sN(3, 4), in0=sN(2, 3), in1=sN(2, 3), op=ALU.mult)
            vop("dve", "scalar_tensor_tensor", [B_sm], [B_sm], out=sN(3, 4), in0=sN(1, 2), scalar=1.0 / 256, in1=sN(3, 4), op0=ALU.mult, op1=ALU.subtract)
            act(sN(3, 4), sN(3, 4), AF.Ln, [B_sm, B_c], [B_sm], bias=epsc[0:NS, 0:1])
            act(sN(3, 4), sN(3, 4), AF.Exp, [B_sm], [B_sm], scale=-0.5)
            vnS_ = s_misc[:, 256:512]
            vop("dve", "tensor_scalar", [B_smisc, B_sm], [B_smisc], out=vnS_, in0=vnS_, scalar1=sN(2, 3), scalar2=sN(3, 4), op0=ALU.add, op1=ALU.mult)
            vop("dve", "tensor_tensor", [B_smisc, B_par], [B_smisc], out=vnS_, in0=vnS_, in1=rowt[0:NS, 0:256], op=ALU.mult)
            vop("dve", "tensor_tensor", [B_smisc, B_par], [B_smisc], out=vnS_, in0=vnS_, in1=rowt[0:NS, 256:512], op=ALU.add)
            odma(vnS[l], vnS_, [B_smisc])
            m4 = s_misc[:, 512:768].rearrange("p (h d) -> p h d", h=4)
            vop("dve", "tensor_tensor", [B_smisc, B_par], [B_smisc], out=m4, in0=vnS_.rearrange("p (h d) -> p h d", h=4),
                in1=rowt[0:NS, 1052:1056].unsqueeze(2).broadcast_to([NS, 4, 64]), op=ALU.mult)
            vop("dve", "tensor_tensor", [B_smisc, B_par], [B_smisc], out=m4, in0=m4, in1=rowt[0:NS, 1056:1060].unsqueeze(2).broadcast_to([NS, 4, 64]), op=ALU.add)
            vop("dve", "tensor_tensor", [B_smisc], [B_scat], out=s_cat[:, 0:256], in0=s_misc[:, 512:768], in1=s_misc[:, 0:256], op=ALU.mult)
            ck(44)
            dma("sp", s_hc, st_c[l], [], [B_shc])
            odma(cS[l, :, 0:2, :], st_c[l].rearrange("(b i) c -> b i c", i=3)[:, 1:3, :], [])
            odma(cS[l, :, 2, :], s_tok[:, 640:1664], [B_stok])
            for cc in range(8):
                mm(pT1[:, cc * 48:(cc + 1) * 48], s_hc[:, cc * 128:(cc + 1) * 128], identf[0:NS * 3, 0:NS * 3], cc == 0, [B_shc, B_c], [B_pT1], stop=(cc == 7))
            mm(pT1[:, 0:384], zerosb[:, 0:128], zerosb[:, 0:384], False, [B_c], [B_pT1])
            vop("dve", "tensor_copy", [B_pT1], [B_shist], out=s_hist, in_=pT1[:, 0:384].rearrange("p (c j) -> p c j", c=8))
            for cc in range(8):
                p, Bp = pDn()
                for kc in range(8):
                    mm(p[:, 0:NS], winb[:, kc, (5 + cc) * 128:(6 + cc) * 128], xb[:, kc, 0:NS], kc == 0, [B_win, B_xb], [Bp], stop=(kc == 7))
                vop("dve", "tensor_copy", [Bp], [B_sxfm], out=s_xfm[:, cc, :], in_=p[:, 0:NS])
            cw4 = colt[:, 32:64].rearrange("p (c t) -> p c t", c=8)
            hist4 = s_hist.rearrange("p c (b i) -> p c b i", i=3)
            vop("dve", "tensor_tensor", [B_sxfm, B_par], [B_scv], out=s_cv, in0=s_xfm, in1=cw4[:, :, 3:4].broadcast_to([128, 8, NS]), op=ALU.mult)
            for t in range(3):
                vop("dve", "tensor_tensor", [B_shist, B_par], [B_scv2], out=s_cv2, in0=hist4[:, :, :, t], in1=cw4[:, :, t:t + 1].broadcast_to([128, 8, NS]), op=ALU.mult)
                vop("dve", "tensor_tensor", [B_scv, B_scv2], [B_scv], out=s_cv, in0=s_cv, in1=s_cv2, op=ALU.add)
            vop("dve", "tensor_tensor", [B_scv, B_par], [B_scv], out=s_cv, in0=s_cv, in1=colt[:, 64:72].unsqueeze(2).broadcast_to([128, 8, NS]), op=ALU.add)
            act(s_cv, s_cv, AF.Silu, [B_scv], [B_scv])
            for cc in range(8):
                bank, Bb = (pT1, B_pT1) if cc < 4 else (pS[0], B_pS[0])
                mm(bank[0:NS, (cc % 4) * 128:(cc % 4 + 1) * 128], s_cv[:, cc, :], identf[:], cc % 4 == 0, [B_scv, B_c], [Bb], stop=(cc % 4 == 3))
            mm(pT1[0:NS, :], zerosb[:, 0:NS], zerosb[:], False, [B_c], [B_pT1])
            mm(pS[0][0:NS, :], zerosb[:, 0:NS], zerosb[:], False, [B_c], [B_pS[0]])
            vop("dve", "tensor_copy", [B_pT1], [B_spk], out=s_pk[:, 0:512], in_=pT1[0:NS, :])
            vop("dve", "tensor_copy", [B_pS[0]], [B_spk], out=s_pk[:, 512:1024], in_=pS[0][0:NS, :])
            dtr = s_pk[:, 1024:1032]; dtt = s_pk[:, 1032:1040]; dAe = s_pk[:, 1040:1048]
            vop("dve", "tensor_tensor", [B_stok, B_par], [B_spk], out=dtr, in0=s_tok[:, 2048:2056], in1=rowt[0:NS, 1024:1032], op=ALU.add)
            vop("dve", "scalar_tensor_tensor", [B_spk], [B_spk], out=dtt, in0=dtr, scalar=-1.0, in1=dtr, op0=ALU.mult, op1=ALU.min)
            act(dtt, dtt, AF.Exp, [B_spk], [B_spk])
            act(dtt, dtt, AF.Ln, [B_spk, B_c], [B_spk], bias=epsc[0:NS, 2:3])
            vop("dve", "scalar_tensor_tensor", [B_spk], [B_spk], out=dtt, in0=dtr, scalar=0.0, in1=dtt, op0=ALU.max, op1=ALU.add)
            vop("dve", "tensor_tensor", [B_spk, B_der], [B_spk], out=dAe, in0=dtt, in1=a_bc[0:NS, :], op=ALU.mult)
            act(dAe, dAe, AF.Exp, [B_spk], [B_spk])
            vop("dve", "tensor_tensor", [B_spk], [B_sact], out=v8(s_act[:, 0:512]), in0=v8(s_pk[:, 0:512]), in1=dtt.unsqueeze(2).broadcast_to([NS, 8, 64]), op=ALU.mult)
            ck(5)
            dma("sp", sbn[l], s_pk, [B_spk], [B_sbn])
            dma("sp", sbx[l], s_act[:, 0:512], [B_sact], [B_sbx])
            dma("sp", sbd[l], s_pk[:, 1040:1048], [B_spk], [B_sbd])
            for hh in range(4):
                dma("sp", sbB[l, :, :, hh, :], s_pk[:, 512:768].rearrange("p (g n) -> p g n", g=2), [B_spk], [B_sbB])
                dma("sp", sbC[l, :, :, hh, :], s_pk[:, 768:1024].rearrange("p (g n) -> p g n", g=2), [B_spk], [B_sbC])
            P.barrier()
            dma("sp", s_x8[:, 0:64], sbx[l].rearrange("b (h d) -> (b h) d", h=8), [B_sbx], [B_sx8])
            dma("sp", s_x8[:, 64:192], sbB[l].rearrange("b g h n -> (b g h) n"), [B_sbB], [B_sx8])
            dma("sp", s_x8[:, 192:320], sbC[l].rearrange("b g h n -> (b g h) n"), [B_sbC], [B_sx8])
            dma("sp", s_x8[:, 320:321], sbd[l].rearrange("b (h o) -> (b h) o", o=1), [B_sbd], [B_sx8])
            sth = st_h[l].rearrange("p (d n) -> p d n", n=128)
            hSl = hS[l].rearrange("p (d n) -> p d n", n=128)
            for half in range(2):
                dma("sp", s_h, sth[:, half * 32:(half + 1) * 32, :], [], [B_sh])
                act(s_h, s_h, AF.Copy, [B_sh, B_sx8], [B_sh], scale=s_x8[:, 320:321])
                for q2 in range(2):
                    d0 = half * 32 + q2 * 16
                    vop("dve", "tensor_tensor", [B_sx8], [B_stmp], out=s_tmp, in0=s_x8[:, d0:d0 + 16].unsqueeze(2).broadcast_to([128, 16, 128]),
                        in1=s_x8[:, 64:192].unsqueeze(1).broadcast_to([128, 16, 128]), op=ALU.mult)
                    vop("pool", "tensor_tensor", [B_sh, B_stmp], [B_sh], out=s_h[:, q2 * 16:(q2 + 1) * 16, :], in0=s_h[:, q2 * 16:(q2 + 1) * 16, :], in1=s_tmp, op=ALU.add)
                odma(hSl[:, half * 32:(half + 1) * 32, :], s_h, [B_sh])
                for q2 in range(2):
                    d0 = half * 32 + q2 * 16
                    vop("dve", "tensor_tensor", [B_sh, B_sx8], [B_stmp], out=s_tmp, in0=s_h[:, q2 * 16:(q2 + 1) * 16, :],
                        in1=s_x8[:, 192:320].unsqueeze(1).broadcast_to([128, 16, 128]), op=ALU.mult)
                    vop("dve", "tensor_reduce", [B_stmp], [B_sy8], out=s_y8[:, d0:d0 + 16], in_=s_tmp, axis=AX.X, op=ALU.add)
            dma("sp", sby[l], s_y8, [B_sy8], [B_sby])
            dma("sp", s_act[:, 512:1024], sby[l].rearrange("(b h) d -> b (h d)", h=8), [B_sby], [B_sact])
            vop("dve", "tensor_tensor", [B_spk, B_par], [B_smisc], out=v8(s_misc[:, 0:512]), in0=v8(s_pk[:, 0:512]), in1=rowt[0:NS, 1044:1052].unsqueeze(2).broadcast_to([NS, 8, 64]), op=ALU.mult)
            vop("dve", "tensor_tensor", [B_smisc, B_sact], [B_smisc], out=s_misc[:, 0:512], in0=s_misc[:, 0:512], in1=s_act[:, 512:1024], op=ALU.add)
            act(s_misc[:, 512:1024], s_tok[:, 2056:2568], AF.Silu, [B_stok], [B_smisc])
            vop("dve", "tensor_tensor", [B_smisc], [B_smisc], out=s_misc[:, 0:512], in0=s_misc[:, 0:512], in1=s_misc[:, 512:1024], op=ALU.mult)
            act(junk[0:NS, :], s_misc[:, 0:512], AF.Square, [B_smisc], [B_junk, B_sm], accum_out=sN(5, 6))
            act(sN(6, 7), sN(5, 6), AF.Ln, [B_sm, B_c], [B_sm], bias=epsc[0:NS, 1:2], scale=1.0 / 512)
            act(sN(6, 7), sN(6, 7), AF.Exp, [B_sm], [B_sm], scale=-0.5)
            vop("dve", "scalar_tensor_tensor", [B_smisc, B_sm, B_par], [B_scat], out=s_cat[:, 512:1024], in0=s_misc[:, 0:512], scalar=sN(6, 7), in1=rowt[0:NS, 512:1024], op0=ALU.mult, op1=ALU.mult)
            for idx, c0 in enumerate((0, 128, 512, 640, 768, 896)):
                tr(pTP[:, idx * NS:(idx + 1) * NS], s_cat[:, c0:c0 + 128], identb[0:NS, 0:NS], [B_scat, B_c], [B_pTP])
            act(s_catT[:, 0:2, :], pTP[:, 0:2 * NS].rearrange("p (a b) -> p a b", a=2), AF.Copy, [B_pTP], [B_scatT])
            act(s_catT[:, 4:8, :], pTP[:, 2 * NS:6 * NS].rearrange("p (a b) -> p a b", a=4), AF.Copy, [B_pTP], [B_scatT])
            for dc in range(8):
                p, Bp = pDn()
                for ec in range(8):
                    mm(p[:, 0:NS], woutb[:, ec, dc * 128:(dc + 1) * 128], s_catT[:, ec, :], ec == 0, [B_wout, B_scatT], [Bp], stop=(ec == 7))
                vop("dve", "scalar_tensor_tensor", [B_X[SI], Bp], [B_X[SI]], out=X[:, dc, scols], in0=X[:, dc, scols], scalar=ALPHA, in1=p[:, 0:NS], op0=ALU.mult, op1=ALU.add)
            layer_norm_fm(scols, NS, [B_X[SI]], 0, 8, xb[:, :, 0:NS], s_sq, [B_xb], [B_ssq], s_nm, s_lv, B_snm, B_slv)

            ck(6)
            P.barrier()
            banks = [(pD[0], B_pD[0]), (pD[1], B_pD[1]), (pT0, B_pT0), (pT1, B_pT1), (pS[0], B_pS[0]), (pS[1], B_pS[1]), (pY, B_pY)]
            pendB = []
            for ft in range(NFT):
                f0 = ft * FT
                blocks = [(f0 + b0, min(512, FT - b0), b0) for b0 in range(0, FT, 512)]
                xbufs = [B_X[j] for j in range(ft * (FT // TA), (ft + 1) * (FT // TA))]
                if ft == NFT - 1:
                    blocks.append((TP, NS, FT))
                    xbufs = xbufs + [B_X[NTA]]
                for (c0, n, o0) in blocks:
                    act(x1b[:, :, o0:o0 + n], X[:, :, c0:c0 + n], AF.Copy, xbufs, [B_x1b])
                bi = 0
                for g in range(16):
                    if g >= 2 and pendB:
                        pendB.pop(0)()
                    wb = w1b[g % 2]; Bw = B_w1b[g % 2]
                    dma("pool", wb, w1[l, g // 2, :, :, (g % 2) * 256:(g % 2) * 256 + 256], [], [Bw])
                    for f2 in range(2):
                        fc = g * 2 + f2
                        for (c0, n, o0) in blocks:
                            p, Bp = banks[bi % 5]; bi += 1
                            for kc in range(8):
                                mm(p[:, 0:n], wb[:, kc, f2 * 128:(f2 + 1) * 128], x1b[:, kc, o0:o0 + n], kc == 0, [Bw, B_x1b], [Bp], stop=(kc == 7))
                            r_ = rt[bi % 2]; Br = B_rt[bi % 2]
                            act(r_[:, 0:n], p[:, 0:n], AF.Relu, [Bp], [Br])
                            vop("dve", "tensor_tensor", [Br], [B_hT], out=hT[:, fc, o0:o0 + n], in0=r_[:, 0:n], in1=r_[:, 0:n], op=ALU.mult)
                while pendB:
                    pendB.pop(0)()
                for dc in range(8):
                    wb = w2b[dc % 2]; Bw = B_w2b[dc % 2]
                    dma("pool", wb, w2[l, dc], [], [Bw])
                    for (c0, n, o0) in blocks:
                        p, Bp = banks[bi % 5]; bi += 1
                        for fc in range(32):
                            mm(p[:, 0:n], wb[:, fc, :], hT[:, fc, o0:o0 + n], fc == 0, [Bw, B_hT], [Bp], stop=(fc == 31))
                        vop("dve", "scalar_tensor_tensor", xbufs + [Bp], xbufs, out=X[:, dc, c0:c0 + n], in0=X[:, dc, c0:c0 + n], scalar=ALPHA, in1=p[:, 0:n], op0=ALU.mult, op1=ALU.add)
                if ft == NFT - 1 and not last and can_prefetch:
                    nl = len(P.live_dma)
                    for kc in range(8):
                        dma("pool", winb[:, kc, :], win[l + 1, :, kc, :], [], [B_win, B_hT])
                    dma("pool", woutb, wout[l + 1], [], [B_wout, B_hT])
                    del P.live_dma[nl:]
                for (c0, n, o0) in blocks:
                    pcs = layer_norm_pieces(slice(c0, c0 + n), n, xbufs, 16, 24, lnx[:, :, 0:n], lnq[:, :, 0:n], [B_w2b[1]], [B_w2b[0]], nmeanB, lvarB, B_nmB, B_lvB,
                                            bk0=(pS[1], B_pS[1]), bk1=(pY, B_pY))
                    if ft < NFT - 1:
                        pendB.extend(pcs)
                    else:
                        for pc in pcs:
                            pc()
            P.barrier()
            ck(7)
            if not last and use_cc:
                dma("sp", cc_x_in.rearrange("p (k t) -> p k t", k=8), X[:, :, TP - 128:TP], [B_X[NTA - 1]], [B_ccx])
                P.op("pool", lambda e: e.collective_compute("AllGather", ALU.bypass, replica_groups=[[0, 1, 2, 3], [4, 5, 6, 7]],
                                                            ins=[cc_x_in], outs=[cc_x_out]), r=[B_ccx], w=[B_ccx_o], dma=True, inc=1)
                dma("sp", gx, cc_x_out.rearrange("(r p) f -> p r f", p=128), [B_ccx_o], [B_gx])
                xh2 = xh[:].rearrange("p k t -> p (k t)")
                vop("dve", "tensor_scalar", [B_gx, B_c], [B_xh], out=xh2, in0=gx[:, 0, :], scalar1=sel[:, 4:5], scalar2=None, op0=ALU.mult)
                for r_ in range(1, 4):
                    vop("dve", "scalar_tensor_tensor", [B_gx, B_c, B_xh], [B_xh], out=xh2, in0=gx[:, r_, :], scalar=sel[:, 4 + r_:5 + r_], in1=xh2, op0=ALU.mult, op1=ALU.add)

          except _Stop:
            break
        for i in range(NTA):
            odma(yT[:, :, i * TA:(i + 1) * TA], X[:, :, i * TA:(i + 1) * TA], [B_X[i]])
        odma(yT[:, :, TP:TX], X[:, :, TP:TX], [B_X[NTA]])
        P.op("sp", None, w=outbufs + [B_kS, B_vS])
        P.emit(st)
        P.stats.update(carve_stats)
    return nc, P


def _perm_cols():
    r = np.arange
    return np.concatenate([r(0, 256), r(512, 576), r(640, 704), r(576, 640), r(704, 768), r(768, 896), r(1536, 2560),
                           r(256, 512), r(896, 1024), r(2560, 2568), r(1024, 1536)])


def _perm_rows():
    r = np.arange
    return np.concatenate([r(0, 256), 256 + r(0, 64), 256 + r(128, 192), 256 + r(64, 128), 256 + r(192, 256), r(512, 1024)])


def prep_shared(inp, L):
    f = np.float32
    w_in = np.asarray(inp["w_in"], f)[:L]
    win = np.ascontiguousarray(w_in[:, :, _perm_cols()].reshape(L, 8, 128, DIN).transpose(0, 2, 1, 3))
    w_out = np.asarray(inp["w_out"], f)[:L]
    wout = np.ascontiguousarray(w_out[:, _perm_rows(), :].reshape(L, 8, 128, D).transpose(0, 2, 1, 3))
    w1 = np.ascontiguousarray(np.asarray(inp["w1"], f)[:L].reshape(L, 8, 128, 8, 512).transpose(0, 3, 2, 1, 4))
    w2 = np.ascontiguousarray(np.asarray(inp["w2"], f)[:L].reshape(L, 32, 128, 8, 128).transpose(0, 3, 2, 1, 4))
    rowp = np.zeros((L, 1, NROW), f)
    rowp[:, 0, 0:256] = inp["ln_v_g"][:L]
    rowp[:, 0, 256:512] = inp["ln_v_b"][:L]
    rowp[:, 0, 512:1024] = inp["gn_w"][:L]
    rowp[:, 0, 1024:1032] = inp["dt_bias"][:L]
    rowp[:, 0, 1032:1040] = inp["a_log"][:L]
    rowp[:, 0, 1040:1044] = np.asarray(inp["sinks"])[:L][:, [0, 2, 1, 3]]
    rowp[:, 0, 1044:1052] = inp["d_skip"][:L]
    rowp[:, 0, 1052:1056] = np.asarray(inp["w_s"])[:L, :, 0, 0]
    rowp[:, 0, 1056:1060] = np.asarray(inp["b_s"])[:L, :, 0]
    colp = np.zeros((L, 128, NCOLP), f)
    for k, name in enumerate(("ln1_g", "ln1_b", "ln2_g", "ln2_b")):
        colp[:, :, 8 * k:8 * k + 8] = np.asarray(inp[name], f)[:L].reshape(L, 8, 128).transpose(0, 2, 1)
    cw = np.asarray(inp["conv_w"], f)[:L].reshape(L, 4, 8, 128)
    colp[:, :, 32:64] = cw.transpose(0, 3, 2, 1).reshape(L, 128, 32)
    colp[:, :, 64:72] = np.asarray(inp["conv_b"], f)[:L].reshape(L, 8, 128).transpose(0, 2, 1)
    dsk = np.repeat(np.asarray(inp["d_skip"], f)[:L], 64, axis=1)
    colp[:, :, 72:76] = dsk.reshape(L, 4, 128).transpose(0, 2, 1)
    wsT = np.ascontiguousarray(np.asarray(inp["w_s"], f)[:L].transpose(0, 3, 1, 2))
    bsd = np.ascontiguousarray(np.asarray(inp["b_s"], f)[:L])
    s_ = np.arange(128)[:, None]
    t_ = np.arange(128)[None, :]
    cst_f = np.stack([np.eye(128, dtype=f), (s_ <= t_).astype(f), np.ones((128, 128), f)], axis=1)
    prev = np.where(s_ > t_, 0.0, NEG).astype(f)
    cur = np.where(s_ <= t_, 0.0, NEG).astype(f)
    negA = np.concatenate([prev, cur, prev, cur], axis=1)
    allneg = np.full((128, 128), NEG, f)
    negA_first = np.concatenate([allneg, cur, allneg, cur], axis=1)
    negS = np.concatenate([cur] * 4, axis=1)
    return dict(win=win, wout=wout, w1=w1, w2=w2, rowp=rowp, colp=colp, wsT=wsT, bsd=bsd, cst_f=cst_f), (negA, negA_first, negS)


def prep_core(inp, shared, masks, c, L, NT):
    f = np.float32
    TP = NT * 128
    b, p = c // 4, c % 4
    t0 = p * TP
    xp = np.asarray(inp["x_prompt"], f)
    xs = np.asarray(inp["x_sample"], f)
    tok = np.concatenate([xp[b, t0:t0 + TP], xs[c * NS:(c + 1) * NS, 0]], axis=0)
    xT0 = np.ascontiguousarray(tok.T.reshape(8, 128, TP + NS).transpose(1, 0, 2))
    if p == 0:
        xh0 = np.zeros((128, 8, 128), f)
    else:
        xh0 = np.ascontiguousarray(xp[b, t0 - 128:t0].T.reshape(8, 128, 128).transpose(1, 0, 2))
    negA, negA_first, negS = masks
    cst_m = np.stack([negA, negA_first if p == 0 else negA, negS], axis=1)
    selp = np.zeros((128, 8), f)
    for r in range(3):
        selp[:, r] = 1.0 if r < p else 0.0
    for r in range(4):
        selp[:, 4 + r] = 1.0 if r == p - 1 else 0.0
    sl = slice(c * NS, (c + 1) * NS)
    d = dict(shared)
    d.update(xT0=xT0, xh0=xh0, cst_m=np.ascontiguousarray(cst_m), selp=selp,
             st_k=np.ascontiguousarray(np.asarray(inp["state_attn_k"], f)[:L, sl].reshape(L, NS, 128, 128)),
             st_v=np.ascontiguousarray(np.asarray(inp["state_attn_v"], f)[:L, sl].reshape(L, NS, 128, 128)),
             st_c=np.ascontiguousarray(np.asarray(inp["state_conv"], f)[:L, sl].reshape(L, NS * 3, 1024)),
             st_h=np.ascontiguousarray(np.asarray(inp["state_ssm"], f)[:L, sl].reshape(L, NS * 8, 64 * 128)))
    return d


def assemble(res, L, NT):
    f = np.float32
    TP = NT * 128
    S = 4 * TP
    yp = np.zeros((2, S, D), f); ys = np.zeros((128, 1, D), f)
    kp = np.zeros((L, 2, 128, 2, 64), f); vp = np.zeros_like(kp)
    cp = np.zeros((L, 2, 3, 1024), f); hp = np.zeros((L, 2, 8, 64, 128), f)
    ks = np.zeros((L, 128, 1, 2, 64), f); vs = np.zeros_like(ks)
    cs = np.zeros((L, 128, 3, 1024), f); hs = np.zeros((L, 128, 8, 64, 128), f); vns = np.zeros((L, 128, 1, 256), f)
    for c in range(NCORES):
        r = res[c]
        b, p = c // 4, c % 4
        tok = np.asarray(r["yT"]).transpose(1, 0, 2).reshape(D, TP + NS).T
        yp[b, p * TP:(p + 1) * TP] = tok[:TP]
        sl = slice(c * NS, (c + 1) * NS)
        ys[sl, 0] = tok[TP:]
        if p == 3:
            kp[:, b] = np.asarray(r["kP"]).reshape(L, 128, 2, 64)
            vp[:, b] = np.asarray(r["vP"]).reshape(L, 128, 2, 64)
            cp[:, b] = np.asarray(r["cP"])
            hp[:, b] = np.asarray(r["hP"]).reshape(L, 128, 8, 64).transpose(0, 2, 3, 1)
        ks[:, sl, 0] = np.asarray(r["kS"]).reshape(L, NS, 2, 64)
        vs[:, sl, 0] = np.asarray(r["vS"]).reshape(L, NS, 2, 64)
        cs[:, sl] = np.asarray(r["cS"])
        hs[:, sl] = np.asarray(r["hS"]).reshape(L, NS, 8, 64, 128)
        vns[:, sl, 0] = np.asarray(r["vnS"])
    return (yp, ys, kp, vp, cp, hp, ks, vs, cs, hs, vns)


_CACHE = {}


def run(inp, L, NT, use_cc=True, trace=False):
    key = (L, NT, use_cc)
    if key not in _CACHE:
        _CACHE[key] = build(L, NT, use_cc)
    nc, P = _CACHE[key]
    shared, masks = prep_shared(inp, L)
    in_maps = [prep_core(inp, shared, masks, c, L, NT) for c in range(NCORES)]
    res = run_bass_kernel_spmd(nc, in_maps, core_ids=list(range(NCORES)), trace=trace)
    return assemble(res.results, L, NT), res


def kernel(**inputs):
    out, _ = run(inputs, DEPTH, SEQ // (4 * 128))
    return out
```

```python
import math
import numpy as np
from contextlib import ExitStack
import concourse.bass as bass
import concourse.mybir as mybir
from concourse.bass_utils import run_bass_kernel_spmd

F32 = mybir.dt.float32
BF16 = mybir.dt.bfloat16
ALU = mybir.AluOpType
AF = mybir.ActivationFunctionType
AX = mybir.AxisListType

D = 1024
DEPTH = 4
SEQ = 8192
NCORES = 8
NS = 16
DIN = 2568
DFF = 4096
ALPHA = (2 * DEPTH) ** 0.25
LN_EPS = 1e-5
RMS_EPS = 1e-6
NEG = -30000.0
NROW = 1068
NCOLP = 80

ENGS = ("pe", "act", "dve", "pool", "sp")


class Buf:
    __slots__ = ("name", "last_w", "rd_eng", "rd_dma")

    def __init__(self, name):
        self.name = name
        self.last_w = None
        self.rd_eng = {}
        self.rd_dma = []


class Op:
    __slots__ = ("eng", "fn", "deps", "is_dma", "sig", "tok", "idx", "vc", "inc")

    def __init__(self, eng, fn, is_dma, inc):
        self.eng = eng
        self.fn = fn
        self.is_dma = is_dma
        self.deps = set()
        self.sig = is_dma
        self.tok = None
        self.vc = None
        self.inc = inc


class Prog:
    def __init__(self, nc, n_dma_slots=10):
        self.nc = nc
        self.ops = []
        self.n_dma_slots = n_dma_slots
        self.live_dma = []

    def op(self, eng, fn, r=(), w=(), dma=False, inc=None):
        o = Op(eng, fn, dma, inc if inc is not None else (16 if dma else 1))
        o.idx = len(self.ops)
        deps = o.deps
        for b in r:
            lw = b.last_w
            if lw is not None:
                if lw.is_dma or dma or lw.eng != eng or eng != "pe":
                    deps.add(lw)
        for b in w:
            lw = b.last_w
            if lw is not None and (lw.is_dma or dma or lw.eng != eng or eng != "pe"):
                deps.add(lw)
            for e, ro in b.rd_eng.items():
                if dma or e != eng or eng != "pe":
                    deps.add(ro)
            for ro in b.rd_dma:
                deps.add(ro)
        for b in w:
            b.last_w = o
            b.rd_eng = {}
            b.rd_dma = []
        for b in r:
            if dma:
                b.rd_dma.append(o)
            else:
                b.rd_eng[eng] = o
        deps.discard(o)
        self.ops.append(o)
        if dma:
            self.live_dma.append(o)
        return o

    def barrier(self):
        self.barrier_fn(self)

    def emit(self, stack):
        nc = self.nc
        ops = self.ops
        for o in ops:
            for d in o.deps:
                d.sig = True
        esem = {e: stack.enter_context(nc.semaphore("s_" + e)) for e in ENGS}
        dsem = {}
        for q in ("sp", "act", "pool"):
            dsem[q] = [stack.enter_context(nc.semaphore("d_%s%d" % (q, i))) for i in range(self.n_dma_slots)]
        ecnt = {e: 0 for e in ENGS}
        dcnt = {q: 0 for q in dsem}
        duse = {q: [0] * self.n_dma_slots for q in dsem}
        dlast = {q: [None] * self.n_dma_slots for q in dsem}
        ccsem = stack.enter_context(nc.semaphore("s_cc"))
        cccnt = 0
        for o in ops:
            if o.is_dma and o.inc != 16:
                cccnt += o.inc
                o.tok = (ccsem, cccnt)
            elif o.is_dma:
                q = o.eng
                slot = dcnt[q] % self.n_dma_slots
                dcnt[q] += 1
                duse[q][slot] += o.inc
                if dlast[q][slot] is not None:
                    o.deps.add(dlast[q][slot])
                dlast[q][slot] = o
                o.tok = (dsem[q][slot], duse[q][slot])
            elif o.sig:
                ecnt[o.eng] += 1
                o.tok = (esem[o.eng], ecnt[o.eng])
        know = {e: {} for e in ENGS}
        streams = {e: [] for e in ENGS}
        nwaits = 0
        for o in ops:
            k = know[o.eng]
            st = streams[o.eng]
            for d in sorted(o.deps, key=lambda x: x.idx):
                sem, val = d.tok
                if k.get(sem, 0) < val:
                    st.append((0, sem, val))
                    nwaits += 1
                    for s2, v2 in d.vc.items():
                        if k.get(s2, 0) < v2:
                            k[s2] = v2
            st.append((1, o))
            if o.sig:
                vc = dict(k)
                vc[o.tok[0]] = o.tok[1]
                o.vc = vc
        self.stats = dict(n_ops=len(ops), n_waits=nwaits, per_eng={e: len(s) for e, s in streams.items()})

        def run(engine, st):
            for it in st:
                if it[0] == 0:
                    engine.wait_ge(it[1], it[2])
                else:
                    o = it[1]
                    if o.fn is None:
                        continue
                    ins = o.fn(engine)
                    if o.sig:
                        ins.then_inc(o.tok[0], o.inc)

        with nc.Block() as block:
            @block.tensor
            def _(e):
                run(e, streams["pe"])

            @block.scalar
            def _(e):
                run(e, streams["act"])

            @block.vector
            def _(e):
                run(e, streams["dve"])

            @block.gpsimd
            def _(e):
                run(e, streams["pool"])

            @block.sync
            def _(e):
                run(e, streams["sp"])


class _Stop(Exception):
    pass


def build(L, NT, use_cc=True, stage=None, debug=False):
    TP = NT * 128
    TX = TP + NS
    TA = 256
    NTA = TP // TA
    FT = min(1024, TP)
    NFT = TP // FT
    FTX = FT + NS
    nc = bass.Bass("TRN2", target_bir_lowering=False)
    P = Prog(nc)

    def din(name, shape, dt=F32):
        return nc.dram_tensor(name, list(shape), dt, kind="ExternalInput").ap()

    def dout(name, shape, dt=F32):
        return nc.dram_tensor(name, list(shape), dt, kind="ExternalOutput").ap()

    def dint(name, shape, dt=F32):
        return nc.dram_tensor(name, list(shape), dt, kind="Internal").ap()

    xT0 = din("xT0", [128, 8, TX])
    xh0 = din("xh0", [128, 8, 128])
    win = din("win", [L, 128, 8, DIN])
    wout = din("wout", [L, 128, 8, D])
    w1 = din("w1", [L, 8, 128, 8, 512])
    w2 = din("w2", [L, 8, 128, 32, 128])
    rowp = din("rowp", [L, 1, NROW])
    colp = din("colp", [L, 128, NCOLP])
    wsT = din("wsT", [L, 128, 4, 128])
    bsd = din("bsd", [L, 4, 128])
    cst_f = din("cst_f", [128, 3, 128])
    cst_m = din("cst_m", [128, 3, 512])
    selp = din("selp", [128, 8])
    st_k = din("st_k", [L, NS, 128, 128])
    st_v = din("st_v", [L, NS, 128, 128])
    st_c = din("st_c", [L, NS * 3, 1024])
    st_h = din("st_h", [L, NS * 8, 64 * 128])

    yT = dout("yT", [128, 8, TX])
    kP = dout("kP", [L, 128, 128])
    vP = dout("vP", [L, 128, 128])
    cP = dout("cP", [L, 3, 1024])
    hP = dout("hP", [L, 128, 512])
    kS = dout("kS", [L, NS, 128])
    vS = dout("vS", [L, NS, 128])
    cS = dout("cS", [L, NS, 3, 1024])
    hS = dout("hS", [L, NS * 8, 64 * 128])
    vnS = dout("vnS", [L, NS, 256])

    cc_st_in = dint("cc_st_in", [128, 520])
    cc_st_out = dint("cc_st_out", [4 * 128, 520])
    cc_x_in = dint("cc_x_in", [128, 1024])
    cc_x_out = dint("cc_x_out", [4 * 128, 1024])
    sbn = dint("sbn", [L, NS, 1048])
    sbx = dint("sbx", [L, NS, 512])
    sbB = dint("sbB", [L, NS, 2, 4, 128])
    sbC = dint("sbC", [L, NS, 2, 4, 128])
    sby = dint("sby", [L, NS * 8, 64])
    sbd = dint("sbd", [L, NS, 8])
    xsc = dint("xsc", [128, 6, NT * 128], BF16)
    B_ccst = Buf("ccst"); B_ccst_o = Buf("ccsto"); B_ccx = Buf("ccx"); B_ccx_o = Buf("ccxo")
    B_sbn = Buf("sbn"); B_sbx = Buf("sbx"); B_sbB = Buf("sbB"); B_sbC = Buf("sbC"); B_sby = Buf("sby"); B_sbd = Buf("sbd")
    B_kS = Buf("kS"); B_vS = Buf("vS")
    B_xsc = [Buf("xsc%d" % i) for i in range(NT // 2)]
    outbufs = []

    with ExitStack() as st:
        def sb(name, shape, dt=F32):
            return st.enter_context(nc.sbuf_tensor(name, list(shape), dt))

        def ps(name, shape, dt=F32):
            return st.enter_context(nc.psum_tensor(name, list(shape), dt))

        X = sb("X", [128, 8, TX]); B_X = [Buf("X%d" % i) for i in range(NTA + 1)]
        identf = sb("identf", [128, 128]); Uf = sb("Uf", [128, 128]); onesf = sb("onesf", [128, 128])
        identb = sb("identb", [128, 128], BF16); onesb = sb("onesb", [128, 128], BF16)
        negA = sb("negA", [128, 512], BF16); negA1 = sb("negA1", [128, 512], BF16); negS = sb("negS", [128, 512], BF16)
        sel = sb("sel", [128, 8]); epsc = sb("epsc", [128, 4]); zerosb = sb("zerosb", [128, 512], BF16)
        B_c = Buf("consts")
        rowt = sb("rowt", [128, NROW]); colt = sb("colt", [128, NCOLP]); B_par = Buf("par")
        wsTb = sb("wsTb", [128, 4, 128], BF16); bsb = sb("bsb", [128, 2, 128])
        Dg = sb("Dg", [128, 4, 128], BF16)
        a_bc = sb("a_bc", [128, 8]); esink = sb("esink", [128, 4]); B_der = Buf("der")
        hst = sb("hst", [128, 512]); hb = sb("hb", [128, 512], BF16); B_h = Buf("h"); B_hb = Buf("hb")
        halo3 = sb("halo3", [128, 8, 3]); B_halo3 = Buf("halo3")
        kT = sb("kT", [128, 128 + TA], BF16); B_kT = Buf("kT")
        vaug = sb("vaug", [128, 3, 2, 65], BF16); B_va = [Buf("va%d" % i) for i in range(3)]
        xh = sb("xh", [128, 8, 128]); B_xh = Buf("xh")
        ostage = sb("ostage", [128, 256]); B_os = Buf("ostage")
        sm = sb("sm", [128, 16, 16]); B_sm = Buf("sm"); B_smg = Buf("smg"); B_sma = Buf("sma"); B_smr = Buf("smr"); B_sms = Buf("sms")

        RA_BYTES = 116 * 1024
        RA = sb("RA", [128, RA_BYTES // 4])
        ra_off = [0]

        def carve(shape, dt=F32, reset=None):
            if reset is not None:
                ra_off[0] = reset
            n = 1
            for s_ in shape[1:]:
                n *= s_
            esz = 4 if dt == F32 else 2
            nb = (n * esz + 31) // 32 * 32
            lo = ra_off[0] // 4
            ap = RA[0:shape[0], lo:lo + nb // 4]
            ra_off[0] += nb
            assert ra_off[0] <= RA_BYTES, ("RA overflow", ra_off[0])
            if dt != F32:
                ap = ap.bitcast(dt)
            ap = ap[:, 0:n]
            if len(shape) == 3:
                ap = ap.rearrange("p (a b) -> p a b", a=shape[1])
            elif len(shape) == 4:
                ap = ap.rearrange("p (a b c) -> p a b c", a=shape[1], b=shape[2])
            return ap

        winb = carve([128, 8, DIN], BF16, reset=0); B_win = Buf("win")
        woutb = carve([128, 8, D], BF16); B_wout = Buf("wout")
        xb = carve([128, 8, TA], BF16); B_xb = Buf("xb")
        xhb = xb[:, :, 0:128]; B_xhb = B_xb
        junk = carve([128, 512], BF16); B_junk = Buf("junk")
        workA = ra_off[0]
        G16 = ra_off[0]; B_g16 = Buf("g16")
        gx = carve([128, 4, 1024]); B_gx = B_g16
        gst = carve([128, 4, 520], reset=G16); B_gst = B_g16
        wsTf = carve([128, 4, 128]);
        cstage = carve([128, 3, 512], reset=G16)
        xbB = carve([128, 8, TA], BF16, reset=G16); gauB = carve([128, 2, TA]); qTB = carve([128, 2, TA], BF16); xactB = carve([128, 8, TA], BF16)
        assert ra_off[0] <= G16 + 16384
        Dcv = carve([128, 6, 4, 128], BF16, reset=G16)
        rawb = [carve([128, TA + 4], BF16) for _ in range(2)]; B_rawb = [Buf("rawb0"), Buf("rawb1")]
        assert ra_off[0] <= G16 + 8320
        ra_off[0] = G16 + 16384
        gau = carve([128, 2, TA]); B_gau = Buf("gau")
        qT = carve([128, 2, TA], BF16); B_qT = Buf("qT")
        raw = [carve([128, TA + 3]) for _ in range(2)]; B_raw = [Buf("raw0"), Buf("raw1")]
        acc = carve([128, TA]); B_acc = Buf("acc")
        xact = carve([128, 8, TA], BF16); B_xact = Buf("xact")
        sqb = xact; B_sqb = B_xact
        gav = carve([128, 256]); B_gav = Buf("gav")
        vn = carve([128, 256], BF16); B_vn = Buf("vn")
        vnf = carve([128, 256]); B_vnf = Buf("vnf")
        sz = carve([128, 2, 512]); B_sz = [Buf("sz0"), Buf("sz1")]
        Eb = [carve([128, 512], BF16) for _ in range(2)]; B_E = [Buf("E0"), Buf("E1")]
        ob = carve([128, 256], BF16); B_ob = Buf("ob")
        Rt = carve([128, 1024]); B_R = Buf("R")
        LT = carve([128, 8, 128], BF16); B_LT = Buf("LT")
        Gt = carve([128, 8, 128], BF16); B_G = Buf("G")
        xstok = carve([128, 512], BF16); B_xstok = Buf("xstok")
        Btok = carve([128, 256], BF16); B_Btok = Buf("Btok")
        xw = carve([128, 512], BF16); B_xw = Buf("xw")
        yt1 = carve([128, 512]); B_yt1 = Buf("yt1")
        yt2 = carve([128, 512]); B_yt2 = Buf("yt2")
        nmean = yt1; B_nm = B_yt1
        lvar = yt2; B_lv = B_yt2
        ocb = carve([128, 512], BF16); B_oc = Buf("oc")
        catT = carve([128, 8, TA], BF16); B_cat = Buf("cat")
        xb2 = [xb, xbB]; B_xb2 = [B_xb, Buf("xbB")]
        gau2 = [gau, gauB]; B_gau2 = [B_gau, Buf("gauB")]
        qT2 = [qT, qTB]; B_qT2 = [B_qT, Buf("qTB")]
        xact2 = [xact, xactB]; B_xact2 = [B_xact, Buf("xactB")]
        cpst = sz.rearrange("p a b -> p (a b)")
        endA = ra_off[0]
        s_tok = carve([NS, DIN], reset=workA); B_stok = Buf("stok")
        s_pk = carve([NS, 1048]); B_spk = Buf("spk")
        s_misc = carve([NS, 1024]); B_smisc = Buf("smisc")
        s_act = carve([NS, 1024]); B_sact = Buf("sact")
        s_cat = carve([NS, 1024], BF16); B_scat = Buf("scat")
        s_catT = carve([128, 8, NS], BF16); B_scatT = Buf("scatT")
        s_q = carve([128, 2, NS], BF16); B_sq = Buf("s_q")
        s_E = carve([128, 64], BF16); B_sE = Buf("sE")
        s_rd = carve([128, 64]); B_srd = Buf("srd")
        s_sq = carve([128, 8, NS], BF16); B_ssq = Buf("ssq")
        s_nm = carve([128, 16]); B_snm = Buf("snm")
        s_lv = carve([128, 16]); B_slv = Buf("slv")
        sampB = ra_off[0]
        s_kb = carve([128, NS, 128], BF16); B_skb = Buf("skb")
        s_vb = carve([128, NS, 128], BF16); B_svb = Buf("svb")
        s_kT = carve([128, NS, 128], BF16); B_skT = Buf("skT")
        s_hc = carve([NS * 3, 1024]); B_shc = Buf("shc")
        s_hist = carve([128, 8, NS * 3]); B_shist = Buf("shist")
        s_xfm = carve([128, 8, NS]); B_sxfm = Buf("sxfm")
        s_cv = carve([128, 8, NS]); B_scv = Buf("scv")
        s_cv2 = carve([128, 8, NS]); B_scv2 = Buf("scv2")
        endS1 = ra_off[0]
        s_h = carve([128, 32, 128], reset=sampB); B_sh = Buf("sh")
        s_tmp = carve([128, 16, 128]); B_stmp = Buf("stmp")
        s_x8 = carve([128, 328]); B_sx8 = Buf("sx8")
        s_y8 = carve([128, 64]); B_sy8 = Buf("sy8")
        endS2 = ra_off[0]
        hT = carve([128, 32, FTX], BF16, reset=0); B_hT = Buf("hT")
        w1_off = ra_off[0]
        w1b = [carve([128, 8, 256], BF16) for _ in range(2)]; B_w1b = [Buf("w1b0"), Buf("w1b1")]
        assert ra_off[0] - w1_off == 8192
        sqbB = RA[:, w1_off // 4:w1_off // 4 + 2048].bitcast(BF16).rearrange("p (a b) -> p a b", a=8)
        w2_off = ra_off[0]
        w2b = [carve([128, 32, 128], BF16) for _ in range(2)]; B_w2b = [Buf("w2b0"), Buf("w2b1")]
        assert ra_off[0] - w2_off == 16384
        lnq = RA[:, w2_off // 4:w2_off // 4 + 2048].bitcast(BF16).rearrange("p (a b) -> p a b", a=8)
        lnx = RA[:, w2_off // 4 + 2048:w2_off // 4 + 4096].bitcast(BF16).rearrange("p (a b) -> p a b", a=8)
        x1b = carve([128, 8, FTX], BF16); B_x1b = Buf("x1b")
        rt = [carve([128, 512], BF16) for _ in range(2)]; B_rt = [Buf("rt0"), Buf("rt1")]
        nmeanB = carve([128, 512]); lvarB = carve([128, 512]); B_nmB = Buf('nmB'); B_lvB = Buf('lvB')
        endB = ra_off[0]
        carve_stats = dict(endA=endA, endS1=endS1, endS2=endS2, endB=endB)

        pD = [ps("pD%d" % i, [128, 512]) for i in range(2)]; B_pD = [Buf("pD0"), Buf("pD1")]
        pT0 = ps("pT0", [128, 512]); B_pT0 = Buf("pT0")
        pT1 = ps("pT1", [128, 512]); B_pT1 = Buf("pT1")
        pS = [ps("pS%d" % i, [128, 512]) for i in range(2)]; B_pS = [Buf("pS0"), Buf("pS1")]
        pY = ps("pY", [128, 512]); B_pY = Buf("pY")
        pMT = ps("pMT", [128, 512])
        pM = pMT[:, 0:256]; B_pM = Buf("pM")
        pTP = pMT[:, 256:512].bitcast(BF16); B_pTP = Buf("pTP")

        def mm(out, lhsT, rhs, start, r, w, stop=True):
            P.op("pe", lambda e: e.matmul(out, lhsT=lhsT, rhs=rhs, start=start, stop=stop, skip_group_check=True), r=r, w=w)

        def tr(out, in_, ident, r, w):
            P.op("pe", lambda e: e.transpose(out, in_, ident), r=r, w=w)

        def act(out, in_, func, r, w, **kw):
            P.op("act", lambda e: e.activation(out=out, in_=in_, func=func, **kw), r=r, w=w)

        def vop(eng, name, r, w, **kw):
            P.op(eng, lambda e: getattr(e, name)(**kw), r=r, w=w)

        def dma(q, out, in_, r, w, **kw):
            P.op(q, lambda e: e.dma_start(out=out, in_=in_, **kw), r=r, w=w, dma=True)

        def odma(out, in_, r):
            b = Buf("o%d" % len(outbufs))
            outbufs.append(b)
            dma("sp", out, in_, r, [b])
            return b

        def dbg(name, ap, r, dt=F32):
            if not debug:
                return
            t_ = nc.dram_tensor("dbg_" + name, list(ap.shape), dt, kind="ExternalOutput").ap()
            odma(t_, ap, r)

        def smt(slot, n=16):
            return sm[:, slot, 0:n]

        bar_sb = sb("bar_sb", [128, 8]); bar_d = dint("bar_d", [128, 2])

        def _barrier(P_):
            b0, b1, b2, b3, b4 = Buf("b0"), Buf("b1"), Buf("b2"), Buf("b3"), Buf("b4")
            mm(pM[:, 0:2], zerosb[:, 0:128], zerosb[:, 0:2], True, [B_c], [B_pM, b0])
            act(bar_sb[:, 0:2], pM[:, 0:2], AF.Copy, [B_pM, b0], [b1])
            vop("dve", "tensor_copy", [b1], [b2], out=bar_sb[:, 2:4], in_=bar_sb[:, 0:2])
            vop("pool", "tensor_copy", [b2], [b3], out=bar_sb[:, 4:6], in_=bar_sb[:, 2:4])
            o = P_.op("sp", lambda e: e.dma_start(out=bar_d, in_=bar_sb[:, 4:6]), r=[b3], w=[b4], dma=True)
            o.deps.update(x for x in P_.live_dma if x is not o)
            P_.live_dma = []
            for e in ENGS:
                P_.op(e, None, r=[b4])
        P.barrier_fn = _barrier

        dq = [0]

        def pDn():
            dq[0] ^= 1
            return pD[dq[0]], B_pD[dq[0]]

        def v8(ap):
            return ap.rearrange("p (h d) -> p h d", h=8)

        dma("sp", cstage[:, :, 0:128], cst_f, [], [B_c, B_g16])
        for dst, j in ((identf, 0), (Uf, 1), (onesf, 2), (identb, 0), (onesb, 2)):
            vop("dve", "tensor_copy", [B_c], [B_c], out=dst[:], in_=cstage[:, j, 0:128])
        dma("sp", cstage, cst_m, [B_c], [B_c, B_g16])
        for dst, j in ((negA, 0), (negA1, 1), (negS, 2)):
            vop("dve", "tensor_copy", [B_c], [B_c], out=dst[:], in_=cstage[:, j, :])
        dma("sp", sel[:], selp, [], [B_c])
        vop("dve", "memset", [], [B_c], ap=epsc[:, 0:1], constant=LN_EPS)
        vop("dve", "memset", [], [B_c], ap=zerosb[:], constant=0.0)
        vop("dve", "memset", [], [B_c], ap=epsc[:, 1:2], constant=RMS_EPS)
        vop("dve", "memset", [], [B_c], ap=epsc[:, 2:3], constant=1.0)
        for i in range(NTA):
            dma("sp", X[:, :, i * TA:(i + 1) * TA], xT0[:, :, i * TA:(i + 1) * TA], [], [B_X[i]])
        dma("sp", X[:, :, TP:TX], xT0[:, :, TP:TX], [], [B_X[NTA]])
        dma("sp", xh[:], xh0, [], [B_xh])
        P.barrier()

        def layer_norm_pieces(cols, n, Bxs, g_off, b_off, xnb_ap, sqb_ap, Bxnb, Bsq, nm_ap, lv_ap, B_nm, B_lv, bk0=None, bk1=None):
            Xs = X[:, :, cols]
            q0, Bq0 = bk0 if bk0 is not None else (pT0, B_pT0)
            q1, Bq1 = bk1 if bk1 is not None else (pT1, B_pT1)
            nm = nm_ap[:, 0:n]; lv = lv_ap[:, 0:n]

            def s1():
                act(xnb_ap, Xs, AF.Copy, Bxs, Bxnb)
                act(sqb_ap, Xs, AF.Square, Bxs, Bsq)

            def s2():
                for kc in range(8):
                    mm(q0[:, 0:n], onesb[:], xnb_ap[:, kc, :], kc == 0, Bxnb + [B_c], [Bq0], stop=(kc == 7))
                for kc in range(8):
                    mm(q1[:, 0:n], onesb[:], sqb_ap[:, kc, :], kc == 0, Bsq + [B_c], [Bq1], stop=(kc == 7))
                vop("dve", "tensor_scalar", [Bq0], [B_nm], out=nm, in0=q0[:, 0:n], scalar1=-1.0 / D, scalar2=None, op0=ALU.mult)
                vop("dve", "tensor_tensor", [B_nm], [B_lv], out=lv, in0=nm, in1=nm, op=ALU.mult)
                vop("dve", "scalar_tensor_tensor", [Bq1, B_lv], [B_lv], out=lv, in0=q1[:, 0:n], scalar=1.0 / D, in1=lv, op0=ALU.mult, op1=ALU.subtract)
                act(lv, lv, AF.Ln, [B_lv, B_c], [B_lv], bias=epsc[:, 0:1])
                act(lv, lv, AF.Exp, [B_lv], [B_lv], scale=-0.5)

            def s3():
                vop("dve", "tensor_tensor", Bxs + [B_nm], Bxs, out=Xs, in0=Xs, in1=nm.unsqueeze(1).broadcast_to([128, 8, n]), op=ALU.add)
                vop("dve", "tensor_tensor", Bxs + [B_lv], Bxs, out=Xs, in0=Xs, in1=lv.unsqueeze(1).broadcast_to([128, 8, n]), op=ALU.mult)

            def s4():
                for kc in range(8):
                    vop("dve", "tensor_scalar", Bxs + [B_par], Bxs, out=X[:, kc, cols], in0=X[:, kc, cols],
                        scalar1=colt[:, g_off + kc:g_off + kc + 1], scalar2=colt[:, b_off + kc:b_off + kc + 1], op0=ALU.mult, op1=ALU.add)
            return [s1, s2, s3, s4]

        def layer_norm_fm(*a, **k):
            for pc in layer_norm_pieces(*a, **k):
                pc()

        def ck(k):
            if stage is not None and stage == k:
                raise _Stop()

        for l in range(L):
          try:
            last = (l == L - 1)
            can_prefetch = (32 * FTX * 2 >= 8 * DIN * 2 + 8 * D * 2)
            if l == 0 or not can_prefetch:
                for kc in range(8):
                    dma("pool", winb[:, kc, :], win[l, :, kc, :], [], [B_win])
                dma("pool", woutb, wout[l], [], [B_wout])
            dma("sp", rowt[:], rowp[l].partition_broadcast(128), [], [B_par])
            dma("sp", colt[:], colp[l], [], [B_par])
            dma("sp", wsTf, wsT[l], [], [B_par, B_g16])
            for h in range(4):
                dma("sp", bsb[(h % 2) * 64:(h % 2) * 64 + 64, h // 2, :], bsd[l, h:h + 1, :].partition_broadcast(64), [], [B_par])
            vop("dve", "tensor_tensor", [B_par, B_c, B_g16], [B_der], out=wsTb[:], in0=wsTf, in1=Uf[:].unsqueeze(1).broadcast_to([128, 4, 128]), op=ALU.mult)
            act(a_bc[:], rowt[:, 1032:1040], AF.Exp, [B_par], [B_der])
            vop("dve", "tensor_scalar", [B_der], [B_der], out=a_bc[:], in0=a_bc[:], scalar1=-1.0, scalar2=None, op0=ALU.mult)
            act(esink[:], rowt[:, 1040:1044], AF.Exp, [B_par], [B_der])
            for j in range(4):
                vop("dve", "tensor_scalar", [B_par, B_c], [B_der], out=Dg[:, j, :], in0=identf[:], scalar1=colt[:, 72 + j:73 + j], scalar2=None, op0=ALU.mult)
            for s_ in range(3):
                vop("pool", "memset", [], [B_va[s_]], ap=vaug[:, s_, :, 64:65], constant=1.0)
            if use_cc:
                for cc in range(6):
                    for t in range(4):
                        vop("dve", "tensor_scalar", [B_par, B_c, B_g16], [B_g16], out=Dcv[:, cc, t, :], in0=identf[:],
                            scalar1=colt[:, 32 + cc * 4 + t:33 + cc * 4 + t], scalar2=None, op0=ALU.mult)
            g_bc = rowt[:, 0:256]; bv_bc = rowt[:, 256:512]; gn_bc = rowt[:, 512:1024]; dtb_bc = rowt[:, 1024:1032]

            def halo_project():
                act(xhb, xh[:], AF.Copy, [B_xh], [B_xhb])
                p, Bp = pDn()
                for kc in range(8):
                    mm(p[:, 0:128], winb[:, kc, 512:640], xhb[:, kc, :], kc == 0, [B_win, B_xhb], [Bp], stop=(kc == 7))
                act(kT[:, 0:128], p[:, 0:128], AF.Copy, [Bp], [B_kT])
                p, Bp = pDn()
                for kc in range(8):
                    mm(p[:, 0:128], xhb[:, kc, :], winb[:, kc, 1920:2048], kc == 0, [B_win, B_xhb], [Bp], stop=(kc == 7))
                vop("dve", "tensor_copy", [Bp], [B_va[0]], out=vaug[:, 0, :, 0:64], in_=p[:, 0:128].rearrange("p (a b) -> p a b", a=2))
                for cc in range(8):
                    p, Bp = pDn()
                    for kc in range(8):
                        mm(p[:, 0:4], winb[:, kc, 640 + cc * 128:768 + cc * 128], xhb[:, kc, 124:128], kc == 0, [B_win, B_xhb], [Bp], stop=(kc == 7))
                    vop("dve", "tensor_copy", [Bp], [B_halo3], out=halo3[:, cc, :], in_=p[:, 1:4])

            def cast_tile(i, par=0):
                act(xb2[par], X[:, :, i * TA:(i + 1) * TA], AF.Copy, [B_X[i]], [B_xb2[par]])

            def conv_chunk(cc, ri, par=0):
                p, Bp = pDn()
                j = 5 + cc
                for kc in range(8):
                    mm(p[:, 0:TA], winb[:, kc, j * 128:(j + 1) * 128], xb2[par][:, kc, :], kc == 0, [B_win, B_xb2[par]], [Bp], stop=(kc == 7))
                rw = raw[ri]; Br = B_raw[ri]
                act(rw[:, 3:3 + TA], p[:, 0:TA], AF.Copy, [Bp], [Br])
                vop("dve", "tensor_copy", [B_halo3], [Br], out=rw[:, 0:3], in_=halo3[:, cc, :])
                vop("dve", "tensor_copy", [Br], [B_halo3], out=halo3[:, cc, :], in_=rw[:, TA:TA + 3])

                def cw(t):
                    return colt[:, 32 + cc * 4 + t:33 + cc * 4 + t]
                vop("dve", "tensor_scalar", [Br, B_par], [B_acc], out=acc, in0=rw[:, 0:TA], scalar1=cw(0), scalar2=None, op0=ALU.mult)
                for t in range(1, 4):
                    vop("dve", "scalar_tensor_tensor", [Br, B_par, B_acc], [B_acc], out=acc, in0=rw[:, t:t + TA], scalar=cw(t), in1=acc, op0=ALU.mult, op1=ALU.add)
                act(xact2[par][:, cc, :], acc, AF.Silu, [B_acc, B_par], [B_xact2[par]], bias=colt[:, 64 + cc:65 + cc])

            pp_banks = [(pD[0], B_pD[0]), (pD[1], B_pD[1]), (pT0, B_pT0), (pT1, B_pT1)]
            pp_i = [0]

            def pp_bank():
                pp_i[0] = (pp_i[0] + 1) % 4
                return pp_banks[pp_i[0]]

            def conv_pe_a(cc, ri):
                p, Bp = pp_bank()
                j = 5 + cc
                for kc in range(8):
                    mm(p[:, 0:TA], winb[:, kc, j * 128:(j + 1) * 128], xb[:, kc, :], kc == 0, [B_win, B_xb], [Bp], stop=(kc == 7))
                rw = rawb[ri]; Br = B_rawb[ri]
                act(rw[:, 3:3 + TA], p[:, 0:TA], AF.Copy, [Bp, B_g16], [Br])
                vop("pool", "tensor_copy", [B_halo3], [Br], out=rw[:, 0:3], in_=halo3[:, cc, :])
                vop("pool", "tensor_copy", [Br], [B_halo3], out=halo3[:, cc, :], in_=rw[:, TA:TA + 3])

            def conv_pe_b(cc, ri):
                rw = rawb[ri]; Br = B_rawb[ri]
                p2, Bp2 = pp_bank()
                for t in range(4):
                    mm(p2[:, 0:TA], Dcv[:, cc, t, :], rw[:, t:t + TA], t == 0, [B_g16, Br], [Bp2], stop=(t == 3))
                act(xact[:, cc, :], p2[:, 0:TA], AF.Silu, [Bp2, B_par], [B_xact], bias=colt[:, 64 + cc:65 + cc])

            def dt_chain(nch):
                n = nch * 8

                def v3(s_):
                    return sm[:, s_, 0:n].rearrange("p (c h) -> p c h", c=nch)
                vop("dve", "tensor_tensor", [B_sm, B_par], [B_sm], out=v3(0), in0=v3(0), in1=dtb_bc.unsqueeze(1).broadcast_to([128, nch, 8]), op=ALU.add)
                vop("dve", "scalar_tensor_tensor", [B_sm], [B_sm], out=smt(1, n), in0=smt(0, n), scalar=-1.0, in1=smt(0, n), op0=ALU.mult, op1=ALU.min)
                act(smt(1, n), smt(1, n), AF.Exp, [B_sm], [B_sm])
                act(smt(1, n), smt(1, n), AF.Ln, [B_sm, B_c], [B_sm], bias=epsc[:, 2:3])
                vop("dve", "scalar_tensor_tensor", [B_sm], [B_sm], out=smt(1, n), in0=smt(0, n), scalar=0.0, in1=smt(1, n), op0=ALU.max, op1=ALU.add)
                vop("dve", "tensor_tensor", [B_sm, B_der], [B_sm], out=v3(2), in0=v3(1), in1=a_bc[:].unsqueeze(1).broadcast_to([128, nch, 8]), op=ALU.mult)
                act(smt(3, n), smt(1, n), AF.Ln, [B_sm], [B_sm])
                mm(pY[:, 0:n], Uf[:], smt(2, n), True, [B_sm, B_c], [B_pY])
                mm(pY[:, 16:16 + n], onesf[:], smt(2, n), False, [B_sm, B_c], [B_pY])
                mm(pY[:, 0:32], zerosb[:, 0:128], zerosb[:, 0:32], False, [B_c], [B_pY])
                act(smt(13, n), pY[:, 0:n], AF.Copy, [B_pY], [B_sm])
                act(smt(14, n), pY[:, 16:16 + n], AF.Copy, [B_pY], [B_sm])
                act(smt(4, n), smt(13, n), AF.Exp, [B_sm], [B_sm])
                vop("dve", "tensor_tensor", [B_sm], [B_sm], out=smt(5, n), in0=smt(3, n), in1=smt(13, n), op=ALU.subtract)
                vop("dve", "tensor_tensor", [B_sm], [B_sm], out=smt(6, n), in0=smt(5, n), in1=smt(14, n), op=ALU.add)
                act(smt(6, n), smt(6, n), AF.Exp, [B_sm], [B_sm])
                act(smt(7, n), smt(14, n), AF.Exp, [B_sm], [B_sm])

            def tok_transposes(c, par=0):
                cs = slice(c * 128, (c + 1) * 128)
                for cc in range(4):
                    tr(pTP[:, cc * 128:(cc + 1) * 128], xact2[par][:, cc, cs], identb[:], [B_xact2[par], B_c], [B_pTP])
                act(xstok, pTP[:, 0:512], AF.Copy, [B_pTP], [B_xstok])
                for g in range(2):
                    tr(pTP[:, g * 128:(g + 1) * 128], xact2[par][:, 4 + g, cs], identb[:], [B_xact2[par], B_c], [B_pTP])
                vop("dve", "tensor_copy", [B_pTP], [B_Btok], out=Btok, in_=pTP[:, 0:256])

            def state_update(c, want_hb=True):
                k8 = slice(c * 8, c * 8 + 8)
                vop("pool", "tensor_tensor", [B_xstok, B_sm], [B_xw], out=v8(xw), in0=v8(xstok), in1=sm[:, 6, k8].unsqueeze(2).broadcast_to([128, 8, 64]), op=ALU.mult)
                for g in range(2):
                    mm(pY[:, g * 256:(g + 1) * 256], Btok[:, g * 128:(g + 1) * 128], xw[:, g * 256:(g + 1) * 256], g == 0, [B_Btok, B_xw], [B_pY], stop=(g == 1))
                vop("dve", "tensor_tensor", [B_h, B_sm], [B_h], out=v8(hst[:]), in0=v8(hst[:]), in1=sm[:, 7, k8].unsqueeze(2).broadcast_to([128, 8, 64]), op=ALU.mult)
                vop("dve", "tensor_tensor", [B_h, B_pY], [B_h], out=hst[:], in0=hst[:], in1=pY[:], op=ALU.add)
                if want_hb:
                    act(hb[:], hst[:], AF.Copy, [B_h], [B_hb])

            ck(1)
            halo_project()
            vop("dve", "memset", [], [B_h], ap=hst[:], constant=0.0)
            if use_cc:
                for i in range(NTA):
                    cast_tile(i)
                    conv_pe_a(0, 0)
                    for cc in range(6):
                        if cc + 1 < 6:
                            conv_pe_a(cc + 1, (cc + 1) % 2)
                        conv_pe_b(cc, cc % 2)
                    nl_ = len(P.live_dma)
                    dma("sp", xsc[:, :, i * TA:(i + 1) * TA], xact[:, 0:6, :], [B_xact], [B_xsc[i]])
                    del P.live_dma[nl_:]
                    for c in range(2):
                        p, Bp = pDn()
                        for kc in range(8):
                            mm(p[:, 0:8], xb[:, kc, c * 128:(c + 1) * 128], winb[:, kc, 2048:2056], kc == 0, [B_win, B_xb], [Bp], stop=(kc == 7))
                        vop("dve", "tensor_copy", [Bp], [B_sm], out=sm[:, 0, c * 8:c * 8 + 8], in_=p[:, 0:8])
                    dt_chain(2)
                    for c in range(2):
                        tok_transposes(c)
                        state_update(c, want_hb=False)
                    if i == 0:
                        vop("dve", "tensor_tensor", [B_sm], [B_sm], out=sm[:, 8, 0:8], in0=sm[:, 7, 0:8], in1=sm[:, 7, 8:16], op=ALU.mult)
                    else:
                        vop("dve", "tensor_tensor", [B_sm], [B_sm], out=sm[:, 8, 0:8], in0=sm[:, 8, 0:8], in1=sm[:, 7, 0:8], op=ALU.mult)
                        vop("dve", "tensor_tensor", [B_sm], [B_sm], out=sm[:, 8, 0:8], in0=sm[:, 8, 0:8], in1=sm[:, 7, 8:16], op=ALU.mult)
                ck(2)
                dma("sp", cc_st_in[:, 0:512], hst[:], [B_h], [B_ccst])
                dma("sp", cc_st_in[:, 512:520], sm[:, 8, 0:8], [B_sm], [B_ccst])
                P.op("pool", lambda e: e.collective_compute("AllGather", ALU.bypass, replica_groups=[[0, 1, 2, 3], [4, 5, 6, 7]],
                                                            ins=[cc_st_in], outs=[cc_st_out]), r=[B_ccst], w=[B_ccst_o], dma=True, inc=1)
                dma("sp", gst, cc_st_out.rearrange("(r p) f -> p r f", p=128), [B_ccst_o], [B_gst])
                vop("dve", "memset", [], [B_h], ap=hst[:], constant=0.0)
                for r_ in range(3):
                    vop("dve", "tensor_tensor", [B_h, B_gst], [B_yt1], out=v8(yt1), in0=v8(hst[:]), in1=gst[:, r_, 512:520].unsqueeze(2).broadcast_to([128, 8, 64]), op=ALU.mult)
                    vop("dve", "tensor_tensor", [B_yt1, B_gst], [B_yt1], out=yt1, in0=yt1, in1=gst[:, r_, 0:512], op=ALU.add)
                    vop("dve", "tensor_tensor", [B_yt1, B_h], [B_yt1], out=yt1, in0=yt1, in1=hst[:], op=ALU.subtract)
                    vop("dve", "scalar_tensor_tensor", [B_yt1, B_h, B_c], [B_h], out=hst[:], in0=yt1, scalar=sel[:, r_:r_ + 1], in1=hst[:], op0=ALU.mult, op1=ALU.add)
                halo_project()
            act(hb[:], hst[:], AF.Copy, [B_h], [B_hb])

            ck(3)
            P.barrier()
            vslot = [0]

            def stage_proj(i, par):
                pieces = []
                pieces.append(lambda: cast_tile(i, par))

                def p_gau(j):
                    p, Bp = pDn()
                    for kc in range(8):
                        mm(p[:, 0:TA], winb[:, kc, j * 128:(j + 1) * 128], xb2[par][:, kc, :], kc == 0, [B_win, B_xb2[par]], [Bp], stop=(kc == 7))
                    act(gau2[par][:, j, :], p[:, 0:TA], AF.Gelu_apprx_tanh, [Bp], [B_gau2[par]])

                def p_q(j):
                    p, Bp = pDn()
                    for kc in range(8):
                        mm(p[:, 0:TA], winb[:, kc, (2 + j) * 128:(3 + j) * 128], xb2[par][:, kc, :], kc == 0, [B_win, B_xb2[par]], [Bp], stop=(kc == 7))
                    act(qT2[par][:, j, :], p[:, 0:TA], AF.Copy, [Bp], [B_qT2[par]], scale=0.125)

                def p_k():
                    p, Bp = pDn()
                    for kc in range(8):
                        mm(p[:, 0:TA], winb[:, kc, 512:640], xb2[par][:, kc, :], kc == 0, [B_win, B_xb2[par]], [Bp], stop=(kc == 7))
                    if i > 0:
                        vop("pool", "tensor_copy", [B_kT], [B_kT], out=kT[:, 0:128], in_=kT[:, TA:TA + 128])
                    act(kT[:, 128:128 + TA], p[:, 0:TA], AF.Copy, [Bp], [B_kT])
                for j in range(2):
                    pieces.append(lambda j=j: p_gau(j))
                for j in range(2):
                    pieces.append(lambda j=j: p_q(j))
                pieces.append(p_k)
                if use_cc:
                    def p_load():
                        nl_ = len(P.live_dma)
                        dma("sp", xact2[par][:, 0:6, :], xsc[:, :, i * TA:(i + 1) * TA], [B_xsc[i]], [B_xact2[par]])
                        del P.live_dma[nl_:]
                    pieces.insert(1, p_load)
                    for cc in range(6, 8):
                        pieces.append(lambda cc=cc: conv_chunk(cc, cc % 2, par))
                else:
                    for cc in range(8):
                        pieces.append(lambda cc=cc: conv_chunk(cc, cc % 2, par))
                return pieces

            pending = []

            def fill(n=1):
                for _ in range(n):
                    if pending:
                        pending.pop(0)()

            def stage_attn(i, par):
                for c in range(2):
                    cs = slice(c * 128, (c + 1) * 128)
                    gtok = i * 2 + c
                    for kc in range(8):
                        mm(pT0[:, 0:392], xb2[par][:, kc, cs], winb[:, kc, 1664:2056], kc == 0, [B_win, B_xb2[par]], [B_pT0], stop=(kc == 7))
                    for kc in range(8):
                        mm(pT1[:], xb2[par][:, kc, cs], winb[:, kc, 2056:2568], kc == 0, [B_win, B_xb2[par]], [B_pT1], stop=(kc == 7))
                    act(gav, pT0[:, 0:256], AF.Gelu_apprx_tanh, [B_pT0], [B_gav, B_smg], accum_out=sm[:, 9, 0:1])
                    prv = vslot[0]
                    cur = (vslot[0] + 1) % 3
                    vslot[0] = cur
                    vop("dve", "tensor_copy", [B_pT0], [B_va[cur]], out=vaug[:, cur, :, 0:64], in_=pT0[:, 256:384].rearrange("p (a b) -> p a b", a=2))
                    vop("dve", "tensor_copy", [B_pT0], [B_sm], out=sm[:, 0, c * 8:c * 8 + 8], in_=pT0[:, 384:392])
                    act(sz[:, c, :], pT1[:], AF.Silu, [B_pT1], [B_sz[c]])
                    if gtok == NT - 1:
                        vop("dve", "tensor_copy", [B_pT0], [B_os], out=ostage[:, 0:128], in_=pT0[:, 256:384])
                        odma(vP[l], ostage[:, 0:128], [B_os])
                        p, Bp = pDn()
                        for kc in range(8):
                            mm(p[:, 0:128], xb2[par][:, kc, cs], winb[:, kc, 512:640], kc == 0, [B_win, B_xb2[par]], [Bp], stop=(kc == 7))
                        vop("dve", "tensor_copy", [Bp], [B_os], out=ostage[:, 128:256], in_=p[:, 0:128])
                        odma(kP[l], ostage[:, 128:256], [B_os])
                    act(junk[:, 0:256], gav, AF.Square, [B_gav], [B_junk, B_smg], accum_out=sm[:, 9, 1:2])
                    vop("dve", "tensor_scalar", [B_smg], [B_smg], out=sm[:, 9, 2:3], in0=sm[:, 9, 0:1], scalar1=-1.0 / 256, scalar2=None, op0=ALU.mult)
                    vop("dve", "tensor_tensor", [B_smg], [B_smg], out=sm[:, 9, 3:4], in0=sm[:, 9, 2:3], in1=sm[:, 9, 2:3], op=ALU.mult)
                    vop("dve", "scalar_tensor_tensor", [B_smg], [B_smg], out=sm[:, 9, 3:4], in0=sm[:, 9, 1:2], scalar=1.0 / 256, in1=sm[:, 9, 3:4], op0=ALU.mult, op1=ALU.subtract)
                    act(sm[:, 9, 3:4], sm[:, 9, 3:4], AF.Ln, [B_smg, B_c], [B_smg], bias=epsc[:, 0:1])
                    act(sm[:, 9, 3:4], sm[:, 9, 3:4], AF.Exp, [B_smg], [B_smg], scale=-0.5)
                    vop("dve", "tensor_scalar", [B_gav, B_smg], [B_vnf], out=vnf, in0=gav, scalar1=sm[:, 9, 2:3], scalar2=sm[:, 9, 3:4], op0=ALU.add, op1=ALU.mult)
                    vop("dve", "tensor_tensor", [B_vnf, B_par], [B_vnf], out=vnf, in0=vnf, in1=g_bc, op=ALU.mult)
                    vop("dve", "tensor_tensor", [B_vnf, B_par], [B_vn], out=vn, in0=vnf, in1=bv_bc, op=ALU.add)
                    for h in range(4):
                        mm(pM[(h % 2) * 64:(h % 2) * 64 + 64, (h // 2) * 128:(h // 2) * 128 + 128], vn[:, h * 64:(h + 1) * 64], wsTb[:, h, :], True, [B_vn, B_der], [B_pM])
                    yv = yt2[:, 0:256].rearrange("p (a b) -> p a b", a=2)
                    vop("dve", "tensor_tensor", [B_pM, B_par], [B_yt2], out=yv, in0=pM.rearrange("p (a b) -> p a b", a=2), in1=bsb[:], op=ALU.add)
                    vop("dve", "tensor_tensor", [B_yt2, B_gau2[par]], [B_cat], out=catT[:, 0:2, cs], in0=yv, in1=gau2[par][:, :, cs], op=ALU.mult)
                    ngA = negA1 if gtok == 0 else negA
                    for kv in range(2):
                        S4 = pS[kv][:].rearrange("p (g b t) -> p g b t", g=2, b=2)
                        mm(pS[kv][:], identb[:], ngA[:], True, [B_c], [B_pS[kv]], stop=False)
                        for gi in range(2):
                            for blk in range(2):
                                kc0 = c * 128 + blk * 128
                                mm(S4[:, gi, blk, :], kT[kv * 64:kv * 64 + 64, kc0:kc0 + 128], qT2[par][kv * 64:kv * 64 + 64, gi, cs], False,
                                   [B_kT, B_qT2[par]], [B_pS[kv]], stop=(gi == 1 and blk == 1))
                        act(Eb[kv], pS[kv][:], AF.Exp, [B_pS[kv]], [B_E[kv]])
                    AO = pY[:, 0:260].rearrange("p (h d) -> p h d", h=4)
                    first = True
                    for kv in range(2):
                        E4 = Eb[kv].rearrange("p (g b t) -> p g b t", g=2, b=2)
                        for gi in range(2):
                            hidx = gi * 2 + kv
                            for blk in range(2):
                                vs = prv if blk == 0 else cur
                                mm(AO[:, hidx, :], E4[:, gi, blk, :], vaug[:, vs, kv, :], first, [B_E[kv], B_va[vs]], [B_pY], stop=(kv == 1 and gi == 1 and blk == 1))
                                first = False
                    vop("dve", "tensor_tensor", [B_pY, B_der], [B_sma], out=sm[:, 10, 0:4], in0=AO[:, :, 64], in1=esink[:], op=ALU.add)
                    vop("dve", "reciprocal", [B_sma], [B_sma], out=sm[:, 10, 4:8], in_=sm[:, 10, 0:4])
                    vop("dve", "tensor_tensor", [B_pY, B_sma], [B_ob], out=ob.rearrange("p (h d) -> p h d", h=4), in0=AO[:, :, 0:64],
                        in1=sm[:, 10, 4:8].unsqueeze(2).broadcast_to([128, 4, 64]), op=ALU.mult)
                    for j in range(2):
                        tr(pTP[:, j * 128:(j + 1) * 128], ob[:, j * 128:(j + 1) * 128], identb[:], [B_ob, B_c], [B_pTP])
                    act(catT[:, 2:4, cs], pTP[:, 0:256].rearrange("p (a b) -> p a b", a=2), AF.Copy, [B_pTP], [B_cat])

            def stage_ssd(i, par):
                dt_chain(2)
                for c in range(2):
                    cs = slice(c * 128, (c + 1) * 128)
                    k8 = slice(c * 8, c * 8 + 8)
                    tok_transposes(c, par)
                    for g in range(2):
                        mm(pT1[:, g * 256:(g + 1) * 256], xact2[par][:, 6 + g, cs], hb[:, g * 256:(g + 1) * 256], g == 0, [B_xact2[par], B_hb], [B_pT1], stop=(g == 1))
                    state_update(c)
                    vop("pool", "tensor_tensor", [B_sm, B_c], [B_R], out=v8(Rt), in0=sm[:, 2, k8].unsqueeze(2).broadcast_to([128, 8, 128]),
                        in1=Uf[:].unsqueeze(1).broadcast_to([128, 8, 128]), op=ALU.mult)
                    for hf in range(2):
                        mm(pS[hf][:], onesf[:], Rt[:, hf * 512:(hf + 1) * 512], True, [B_R, B_c], [B_pS[hf]], stop=False)
                        mm(pS[hf][:], identb[:], negS[:], False, [B_c], [B_pS[hf]])
                        for hh in range(4):
                            h = hf * 4 + hh
                            act(LT[:, h, :], pS[hf][:, hh * 128:(hh + 1) * 128], AF.Exp, [B_pS[hf], B_sm], [B_LT], bias=sm[:, 5, c * 8 + h:c * 8 + h + 1])
                    fill(1)
                    for g in range(2):
                        mm(pY[:, 64 + g * 128:192 + g * 128], xact2[par][:, 4 + g, cs], xact2[par][:, 6 + g, cs], True, [B_xact2[par]], [B_pY])
                    for g in range(2):
                        vop("dve", "tensor_tensor", [B_LT, B_pY], [B_G], out=Gt[:, g * 4:g * 4 + 4, :], in0=LT[:, g * 4:g * 4 + 4, :],
                            in1=pY[:, 64 + g * 128:192 + g * 128].unsqueeze(1).broadcast_to([128, 4, 128]), op=ALU.mult)
                    for h in range(8):
                        mm(pT0[:, h * 64:(h + 1) * 64], Gt[:, h, :], xstok[:, h * 64:(h + 1) * 64], h == 0, [B_G, B_xstok], [B_pT0], stop=False)
                    for cc in range(4):
                        mm(pT0[:, cc * 128:(cc + 1) * 128], xact2[par][:, cc, cs], Dg[:, cc, :], False, [B_xact2[par], B_der], [B_pT0], stop=(cc == 3))
                    fill(1)
                    vop("dve", "tensor_tensor", [B_pT1, B_sm], [B_yt1], out=v8(yt1), in0=v8(pT1[:]), in1=sm[:, 4, k8].unsqueeze(2).broadcast_to([128, 8, 64]), op=ALU.mult)
                    vop("dve", "tensor_tensor", [B_yt1, B_pT0], [B_yt1], out=yt1, in0=yt1, in1=pT0[:], op=ALU.add)
                    vop("dve", "tensor_tensor", [B_yt1, B_sz[c]], [B_yt2], out=yt2, in0=yt1, in1=sz[:, c, :], op=ALU.mult)
                    act(junk, yt2, AF.Square, [B_yt2], [B_junk, B_smr], accum_out=sm[:, 11, 0:1])
                    act(sm[:, 11, 1:2], sm[:, 11, 0:1], AF.Ln, [B_smr, B_c], [B_smr], bias=epsc[:, 1:2], scale=1.0 / 512)
                    act(sm[:, 11, 1:2], sm[:, 11, 1:2], AF.Exp, [B_smr], [B_smr], scale=-0.5)
                    vop("dve", "scalar_tensor_tensor", [B_yt2, B_smr, B_par], [B_oc], out=ocb, in0=yt2, scalar=sm[:, 11, 1:2], in1=gn_bc, op0=ALU.mult, op1=ALU.mult)
                    fill(1)
                    for cc in range(4):
                        tr(pTP[:, cc * 128:(cc + 1) * 128], ocb[:, cc * 128:(cc + 1) * 128], identb[:], [B_oc, B_c], [B_pTP])
                    act(catT[:, 4:8, cs], pTP[:, 0:512].rearrange("p (a b) -> p a b", a=4), AF.Copy, [B_pTP], [B_cat])
                    fill(1)

            def stage_out(i, par):
                cols = slice(i * TA, (i + 1) * TA)
                for dc in range(8):
                    p, Bp = pDn()
                    for ec in range(8):
                        mm(p[:, 0:TA], woutb[:, ec, dc * 128:(dc + 1) * 128], catT[:, ec, :], ec == 0, [B_wout, B_cat], [Bp], stop=(ec == 7))
                    vop("dve", "scalar_tensor_tensor", [B_X[i], Bp], [B_X[i]], out=X[:, dc, cols], in0=X[:, dc, cols], scalar=ALPHA, in1=p[:, 0:TA], op0=ALU.mult, op1=ALU.add)
                if i == NTA - 1:
                    for half in range(2):
                        p, Bp = pDn()
                        for kc in range(8):
                            mm(p[0:3, :], xb2[par][:, kc, TA - 3:TA], winb[:, kc, 640 + half * 512:1152 + half * 512], kc == 0, [B_win, B_xb2[par]], [Bp], stop=(kc == 7))
                        vop("dve", "tensor_copy", [Bp], B_sz, out=cpst[0:3, half * 512:(half + 1) * 512], in_=p[0:3, :])
                    odma(cP[l], cpst[0:3, :], B_sz)
                layer_norm_fm(cols, TA, [B_X[i]], 0, 8, xb2[par], xact2[par], [B_xb2[par]], [B_xact2[par]], nmean, lvar, B_nm, B_lv)

            for pc in stage_proj(0, 0):
                pc()
            for i in range(NTA):
                par = i % 2
                stage_attn(i, par)
                if i + 1 < NTA:
                    pending.extend(stage_proj(i + 1, 1 - par))
                    fill(1)
                stage_ssd(i, par)
                stage_out(i, par)
                fill(len(pending))
            vop("dve", "tensor_copy", [B_h], [B_yt1], out=yt1, in_=hst[:])
            odma(hP[l], yt1, [B_yt1])

            dbg("sm", sm[:], [B_sm]); dbg("hst", hst[:], [B_h]); dbg("gst", gst, [B_gst]); dbg("yt1", yt1, [B_yt1]); dbg("yt2", yt2, [B_yt2])
            dbg("catT", catT, [B_cat], BF16); dbg("xact", xact, [B_xact], BF16); dbg("LT", LT, [B_LT], BF16); dbg("Gt", Gt, [B_G], BF16)
            dbg("xstok", xstok, [B_xstok], BF16); dbg("sz", sz, B_sz); dbg("Rt", Rt, [B_R]); dbg("gau", gau, [B_gau]); dbg("kT", kT[:], [B_kT], BF16)
            dbg("vaug", vaug[:], B_va, BF16); dbg("Eb0", Eb[0], [B_E[0]], BF16); dbg("ob", ob, [B_ob], BF16); dbg("qT", qT, [B_qT], BF16)
            ck(4)
            P.barrier()
            SI = NTA
            scols = slice(TP, TX)
            xbs = xb[:, :, 0:NS]
            act(xbs, X[:, :, scols], AF.Copy, [B_X[SI]], [B_xb])
            ck(401)
            for (c0, c1) in ((0, 512), (512, 1024), (1024, 1536), (1536, 2048), (2048, 2560), (2560, 2568)):
                p, Bp = pDn()
                for kc in range(8):
                    mm(p[0:NS, 0:c1 - c0], xb[:, kc, 0:NS], winb[:, kc, c0:c1], kc == 0, [B_win, B_xb], [Bp], stop=(kc == 7))
                vop("dve", "tensor_copy", [Bp], [B_stok], out=s_tok[:, c0:c1], in_=p[0:NS, 0:c1 - c0])
            ck(402)
            dma("sp", kS[l], s_tok[:, 512:640], [B_stok], [B_kS])
            dma("sp", vS[l], s_tok[:, 1920:2048], [B_stok], [B_vS])
            ck(403)
            for b in range(NS):
                dma("pool", s_kb[:, b, :], st_k[l, b], [], [B_skb])
                dma("pool", s_vb[:, b, :], st_v[l, b], [], [B_svb])
            ck(41)
            dma("pool", s_kb[0:1, :, :], kS[l:l + 1], [B_kS], [B_skb])
            dma("pool", s_vb[0:1, :, :], vS[l:l + 1], [B_vS], [B_svb])
            for b0 in range(0, NS, 4):
                for b in range(b0, b0 + 4):
                    tr(pTP[:, (b - b0) * 128:(b - b0 + 1) * 128], s_kb[:, b, :], identb[:], [B_skb, B_c], [B_pTP])
                act(s_kT[:, b0:b0 + 4, :], pTP[:, 0:512].rearrange("p (a b) -> p a b", a=4), AF.Copy, [B_pTP], [B_skT])
            for j in range(2):
                p, Bp = pDn()
                for kc in range(8):
                    mm(p[:, 0:NS], winb[:, kc, (2 + j) * 128:(3 + j) * 128], xb[:, kc, 0:NS], kc == 0, [B_win, B_xb], [Bp], stop=(kc == 7))
                act(s_q[:, j, :], p[:, 0:NS], AF.Copy, [Bp], [B_sq], scale=0.125)
            for kv in range(2):
                for gi in range(2):
                    for b in range(NS):
                        col = gi * NS + b
                        mm(pS[kv][:, col:col + 1], s_kT[kv * 64:kv * 64 + 64, b, :], s_q[kv * 64:kv * 64 + 64, gi, b:b + 1], True, [B_skT, B_sq], [B_pS[kv]])
                act(s_E[:, kv * 32:(kv + 1) * 32], pS[kv][:, 0:32], AF.Exp, [B_pS[kv]], [B_sE])
            ck(42)
            mm(pY[:, 0:64], onesb[:], s_E, True, [B_sE, B_c], [B_pY])
            for kv in range(2):
                for gi in range(2):
                    hidx = gi * 2 + kv
                    c0 = kv * 32 + gi * NS
                    vop("dve", "tensor_scalar", [B_pY, B_der], [B_srd], out=s_rd[:, c0:c0 + NS], in0=pY[:, c0:c0 + NS], scalar1=esink[:, hidx:hidx + 1], scalar2=None, op0=ALU.add)
            vop("dve", "reciprocal", [B_srd], [B_srd], out=s_rd, in_=s_rd)
            pO = pT0[:, 0:2 * NS].rearrange("p (g b) -> p g b", g=2)
            for kv in range(2):
                for gi in range(2):
                    for b in range(NS):
                        col = kv * 32 + gi * NS + b
                        mm(pO[kv * 64:kv * 64 + 64, gi, b:b + 1], s_vb[:, b, kv * 64:kv * 64 + 64], s_E[:, col:col + 1], True, [B_svb, B_sE], [B_pT0])
            for kv in range(2):
                vop("dve", "tensor_tensor", [B_pT0, B_srd], [B_scatT], out=s_catT[kv * 64:kv * 64 + 64, 2:4, :], in0=pO[kv * 64:kv * 64 + 64, :, :],
                    in1=s_rd[kv * 64:kv * 64 + 64, kv * 32:kv * 32 + 32].rearrange("p (g b) -> p g b", g=2), op=ALU.mult)
            ck(43)
            sN = lambda a, b: sm[0:NS, 12, a:b]
            act(s_misc[:, 0:256], s_tok[:, 0:256], AF.Gelu_apprx_tanh, [B_stok], [B_smisc])
            act(s_misc[:, 256:512], s_tok[:, 1664:1920], AF.Gelu_apprx_tanh, [B_stok], [B_smisc, B_sms], accum_out=sN(0, 1))
            act(junk[0:NS, 0:256], s_misc[:, 256:512], AF.Square, [B_smisc], [B_junk, B_sms], accum_out=sN(1, 2))
            vop("dve", "tensor_scalar", [B_sms], [B_sms], out=sN(2, 3), in0=sN(0, 1), scalar1=-1.0 / 256, scalar2=None, op0=ALU.mult)
            vop("dve", "tensor_tensor", [B_sms], [B_sms], out=sN(3, 4), in0=sN(2, 3), in1=sN(2, 3), op=ALU.mult)
            vop("dve", "scalar_tensor_tensor", [B_sms], [B_sms], out=sN(3, 4), in0=sN(1, 2), scalar=1.0 / 256, in1=sN(3, 4), op0=ALU.mult, op1=ALU.subtract)
            act(sN(3, 4), sN(3, 4), AF.Ln, [B_sms, B_c], [B_sms], bias=epsc[0:NS, 0:1])
            act(sN(3, 4), sN(3, 4), AF.Exp, [B_sms], [B_sms], scale=-0.5)
            vnS_ = s_misc[:, 256:512]
            vop("dve", "tensor_scalar", [B_smisc, B_sms], [B_smisc], out=vnS_, in0=vnS_, scalar1=sN(2, 3), scalar2=sN(3, 4), op0=ALU.add, op1=ALU.mult)
            vop("dve", "tensor_tensor", [B_smisc, B_par], [B_smisc], out=vnS_, in0=vnS_, in1=rowt[0:NS, 0:256], op=ALU.mult)
            vop("dve", "tensor_tensor", [B_smisc, B_par], [B_smisc], out=vnS_, in0=vnS_, in1=rowt[0:NS, 256:512], op=ALU.add)
            odma(vnS[l], vnS_, [B_smisc])
            m4 = s_misc[:, 512:768].rearrange("p (h d) -> p h d", h=4)
            vop("dve", "tensor_tensor", [B_smisc, B_par], [B_smisc], out=m4, in0=vnS_.rearrange("p (h d) -> p h d", h=4),
                in1=rowt[0:NS, 1052:1056].unsqueeze(2).broadcast_to([NS, 4, 64]), op=ALU.mult)
            vop("dve", "tensor_tensor", [B_smisc, B_par], [B_smisc], out=m4, in0=m4, in1=rowt[0:NS, 1056:1060].unsqueeze(2).broadcast_to([NS, 4, 64]), op=ALU.add)
            vop("dve", "tensor_tensor", [B_smisc], [B_scat], out=s_cat[:, 0:256], in0=s_misc[:, 512:768], in1=s_misc[:, 0:256], op=ALU.mult)
            ck(44)
            dma("sp", s_hc, st_c[l], [], [B_shc])
            odma(cS[l, :, 0:2, :], st_c[l].rearrange("(b i) c -> b i c", i=3)[:, 1:3, :], [])
            odma(cS[l, :, 2, :], s_tok[:, 640:1664], [B_stok])
            for cc in range(8):
                mm(pT1[:, cc * 48:(cc + 1) * 48], s_hc[:, cc * 128:(cc + 1) * 128], identf[0:NS * 3, 0:NS * 3], cc == 0, [B_shc, B_c], [B_pT1], stop=(cc == 7))
            mm(pT1[:, 0:384], zerosb[:, 0:128], zerosb[:, 0:384], False, [B_c], [B_pT1])
            vop("dve", "tensor_copy", [B_pT1], [B_shist], out=s_hist, in_=pT1[:, 0:384].rearrange("p (c j) -> p c j", c=8))
            for cc in range(8):
                p, Bp = pDn()
                for kc in range(8):
                    mm(p[:, 0:NS], winb[:, kc, (5 + cc) * 128:(6 + cc) * 128], xb[:, kc, 0:NS], kc == 0, [B_win, B_xb], [Bp], stop=(kc == 7))
                vop("dve", "tensor_copy", [Bp], [B_sxfm], out=s_xfm[:, cc, :], in_=p[:, 0:NS])
            cw4 = colt[:, 32:64].rearrange("p (c t) -> p c t", c=8)
            hist4 = s_hist.rearrange("p c (b i) -> p c b i", i=3)
            vop("dve", "tensor_tensor", [B_sxfm, B_par], [B_scv], out=s_cv, in0=s_xfm, in1=cw4[:, :, 3:4].broadcast_to([128, 8, NS]), op=ALU.mult)
            for t in range(3):
                vop("dve", "tensor_tensor", [B_shist, B_par], [B_scv2], out=s_cv2, in0=hist4[:, :, :, t], in1=cw4[:, :, t:t + 1].broadcast_to([128, 8, NS]), op=ALU.mult)
                vop("dve", "tensor_tensor", [B_scv, B_scv2], [B_scv], out=s_cv, in0=s_cv, in1=s_cv2, op=ALU.add)
            vop("dve", "tensor_tensor", [B_scv, B_par], [B_scv], out=s_cv, in0=s_cv, in1=colt[:, 64:72].unsqueeze(2).broadcast_to([128, 8, NS]), op=ALU.add)
            act(s_cv, s_cv, AF.Silu, [B_scv], [B_scv])
            for cc in range(8):
                bank, Bb = (pT1, B_pT1) if cc < 4 else (pS[0], B_pS[0])
                mm(bank[0:NS, (cc % 4) * 128:(cc % 4 + 1) * 128], s_cv[:, cc, :], identf[:], cc % 4 == 0, [B_scv, B_c], [Bb], stop=(cc % 4 == 3))
            mm(pT1[0:NS, :], zerosb[:, 0:NS], zerosb[:], False, [B_c], [B_pT1])
            mm(pS[0][0:NS, :], zerosb[:, 0:NS], zerosb[:], False, [B_c], [B_pS[0]])
            vop("dve", "tensor_copy", [B_pT1], [B_spk], out=s_pk[:, 0:512], in_=pT1[0:NS, :])
            vop("dve", "tensor_copy", [B_pS[0]], [B_spk], out=s_pk[:, 512:1024], in_=pS[0][0:NS, :])
            dtr = s_pk[:, 1024:1032]; dtt = s_pk[:, 1032:1040]; dAe = s_pk[:, 1040:1048]
            vop("dve", "tensor_tensor", [B_stok, B_par], [B_spk], out=dtr, in0=s_tok[:, 2048:2056], in1=rowt[0:NS, 1024:1032], op=ALU.add)
            vop("dve", "scalar_tensor_tensor", [B_spk], [B_spk], out=dtt, in0=dtr, scalar=-1.0, in1=dtr, op0=ALU.mult, op1=ALU.min)
            act(dtt, dtt, AF.Exp, [B_spk], [B_spk])
            act(dtt, dtt, AF.Ln, [B_spk, B_c], [B_spk], bias=epsc[0:NS, 2:3])
            vop("dve", "scalar_tensor_tensor", [B_spk], [B_spk], out=dtt, in0=dtr, scalar=0.0, in1=dtt, op0=ALU.max, op1=ALU.add)
            vop("dve", "tensor_tensor", [B_spk, B_der], [B_spk], out=dAe, in0=dtt, in1=a_bc[0:NS, :], op=ALU.mult)
            act(dAe, dAe, AF.Exp, [B_spk], [B_spk])
            vop("dve", "tensor_tensor", [B_spk], [B_sact], out=v8(s_act[:, 0:512]), in0=v8(s_pk[:, 0:512]), in1=dtt.unsqueeze(2).broadcast_to([NS, 8, 64]), op=ALU.mult)
            ck(5)
            dma("sp", sbn[l], s_pk, [B_spk], [B_sbn])
            dma("sp", sbx[l], s_act[:, 0:512], [B_sact], [B_sbx])
            dma("sp", sbd[l], s_pk[:, 1040:1048], [B_spk], [B_sbd])
            for hh in range(4):
                dma("sp", sbB[l, :, :, hh, :], s_pk[:, 512:768].rearrange("p (g n) -> p g n", g=2), [B_spk], [B_sbB])
                dma("sp", sbC[l, :, :, hh, :], s_pk[:, 768:1024].rearrange("p (g n) -> p g n", g=2), [B_spk], [B_sbC])
            P.barrier()
            dma("sp", s_x8[:, 0:64], sbx[l].rearrange("b (h d) -> (b h) d", h=8), [B_sbx], [B_sx8])
            dma("sp", s_x8[:, 64:192], sbB[l].rearrange("b g h n -> (b g h) n"), [B_sbB], [B_sx8])
            dma("sp", s_x8[:, 192:320], sbC[l].rearrange("b g h n -> (b g h) n"), [B_sbC], [B_sx8])
            dma("sp", s_x8[:, 320:321], sbd[l].rearrange("b (h o) -> (b h) o", o=1), [B_sbd], [B_sx8])
            sth = st_h[l].rearrange("p (d n) -> p d n", n=128)
            hSl = hS[l].rearrange("p (d n) -> p d n", n=128)
            for half in range(2):
                dma("sp", s_h, sth[:, half * 32:(half + 1) * 32, :], [], [B_sh])
                act(s_h, s_h, AF.Copy, [B_sh, B_sx8], [B_sh], scale=s_x8[:, 320:321])
                for q2 in range(2):
                    d0 = half * 32 + q2 * 16
                    vop("dve", "tensor_tensor", [B_sx8], [B_stmp], out=s_tmp, in0=s_x8[:, d0:d0 + 16].unsqueeze(2).broadcast_to([128, 16, 128]),
                        in1=s_x8[:, 64:192].unsqueeze(1).broadcast_to([128, 16, 128]), op=ALU.mult)
                    vop("pool", "tensor_tensor", [B_sh, B_stmp], [B_sh], out=s_h[:, q2 * 16:(q2 + 1) * 16, :], in0=s_h[:, q2 * 16:(q2 + 1) * 16, :], in1=s_tmp, op=ALU.add)
                odma(hSl[:, half * 32:(half + 1) * 32, :], s_h, [B_sh])
                for q2 in range(2):
                    d0 = half * 32 + q2 * 16
                    vop("dve", "tensor_tensor", [B_sh, B_sx8], [B_stmp], out=s_tmp, in0=s_h[:, q2 * 16:(q2 + 1) * 16, :],
                        in1=s_x8[:, 192:320].unsqueeze(1).broadcast_to([128, 16, 128]), op=ALU.mult)
                    vop("dve", "tensor_reduce", [B_stmp], [B_sy8], out=s_y8[:, d0:d0 + 16], in_=s_tmp, axis=AX.X, op=ALU.add)
            dma("sp", sby[l], s_y8, [B_sy8], [B_sby])
            dma("sp", s_act[:, 512:1024], sby[l].rearrange("(b h) d -> b (h d)", h=8), [B_sby], [B_sact])
            vop("dve", "tensor_tensor", [B_spk, B_par], [B_smisc], out=v8(s_misc[:, 0:512]), in0=v8(s_pk[:, 0:512]), in1=rowt[0:NS, 1044:1052].unsqueeze(2).broadcast_to([NS, 8, 64]), op=ALU.mult)
            vop("dve", "tensor_tensor", [B_smisc, B_sact], [B_smisc], out=s_misc[:, 0:512], in0=s_misc[:, 0:512], in1=s_act[:, 512:1024], op=ALU.add)
            act(s_misc[:, 512:1024], s_tok[:, 2056:2568], AF.Silu, [B_stok], [B_smisc])
            vop("dve", "tensor_tensor", [B_smisc], [B_smisc], out=s_misc[:, 0:512], in0=s_misc[:, 0:512], in1=s_misc[:, 512:1024], op=ALU.mult)
            act(junk[0:NS, :], s_misc[:, 0:512], AF.Square, [B_smisc], [B_junk, B_sms], accum_out=sN(5, 6))
            act(sN(6, 7), sN(5, 6), AF.Ln, [B_sms, B_c], [B_sms], bias=epsc[0:NS, 1:2], scale=1.0 / 512)
            act(sN(6, 7), sN(6, 7), AF.Exp, [B_sms], [B_sms], scale=-0.5)
            vop("dve", "scalar_tensor_tensor", [B_smisc, B_sms, B_par], [B_scat], out=s_cat[:, 512:1024], in0=s_misc[:, 0:512], scalar=sN(6, 7), in1=rowt[0:NS, 512:1024], op0=ALU.mult, op1=ALU.mult)
            for idx, c0 in enumerate((0, 128, 512, 640, 768, 896)):
                tr(pTP[:, idx * NS:(idx + 1) * NS], s_cat[:, c0:c0 + 128], identb[0:NS, 0:NS], [B_scat, B_c], [B_pTP])
            act(s_catT[:, 0:2, :], pTP[:, 0:2 * NS].rearrange("p (a b) -> p a b", a=2), AF.Copy, [B_pTP], [B_scatT])
            act(s_catT[:, 4:8, :], pTP[:, 2 * NS:6 * NS].rearrange("p (a b) -> p a b", a=4), AF.Copy, [B_pTP], [B_scatT])
            for dc in range(8):
                p, Bp = pDn()
                for ec in range(8):
                    mm(p[:, 0:NS], woutb[:, ec, dc * 128:(dc + 1) * 128], s_catT[:, ec, :], ec == 0, [B_wout, B_scatT], [Bp], stop=(ec == 7))
                vop("dve", "scalar_tensor_tensor", [B_X[SI], Bp], [B_X[SI]], out=X[:, dc, scols], in0=X[:, dc, scols], scalar=ALPHA, in1=p[:, 0:NS], op0=ALU.mult, op1=ALU.add)
            layer_norm_fm(scols, NS, [B_X[SI]], 0, 8, xb[:, :, 0:NS], s_sq, [B_xb], [B_ssq], s_nm, s_lv, B_snm, B_slv)

            ck(6)
            P.barrier()
            banks = [(pD[0], B_pD[0]), (pD[1], B_pD[1]), (pT0, B_pT0), (pT1, B_pT1), (pS[0], B_pS[0]), (pS[1], B_pS[1]), (pY, B_pY)]
            pendB = []
            for ft in range(NFT):
                f0 = ft * FT
                blocks = [(f0 + b0, min(512, FT - b0), b0) for b0 in range(0, FT, 512)]
                xbufs = [B_X[j] for j in range(ft * (FT // TA), (ft + 1) * (FT // TA))]
                if ft == NFT - 1:
                    blocks.append((TP, NS, FT))
                    xbufs = xbufs + [B_X[NTA]]
                for (c0, n, o0) in blocks:
                    act(x1b[:, :, o0:o0 + n], X[:, :, c0:c0 + n], AF.Copy, xbufs, [B_x1b])
                bi = 0
                for g in range(16):
                    if g >= 2 and pendB:
                        pendB.pop(0)()
                    wb = w1b[g % 2]; Bw = B_w1b[g % 2]
                    dma("pool", wb, w1[l, g // 2, :, :, (g % 2) * 256:(g % 2) * 256 + 256], [], [Bw])
                    for f2 in range(2):
                        fc = g * 2 + f2
                        for (c0, n, o0) in blocks:
                            p, Bp = banks[bi % 5]; bi += 1
                            for kc in range(8):
                                mm(p[:, 0:n], wb[:, kc, f2 * 128:(f2 + 1) * 128], x1b[:, kc, o0:o0 + n], kc == 0, [Bw, B_x1b], [Bp], stop=(kc == 7))
                            r_ = rt[bi % 2]; Br = B_rt[bi % 2]
                            act(r_[:, 0:n], p[:, 0:n], AF.Relu, [Bp], [Br])
                            vop("dve", "tensor_tensor", [Br], [B_hT], out=hT[:, fc, o0:o0 + n], in0=r_[:, 0:n], in1=r_[:, 0:n], op=ALU.mult)
                while pendB:
                    pendB.pop(0)()
                for dc in range(8):
                    wb = w2b[dc % 2]; Bw = B_w2b[dc % 2]
                    dma("pool", wb, w2[l, dc], [], [Bw])
                    for (c0, n, o0) in blocks:
                        p, Bp = banks[bi % 5]; bi += 1
                        for fc in range(32):
                            mm(p[:, 0:n], wb[:, fc, :], hT[:, fc, o0:o0 + n], fc == 0, [Bw, B_hT], [Bp], stop=(fc == 31))
                        vop("dve", "scalar_tensor_tensor", xbufs + [Bp], xbufs, out=X[:, dc, c0:c0 + n], in0=X[:, dc, c0:c0 + n], scalar=ALPHA, in1=p[:, 0:n], op0=ALU.mult, op1=ALU.add)
                if ft == NFT - 1 and not last and can_prefetch:
                    nl = len(P.live_dma)
                    for kc in range(8):
                        dma("pool", winb[:, kc, :], win[l + 1, :, kc, :], [], [B_win, B_hT])
                    dma("pool", woutb, wout[l + 1], [], [B_wout, B_hT])
                    del P.live_dma[nl:]
                for (c0, n, o0) in blocks:
                    pcs = layer_norm_pieces(slice(c0, c0 + n), n, xbufs, 16, 24, lnx[:, :, 0:n], lnq[:, :, 0:n], [B_w2b[1]], [B_w2b[0]], nmeanB, lvarB, B_nmB, B_lvB,
                                            bk0=(pS[1], B_pS[1]), bk1=(pY, B_pY))
                    if ft < NFT - 1:
                        pendB.extend(pcs)
                    else:
                        for pc in pcs:
                            pc()
            P.barrier()
            ck(7)
            if not last and use_cc:
                dma("sp", cc_x_in.rearrange("p (k t) -> p k t", k=8), X[:, :, TP - 128:TP], [B_X[NTA - 1]], [B_ccx])
                P.op("pool", lambda e: e.collective_compute("AllGather", ALU.bypass, replica_groups=[[0, 1, 2, 3], [4, 5, 6, 7]],
                                                            ins=[cc_x_in], outs=[cc_x_out]), r=[B_ccx], w=[B_ccx_o], dma=True, inc=1)
                dma("sp", gx, cc_x_out.rearrange("(r p) f -> p r f", p=128), [B_ccx_o], [B_gx])
                xh2 = xh[:].rearrange("p k t -> p (k t)")
                vop("dve", "tensor_scalar", [B_gx, B_c], [B_xh], out=xh2, in0=gx[:, 0, :], scalar1=sel[:, 4:5], scalar2=None, op0=ALU.mult)
                for r_ in range(1, 4):
                    vop("dve", "scalar_tensor_tensor", [B_gx, B_c, B_xh], [B_xh], out=xh2, in0=gx[:, r_, :], scalar=sel[:, 4 + r_:5 + r_], in1=xh2, op0=ALU.mult, op1=ALU.add)

          except _Stop:
            break
        for i in range(NTA):
            odma(yT[:, :, i * TA:(i + 1) * TA], X[:, :, i * TA:(i + 1) * TA], [B_X[i]])
        odma(yT[:, :, TP:TX], X[:, :, TP:TX], [B_X[NTA]])
        P.op("sp", None, w=outbufs + [B_kS, B_vS])
        P.emit(st)
        P.stats.update(carve_stats)
    return nc, P


def _perm_cols():
    r = np.arange
    return np.concatenate([r(0, 256), r(512, 576), r(640, 704), r(576, 640), r(704, 768), r(768, 896), r(1536, 2560),
                           r(256, 512), r(896, 1024), r(2560, 2568), r(1024, 1536)])


def _perm_rows():
    r = np.arange
    return np.concatenate([r(0, 256), 256 + r(0, 64), 256 + r(128, 192), 256 + r(64, 128), 256 + r(192, 256), r(512, 1024)])


def prep_shared(inp, L):
    f = np.float32
    w_in = np.asarray(inp["w_in"], f)[:L]
    win = np.ascontiguousarray(w_in[:, :, _perm_cols()].reshape(L, 8, 128, DIN).transpose(0, 2, 1, 3))
    w_out = np.asarray(inp["w_out"], f)[:L]
    wout = np.ascontiguousarray(w_out[:, _perm_rows(), :].reshape(L, 8, 128, D).transpose(0, 2, 1, 3))
    w1 = np.ascontiguousarray(np.asarray(inp["w1"], f)[:L].reshape(L, 8, 128, 8, 512).transpose(0, 3, 2, 1, 4))
    w2 = np.ascontiguousarray(np.asarray(inp["w2"], f)[:L].reshape(L, 32, 128, 8, 128).transpose(0, 3, 2, 1, 4))
    rowp = np.zeros((L, 1, NROW), f)
    rowp[:, 0, 0:256] = inp["ln_v_g"][:L]
    rowp[:, 0, 256:512] = inp["ln_v_b"][:L]
    rowp[:, 0, 512:1024] = inp["gn_w"][:L]
    rowp[:, 0, 1024:1032] = inp["dt_bias"][:L]
    rowp[:, 0, 1032:1040] = inp["a_log"][:L]
    rowp[:, 0, 1040:1044] = np.asarray(inp["sinks"])[:L][:, [0, 2, 1, 3]]
    rowp[:, 0, 1044:1052] = inp["d_skip"][:L]
    rowp[:, 0, 1052:1056] = np.asarray(inp["w_s"])[:L, :, 0, 0]
    rowp[:, 0, 1056:1060] = np.asarray(inp["b_s"])[:L, :, 0]
    colp = np.zeros((L, 128, NCOLP), f)
    for k, name in enumerate(("ln1_g", "ln1_b", "ln2_g", "ln2_b")):
        colp[:, :, 8 * k:8 * k + 8] = np.asarray(inp[name], f)[:L].reshape(L, 8, 128).transpose(0, 2, 1)
    cw = np.asarray(inp["conv_w"], f)[:L].reshape(L, 4, 8, 128)
    colp[:, :, 32:64] = cw.transpose(0, 3, 2, 1).reshape(L, 128, 32)
    colp[:, :, 64:72] = np.asarray(inp["conv_b"], f)[:L].reshape(L, 8, 128).transpose(0, 2, 1)
    dsk = np.repeat(np.asarray(inp["d_skip"], f)[:L], 64, axis=1)
    colp[:, :, 72:76] = dsk.reshape(L, 4, 128).transpose(0, 2, 1)
    wsT = np.ascontiguousarray(np.asarray(inp["w_s"], f)[:L].transpose(0, 3, 1, 2))
    bsd = np.ascontiguousarray(np.asarray(inp["b_s"], f)[:L])
    s_ = np.arange(128)[:, None]
    t_ = np.arange(128)[None, :]
    cst_f = np.stack([np.eye(128, dtype=f), (s_ <= t_).astype(f), np.ones((128, 128), f)], axis=1)
    prev = np.where(s_ > t_, 0.0, NEG).astype(f)
    cur = np.where(s_ <= t_, 0.0, NEG).astype(f)
    negA = np.concatenate([prev, cur, prev, cur], axis=1)
    allneg = np.full((128, 128), NEG, f)
    negA_first = np.concatenate([allneg, cur, allneg, cur], axis=1)
    negS = np.concatenate([cur] * 4, axis=1)
    return dict(win=win, wout=wout, w1=w1, w2=w2, rowp=rowp, colp=colp, wsT=wsT, bsd=bsd, cst_f=cst_f), (negA, negA_first, negS)


def prep_core(inp, shared, masks, c, L, NT):
    f = np.float32
    TP = NT * 128
    b, p = c // 4, c % 4
    t0 = p * TP
    xp = np.asarray(inp["x_prompt"], f)
    xs = np.asarray(inp["x_sample"], f)
    tok = np.concatenate([xp[b, t0:t0 + TP], xs[c * NS:(c + 1) * NS, 0]], axis=0)
    xT0 = np.ascontiguousarray(tok.T.reshape(8, 128, TP + NS).transpose(1, 0, 2))
    if p == 0:
        xh0 = np.zeros((128, 8, 128), f)
    else:
        xh0 = np.ascontiguousarray(xp[b, t0 - 128:t0].T.reshape(8, 128, 128).transpose(1, 0, 2))
    negA, negA_first, negS = masks
    cst_m = np.stack([negA, negA_first if p == 0 else negA, negS], axis=1)
    selp = np.zeros((128, 8), f)
    for r in range(3):
        selp[:, r] = 1.0 if r < p else 0.0
    for r in range(4):
        selp[:, 4 + r] = 1.0 if r == p - 1 else 0.0
    sl = slice(c * NS, (c + 1) * NS)
    d = dict(shared)
    d.update(xT0=xT0, xh0=xh0, cst_m=np.ascontiguousarray(cst_m), selp=selp,
             st_k=np.ascontiguousarray(np.asarray(inp["state_attn_k"], f)[:L, sl].reshape(L, NS, 128, 128)),
             st_v=np.ascontiguousarray(np.asarray(inp["state_attn_v"], f)[:L, sl].reshape(L, NS, 128, 128)),
             st_c=np.ascontiguousarray(np.asarray(inp["state_conv"], f)[:L, sl].reshape(L, NS * 3, 1024)),
             st_h=np.ascontiguousarray(np.asarray(inp["state_ssm"], f)[:L, sl].reshape(L, NS * 8, 64 * 128)))
    return d


def assemble(res, L, NT):
    f = np.float32
    TP = NT * 128
    S = 4 * TP
    yp = np.zeros((2, S, D), f); ys = np.zeros((128, 1, D), f)
    kp = np.zeros((L, 2, 128, 2, 64), f); vp = np.zeros_like(kp)
    cp = np.zeros((L, 2, 3, 1024), f); hp = np.zeros((L, 2, 8, 64, 128), f)
    ks = np.zeros((L, 128, 1, 2, 64), f); vs = np.zeros_like(ks)
    cs = np.zeros((L, 128, 3, 1024), f); hs = np.zeros((L, 128, 8, 64, 128), f); vns = np.zeros((L, 128, 1, 256), f)
    for c in range(NCORES):
        r = res[c]
        b, p = c // 4, c % 4
        tok = np.asarray(r["yT"]).transpose(1, 0, 2).reshape(D, TP + NS).T
        yp[b, p * TP:(p + 1) * TP] = tok[:TP]
        sl = slice(c * NS, (c + 1) * NS)
        ys[sl, 0] = tok[TP:]
        if p == 3:
            kp[:, b] = np.asarray(r["kP"]).reshape(L, 128, 2, 64)
            vp[:, b] = np.asarray(r["vP"]).reshape(L, 128, 2, 64)
            cp[:, b] = np.asarray(r["cP"])
            hp[:, b] = np.asarray(r["hP"]).reshape(L, 128, 8, 64).transpose(0, 2, 3, 1)
        ks[:, sl, 0] = np.asarray(r["kS"]).reshape(L, NS, 2, 64)
        vs[:, sl, 0] = np.asarray(r["vS"]).reshape(L, NS, 2, 64)
        cs[:, sl] = np.asarray(r["cS"])
        hs[:, sl] = np.asarray(r["hS"]).reshape(L, NS, 8, 64, 128)
        vns[:, sl, 0] = np.asarray(r["vnS"])
    return (yp, ys, kp, vp, cp, hp, ks, vs, cs, hs, vns)


_CACHE = {}


def run(inp, L, NT, use_cc=True, trace=False):
    key = (L, NT, use_cc)
    if key not in _CACHE:
        _CACHE[key] = build(L, NT, use_cc)
    nc, P = _CACHE[key]
    shared, masks = prep_shared(inp, L)
    in_maps = [prep_core(inp, shared, masks, c, L, NT) for c in range(NCORES)]
    res = run_bass_kernel_spmd(nc, in_maps, core_ids=list(range(NCORES)), trace=trace)
    return assemble(res.results, L, NT), res


def kernel(**inputs):
    out, _ = run(inputs, DEPTH, SEQ // (4 * 128))
    return out
```

```python
import math
import numpy as np
from contextlib import ExitStack
import concourse.bass as bass
import concourse.mybir as mybir
from concourse.bass_utils import run_bass_kernel_spmd

F32 = mybir.dt.float32
BF16 = mybir.dt.bfloat16
ALU = mybir.AluOpType
AF = mybir.ActivationFunctionType
AX = mybir.AxisListType

D = 1024
DEPTH = 4
SEQ = 8192
NCORES = 8
NS = 16
DIN = 2568
DFF = 4096
ALPHA = (2 * DEPTH) ** 0.25
LN_EPS = 1e-5
RMS_EPS = 1e-6
NEG = -30000.0
NROW = 1068
NCOLP = 80

ENGS = ("pe", "act", "dve", "pool", "sp")


class Buf:
    __slots__ = ("name", "last_w", "rd_eng", "rd_dma")

    def __init__(self, name):
        self.name = name
        self.last_w = None
        self.rd_eng = {}
        self.rd_dma = []


class Op:
    __slots__ = ("eng", "fn", "deps", "is_dma", "sig", "tok", "idx", "vc", "inc")

    def __init__(self, eng, fn, is_dma, inc):
        self.eng = eng
        self.fn = fn
        self.is_dma = is_dma
        self.deps = set()
        self.sig = is_dma
        self.tok = None
        self.vc = None
        self.inc = inc


class Prog:
    def __init__(self, nc, n_dma_slots=16):
        self.nc = nc
        self.ops = []
        self.n_dma_slots = n_dma_slots
        self.live_dma = []

    def op(self, eng, fn, r=(), w=(), dma=False, inc=None):
        o = Op(eng, fn, dma, inc if inc is not None else (16 if dma else 1))
        o.idx = len(self.ops)
        deps = o.deps
        for b in r:
            lw = b.last_w
            if lw is not None:
                if lw.is_dma or dma or lw.eng != eng or eng != "pe":
                    deps.add(lw)
        for b in w:
            lw = b.last_w
            if lw is not None and (lw.is_dma or dma or lw.eng != eng):
                deps.add(lw)
            for e, ro in b.rd_eng.items():
                if dma or e != eng:
                    deps.add(ro)
            for ro in b.rd_dma:
                deps.add(ro)
        for b in w:
            b.last_w = o
            b.rd_eng = {}
            b.rd_dma = []
        for b in r:
            if dma:
                b.rd_dma.append(o)
            else:
                b.rd_eng[eng] = o
        deps.discard(o)
        self.ops.append(o)
        if dma:
            self.live_dma.append(o)
        return o

    def barrier(self):
        self.barrier_fn(self)

    def emit(self, stack):
        nc = self.nc
        ops = self.ops
        for o in ops:
            for d in o.deps:
                d.sig = True
        esem = {e: stack.enter_context(nc.semaphore("s_" + e)) for e in ENGS}
        dsem = {}
        for q in ("sp", "act", "pool"):
            dsem[q] = [stack.enter_context(nc.semaphore("d_%s%d" % (q, i))) for i in range(self.n_dma_slots)]
        ecnt = {e: 0 for e in ENGS}
        dcnt = {q: 0 for q in dsem}
        duse = {q: [0] * self.n_dma_slots for q in dsem}
        dlast = {q: [None] * self.n_dma_slots for q in dsem}
        ccsem = stack.enter_context(nc.semaphore("s_cc"))
        cccnt = 0
        for o in ops:
            if o.is_dma and o.inc != 16:
                cccnt += o.inc
                o.tok = (ccsem, cccnt)
            elif o.is_dma:
                q = o.eng
                slot = dcnt[q] % self.n_dma_slots
                dcnt[q] += 1
                duse[q][slot] += o.inc
                if dlast[q][slot] is not None:
                    o.deps.add(dlast[q][slot])
                dlast[q][slot] = o
                o.tok = (dsem[q][slot], duse[q][slot])
            elif o.sig:
                ecnt[o.eng] += 1
                o.tok = (esem[o.eng], ecnt[o.eng])
        know = {e: {} for e in ENGS}
        streams = {e: [] for e in ENGS}
        nwaits = 0
        for o in ops:
            k = know[o.eng]
            st = streams[o.eng]
            for d in sorted(o.deps, key=lambda x: x.idx):
                sem, val = d.tok
                if k.get(sem, 0) < val:
                    st.append((0, sem, val))
                    nwaits += 1
                    for s2, v2 in d.vc.items():
                        if k.get(s2, 0) < v2:
                            k[s2] = v2
            st.append((1, o))
            if o.sig:
                vc = dict(k)
                vc[o.tok[0]] = o.tok[1]
                o.vc = vc
        self.stats = dict(n_ops=len(ops), n_waits=nwaits, per_eng={e: len(s) for e, s in streams.items()})

        def run(engine, st):
            for it in st:
                if it[0] == 0:
                    engine.wait_ge(it[1], it[2])
                else:
                    o = it[1]
                    if o.fn is None:
                        continue
                    ins = o.fn(engine)
                    if o.sig:
                        ins.then_inc(o.tok[0], o.inc)

        with nc.Block() as block:
            @block.tensor
            def _(e):
                run(e, streams["pe"])

            @block.scalar
            def _(e):
                run(e, streams["act"])

            @block.vector
            def _(e):
                run(e, streams["dve"])

            @block.gpsimd
            def _(e):
                run(e, streams["pool"])

            @block.sync
            def _(e):
                run(e, streams["sp"])


class _Stop(Exception):
    pass


def build(L, NT, use_cc=True, stage=None, debug=False):
    TP = NT * 128
    TX = TP + NS
    TA = 256
    NTA = TP // TA
    FT = min(1024, TP)
    NFT = TP // FT
    FTX = FT + NS
    nc = bass.Bass("TRN2", target_bir_lowering=False)
    P = Prog(nc)

    def din(name, shape, dt=F32):
        return nc.dram_tensor(name, list(shape), dt, kind="ExternalInput").ap()

    def dout(name, shape, dt=F32):
        return nc.dram_tensor(name, list(shape), dt, kind="ExternalOutput").ap()

    def dint(name, shape, dt=F32):
        return nc.dram_tensor(name, list(shape), dt, kind="Internal").ap()

    xT0 = din("xT0", [128, 8, TX])
    xh0 = din("xh0", [128, 8, 128])
    win = din("win", [L, 128, 8, DIN])
    wout = din("wout", [L, 128, 8, D])
    w1 = din("w1", [L, 8, 128, 8, 512])
    w2 = din("w2", [L, 8, 128, 32, 128])
    rowp = din("rowp", [L, 1, NROW])
    colp = din("colp", [L, 128, NCOLP])
    wsT = din("wsT", [L, 128, 4, 128])
    bsd = din("bsd", [L, 4, 128])
    cst_f = din("cst_f", [128, 3, 128])
    cst_m = din("cst_m", [128, 3, 512])
    selp = din("selp", [128, 8])
    st_k = din("st_k", [L, NS, 128, 128])
    st_v = din("st_v", [L, NS, 128, 128])
    st_c = din("st_c", [L, NS * 3, 1024])
    st_h = din("st_h", [L, NS * 8, 64 * 128])

    yT = dout("yT", [128, 8, TX])
    kP = dout("kP", [L, 128, 128])
    vP = dout("vP", [L, 128, 128])
    cP = dout("cP", [L, 3, 1024])
    hP = dout("hP", [L, 128, 512])
    kS = dout("kS", [L, NS, 128])
    vS = dout("vS", [L, NS, 128])
    cS = dout("cS", [L, NS, 3, 1024])
    hS = dout("hS", [L, NS * 8, 64 * 128])
    vnS = dout("vnS", [L, NS, 256])

    cc_st_in = dint("cc_st_in", [128, 520])
    cc_st_out = dint("cc_st_out", [4 * 128, 520])
    cc_x_in = dint("cc_x_in", [128, 1024])
    cc_x_out = dint("cc_x_out", [4 * 128, 1024])
    sbn = dint("sbn", [L, NS, 1048])
    sbx = dint("sbx", [L, NS, 512])
    sbB = dint("sbB", [L, NS, 2, 4, 128])
    sbC = dint("sbC", [L, NS, 2, 4, 128])
    sby = dint("sby", [L, NS * 8, 64])
    sbd = dint("sbd", [L, NS, 8])
    xsc = dint("xsc", [128, 6, NT * 128], BF16)
    B_ccst = Buf("ccst"); B_ccst_o = Buf("ccsto"); B_ccx = Buf("ccx"); B_ccx_o = Buf("ccxo")
    B_sbn = Buf("sbn"); B_sbx = Buf("sbx"); B_sbB = Buf("sbB"); B_sbC = Buf("sbC"); B_sby = Buf("sby"); B_sbd = Buf("sbd")
    B_kS = Buf("kS"); B_vS = Buf("vS")
    B_xsc = [Buf("xsc%d" % i) for i in range(NT // 2)]
    outbufs = []

    with ExitStack() as st:
        def sb(name, shape, dt=F32):
            return st.enter_context(nc.sbuf_tensor(name, list(shape), dt))

        def ps(name, shape, dt=F32):
            return st.enter_context(nc.psum_tensor(name, list(shape), dt))

        X = sb("X", [128, 8, TX]); B_X = [Buf("X%d" % i) for i in range(NTA + 1)]
        identf = sb("identf", [128, 128]); Uf = sb("Uf", [128, 128]); onesf = sb("onesf", [128, 128])
        identb = sb("identb", [128, 128], BF16); onesb = sb("onesb", [128, 128], BF16)
        negA = sb("negA", [128, 512], BF16); negA1 = sb("negA1", [128, 512], BF16); negS = sb("negS", [128, 512], BF16)
        sel = sb("sel", [128, 8]); epsc = sb("epsc", [128, 4]); zerosb = sb("zerosb", [128, 512], BF16)
        B_c = Buf("consts")
        rowt = sb("rowt", [128, NROW]); colt = sb("colt", [128, NCOLP]); B_par = Buf("par")
        wsTb = sb("wsTb", [128, 4, 128], BF16); bsb = sb("bsb", [128, 2, 128])
        Dg = sb("Dg", [128, 4, 128], BF16)
        a_bc = sb("a_bc", [128, 8]); esink = sb("esink", [128, 4]); B_der = Buf("der")
        hst = sb("hst", [128, 512]); hb = sb("hb", [128, 512], BF16); B_h = Buf("h"); B_hb = Buf("hb")
        halo3 = sb("halo3", [128, 8, 3]); B_halo3 = Buf("halo3")
        kT = sb("kT", [128, 128 + TA], BF16); B_kT = Buf("kT")
        vaug = sb("vaug", [128, 3, 2, 65], BF16); B_va = [Buf("va%d" % i) for i in range(3)]
        xh = sb("xh", [128, 8, 128]); B_xh = Buf("xh")
        ostage = sb("ostage", [128, 256]); B_os = Buf("ostage")
        sm = sb("sm", [128, 16, 16]); B_sm = Buf("sm")

        RA_BYTES = 116 * 1024
        RA = sb("RA", [128, RA_BYTES // 4])
        ra_off = [0]

        def carve(shape, dt=F32, reset=None):
            if reset is not None:
                ra_off[0] = reset
            n = 1
            for s_ in shape[1:]:
                n *= s_
            esz = 4 if dt == F32 else 2
            nb = (n * esz + 31) // 32 * 32
            lo = ra_off[0] // 4
            ap = RA[0:shape[0], lo:lo + nb // 4]
            ra_off[0] += nb
            assert ra_off[0] <= RA_BYTES, ("RA overflow", ra_off[0])
            if dt != F32:
                ap = ap.bitcast(dt)
            ap = ap[:, 0:n]
            if len(shape) == 3:
                ap = ap.rearrange("p (a b) -> p a b", a=shape[1])
            elif len(shape) == 4:
                ap = ap.rearrange("p (a b c) -> p a b c", a=shape[1], b=shape[2])
            return ap

        winb = carve([128, 8, DIN], BF16, reset=0); B_win = Buf("win")
        woutb = carve([128, 8, D], BF16); B_wout = Buf("wout")
        xb = carve([128, 8, TA], BF16); B_xb = Buf("xb")
        xhb = xb[:, :, 0:128]; B_xhb = B_xb
        junk = carve([128, 512], BF16); B_junk = Buf("junk")
        workA = ra_off[0]
        G16 = ra_off[0]; B_g16 = Buf("g16")
        gx = carve([128, 4, 1024]); B_gx = B_g16
        gst = carve([128, 4, 520], reset=G16); B_gst = B_g16
        wsTf = carve([128, 4, 128]);
        cstage = carve([128, 3, 512], reset=G16)
        xbB = carve([128, 8, TA], BF16, reset=G16); gauB = carve([128, 2, TA]); qTB = carve([128, 2, TA], BF16); xactB = carve([128, 8, TA], BF16)
        assert ra_off[0] <= G16 + 16384
        Dcv = carve([128, 6, 4, 128], BF16, reset=G16)
        rawb = [carve([128, TA + 4], BF16) for _ in range(2)]; B_rawb = [Buf("rawb0"), Buf("rawb1")]
        assert ra_off[0] <= G16 + 8320
        ra_off[0] = G16 + 16384
        gau = carve([128, 2, TA]); B_gau = Buf("gau")
        qT = carve([128, 2, TA], BF16); B_qT = Buf("qT")
        raw = [carve([128, TA + 3]) for _ in range(2)]; B_raw = [Buf("raw0"), Buf("raw1")]
        acc = carve([128, TA]); B_acc = Buf("acc")
        xact = carve([128, 8, TA], BF16); B_xact = Buf("xact")
        sqb = xact; B_sqb = B_xact
        gav = carve([128, 256]); B_gav = Buf("gav")
        vn = carve([128, 256], BF16); B_vn = Buf("vn")
        vnf = carve([128, 256]); B_vnf = Buf("vnf")
        sz = carve([128, 2, 512]); B_sz = [Buf("sz0"), Buf("sz1")]
        Eb = [carve([128, 512], BF16) for _ in range(2)]; B_E = [Buf("E0"), Buf("E1")]
        ob = carve([128, 256], BF16); B_ob = Buf("ob")
        Rt = carve([128, 1024]); B_R = Buf("R")
        LT = carve([128, 8, 128], BF16); B_LT = Buf("LT")
        Gt = carve([128, 8, 128], BF16); B_G = Buf("G")
        xstok = carve([128, 512], BF16); B_xstok = Buf("xstok")
        Btok = carve([128, 256], BF16); B_Btok = Buf("Btok")
        xw = carve([128, 512], BF16); B_xw = Buf("xw")
        yt1 = carve([128, 512]); B_yt1 = Buf("yt1")
        yt2 = carve([128, 512]); B_yt2 = Buf("yt2")
        nmean = yt1; B_nm = B_yt1
        lvar = yt2; B_lv = B_yt2
        ocb = carve([128, 512], BF16); B_oc = Buf("oc")
        catT = carve([128, 8, TA], BF16); B_cat = Buf("cat")
        xb2 = [xb, xbB]; B_xb2 = [B_xb, Buf("xbB")]
        gau2 = [gau, gauB]; B_gau2 = [B_gau, Buf("gauB")]
        qT2 = [qT, qTB]; B_qT2 = [B_qT, Buf("qTB")]
        xact2 = [xact, xactB]; B_xact2 = [B_xact, Buf("xactB")]
        cpst = sz.rearrange("p a b -> p (a b)")
        endA = ra_off[0]
        s_tok = carve([NS, DIN], reset=workA); B_stok = Buf("stok")
        s_pk = carve([NS, 1048]); B_spk = Buf("spk")
        s_misc = carve([NS, 1024]); B_smisc = Buf("smisc")
        s_act = carve([NS, 1024]); B_sact = Buf("sact")
        s_cat = carve([NS, 1024], BF16); B_scat = Buf("scat")
        s_catT = carve([128, 8, NS], BF16); B_scatT = Buf("scatT")
        s_q = carve([128, 2, NS], BF16); B_sq = Buf("s_q")
        s_E = carve([128, 64], BF16); B_sE = Buf("sE")
        s_rd = carve([128, 64]); B_srd = Buf("srd")
        s_sq = carve([128, 8, NS], BF16); B_ssq = Buf("ssq")
        s_nm = carve([128, 16]); B_snm = Buf("snm")
        s_lv = carve([128, 16]); B_slv = Buf("slv")
        sampB = ra_off[0]
        s_kb = carve([128, NS, 128], BF16); B_skb = Buf("skb")
        s_vb = carve([128, NS, 128], BF16); B_svb = Buf("svb")
        s_kT = carve([128, NS, 128], BF16); B_skT = Buf("skT")
        s_hc = carve([NS * 3, 1024]); B_shc = Buf("shc")
        s_hist = carve([128, 8, NS * 3]); B_shist = Buf("shist")
        s_xfm = carve([128, 8, NS]); B_sxfm = Buf("sxfm")
        s_cv = carve([128, 8, NS]); B_scv = Buf("scv")
        s_cv2 = carve([128, 8, NS]); B_scv2 = Buf("scv2")
        endS1 = ra_off[0]
        s_h = carve([128, 32, 128], reset=sampB); B_sh = Buf("sh")
        s_tmp = carve([128, 16, 128]); B_stmp = Buf("stmp")
        s_x8 = carve([128, 328]); B_sx8 = Buf("sx8")
        s_y8 = carve([128, 64]); B_sy8 = Buf("sy8")
        endS2 = ra_off[0]
        hT = carve([128, 32, FTX], BF16, reset=0); B_hT = Buf("hT")
        w1_off = ra_off[0]
        w1b = [carve([128, 8, 256], BF16) for _ in range(2)]; B_w1b = [Buf("w1b0"), Buf("w1b1")]
        assert ra_off[0] - w1_off == 8192
        sqbB = RA[:, w1_off // 4:w1_off // 4 + 2048].bitcast(BF16).rearrange("p (a b) -> p a b", a=8)
        w2_off = ra_off[0]
        w2b = [carve([128, 32, 128], BF16) for _ in range(2)]; B_w2b = [Buf("w2b0"), Buf("w2b1")]
        assert ra_off[0] - w2_off == 16384
        lnq = RA[:, w2_off // 4:w2_off // 4 + 2048].bitcast(BF16).rearrange("p (a b) -> p a b", a=8)
        lnx = RA[:, w2_off // 4 + 2048:w2_off // 4 + 4096].bitcast(BF16).rearrange("p (a b) -> p a b", a=8)
        x1b = carve([128, 8, FTX], BF16); B_x1b = Buf("x1b")
        rt = [carve([128, 512], BF16) for _ in range(2)]; B_rt = [Buf("rt0"), Buf("rt1")]
        nmeanB = carve([128, 512]); lvarB = carve([128, 512]); B_nmB = Buf('nmB'); B_lvB = Buf('lvB')
        endB = ra_off[0]
        carve_stats = dict(endA=endA, endS1=endS1, endS2=endS2, endB=endB)

        pD = [ps("pD%d" % i, [128, 512]) for i in range(2)]; B_pD = [Buf("pD0"), Buf("pD1")]
        pT0 = ps("pT0", [128, 512]); B_pT0 = Buf("pT0")
        pT1 = ps("pT1", [128, 512]); B_pT1 = Buf("pT1")
        pS = [ps("pS%d" % i, [128, 512]) for i in range(2)]; B_pS = [Buf("pS0"), Buf("pS1")]
        pY = ps("pY", [128, 512]); B_pY = Buf("pY")
        pMT = ps("pMT", [128, 512])
        pM = pMT[:, 0:256]; B_pM = Buf("pM")
        pTP = pMT[:, 256:512].bitcast(BF16); B_pTP = Buf("pTP")

        def mm(out, lhsT, rhs, start, r, w, stop=True):
            P.op("pe", lambda e: e.matmul(out, lhsT=lhsT, rhs=rhs, start=start, stop=stop, skip_group_check=True), r=r, w=w)

        def tr(out, in_, ident, r, w):
            P.op("pe", lambda e: e.transpose(out, in_, ident), r=r, w=w)

        def act(out, in_, func, r, w, **kw):
            P.op("act", lambda e: e.activation(out=out, in_=in_, func=func, **kw), r=r, w=w)

        def vop(eng, name, r, w, **kw):
            P.op(eng, lambda e: getattr(e, name)(**kw), r=r, w=w)

        def dma(q, out, in_, r, w, **kw):
            P.op(q, lambda e: e.dma_start(out=out, in_=in_, **kw), r=r, w=w, dma=True)

        def odma(out, in_, r):
            b = Buf("o%d" % len(outbufs))
            outbufs.append(b)
            dma("sp", out, in_, r, [b])
            return b

        def dbg(name, ap, r, dt=F32):
            if not debug:
                return
            t_ = nc.dram_tensor("dbg_" + name, list(ap.shape), dt, kind="ExternalOutput").ap()
            odma(t_, ap, r)

        def smt(slot, n=16):
            return sm[:, slot, 0:n]

        bar_sb = sb("bar_sb", [128, 8]); bar_d = dint("bar_d", [128, 2])

        def _barrier(P_):
            b0, b1, b2, b3, b4 = Buf("b0"), Buf("b1"), Buf("b2"), Buf("b3"), Buf("b4")
            mm(pM[:, 0:2], zerosb[:, 0:128], zerosb[:, 0:2], True, [B_c], [B_pM, b0])
            act(bar_sb[:, 0:2], pM[:, 0:2], AF.Copy, [B_pM, b0], [b1])
            vop("dve", "tensor_copy", [b1], [b2], out=bar_sb[:, 2:4], in_=bar_sb[:, 0:2])
            vop("pool", "tensor_copy", [b2], [b3], out=bar_sb[:, 4:6], in_=bar_sb[:, 2:4])
            o = P_.op("sp", lambda e: e.dma_start(out=bar_d, in_=bar_sb[:, 4:6]), r=[b3], w=[b4], dma=True)
            o.deps.update(x for x in P_.live_dma if x is not o)
            P_.live_dma = []
            for e in ENGS:
                P_.op(e, None, r=[b4])
        P.barrier_fn = _barrier

        dq = [0]

        def pDn():
            dq[0] ^= 1
            return pD[dq[0]], B_pD[dq[0]]

        def v8(ap):
            return ap.rearrange("p (h d) -> p h d", h=8)

        dma("sp", cstage[:, :, 0:128], cst_f, [], [B_c, B_g16])
        for dst, j in ((identf, 0), (Uf, 1), (onesf, 2), (identb, 0), (onesb, 2)):
            vop("dve", "tensor_copy", [B_c], [B_c], out=dst[:], in_=cstage[:, j, 0:128])
        dma("sp", cstage, cst_m, [B_c], [B_c, B_g16])
        for dst, j in ((negA, 0), (negA1, 1), (negS, 2)):
            vop("dve", "tensor_copy", [B_c], [B_c], out=dst[:], in_=cstage[:, j, :])
        dma("sp", sel[:], selp, [], [B_c])
        vop("dve", "memset", [], [B_c], ap=epsc[:, 0:1], constant=LN_EPS)
        vop("dve", "memset", [], [B_c], ap=zerosb[:], constant=0.0)
        vop("dve", "memset", [], [B_c], ap=epsc[:, 1:2], constant=RMS_EPS)
        vop("dve", "memset", [], [B_c], ap=epsc[:, 2:3], constant=1.0)
        for i in range(NTA):
            dma("sp", X[:, :, i * TA:(i + 1) * TA], xT0[:, :, i * TA:(i + 1) * TA], [], [B_X[i]])
        dma("sp", X[:, :, TP:TX], xT0[:, :, TP:TX], [], [B_X[NTA]])
        dma("sp", xh[:], xh0, [], [B_xh])
        P.barrier()

        def layer_norm_pieces(cols, n, Bxs, g_off, b_off, xnb_ap, sqb_ap, Bxnb, Bsq, nm_ap, lv_ap, B_nm, B_lv, bk0=None, bk1=None):
            Xs = X[:, :, cols]
            q0, Bq0 = bk0 if bk0 is not None else (pT0, B_pT0)
            q1, Bq1 = bk1 if bk1 is not None else (pT1, B_pT1)
            nm = nm_ap[:, 0:n]; lv = lv_ap[:, 0:n]

            def s1():
                act(xnb_ap, Xs, AF.Copy, Bxs, Bxnb)
                act(sqb_ap, Xs, AF.Square, Bxs, Bsq)

            def s2():
                for kc in range(8):
                    mm(q0[:, 0:n], onesb[:], xnb_ap[:, kc, :], kc == 0, Bxnb + [B_c], [Bq0], stop=(kc == 7))
                for kc in range(8):
                    mm(q1[:, 0:n], onesb[:], sqb_ap[:, kc, :], kc == 0, Bsq + [B_c], [Bq1], stop=(kc == 7))
                vop("dve", "tensor_scalar", [Bq0], [B_nm], out=nm, in0=q0[:, 0:n], scalar1=-1.0 / D, scalar2=None, op0=ALU.mult)
                vop("dve", "tensor_tensor", [B_nm], [B_lv], out=lv, in0=nm, in1=nm, op=ALU.mult)
                vop("dve", "scalar_tensor_tensor", [Bq1, B_lv], [B_lv], out=lv, in0=q1[:, 0:n], scalar=1.0 / D, in1=lv, op0=ALU.mult, op1=ALU.subtract)
                act(lv, lv, AF.Ln, [B_lv, B_c], [B_lv], bias=epsc[:, 0:1])
                act(lv, lv, AF.Exp, [B_lv], [B_lv], scale=-0.5)

            def s3():
                vop("dve", "tensor_tensor", Bxs + [B_nm], Bxs, out=Xs, in0=Xs, in1=nm.unsqueeze(1).broadcast_to([128, 8, n]), op=ALU.add)
                vop("dve", "tensor_tensor", Bxs + [B_lv], Bxs, out=Xs, in0=Xs, in1=lv.unsqueeze(1).broadcast_to([128, 8, n]), op=ALU.mult)

            def s4():
                for kc in range(8):
                    vop("dve", "tensor_scalar", Bxs + [B_par], Bxs, out=X[:, kc, cols], in0=X[:, kc, cols],
                        scalar1=colt[:, g_off + kc:g_off + kc + 1], scalar2=colt[:, b_off + kc:b_off + kc + 1], op0=ALU.mult, op1=ALU.add)
            return [s1, s2, s3, s4]

        def layer_norm_fm(*a, **k):
            for pc in layer_norm_pieces(*a, **k):
                pc()

        def ck(k):
            if stage is not None and stage == k:
                raise _Stop()

        for l in range(L):
          try:
            last = (l == L - 1)
            can_prefetch = (32 * FTX * 2 >= 8 * DIN * 2 + 8 * D * 2)
            if l == 0 or not can_prefetch:
                for kc in range(8):
                    dma("pool", winb[:, kc, :], win[l, :, kc, :], [], [B_win])
                dma("pool", woutb, wout[l], [], [B_wout])
            dma("sp", rowt[:], rowp[l].partition_broadcast(128), [], [B_par])
            dma("sp", colt[:], colp[l], [], [B_par])
            dma("sp", wsTf, wsT[l], [], [B_par, B_g16])
            for h in range(4):
                dma("sp", bsb[(h % 2) * 64:(h % 2) * 64 + 64, h // 2, :], bsd[l, h:h + 1, :].partition_broadcast(64), [], [B_par])
            vop("dve", "tensor_tensor", [B_par, B_c, B_g16], [B_der], out=wsTb[:], in0=wsTf, in1=Uf[:].unsqueeze(1).broadcast_to([128, 4, 128]), op=ALU.mult)
            act(a_bc[:], rowt[:, 1032:1040], AF.Exp, [B_par], [B_der])
            vop("dve", "tensor_scalar", [B_der], [B_der], out=a_bc[:], in0=a_bc[:], scalar1=-1.0, scalar2=None, op0=ALU.mult)
            act(esink[:], rowt[:, 1040:1044], AF.Exp, [B_par], [B_der])
            for j in range(4):
                vop("dve", "tensor_scalar", [B_par, B_c], [B_der], out=Dg[:, j, :], in0=identf[:], scalar1=colt[:, 72 + j:73 + j], scalar2=None, op0=ALU.mult)
            for s_ in range(3):
                vop("pool", "memset", [], [B_va[s_]], ap=vaug[:, s_, :, 64:65], constant=1.0)
            if use_cc:
                for cc in range(6):
                    for t in range(4):
                        vop("dve", "tensor_scalar", [B_par, B_c, B_g16], [B_g16], out=Dcv[:, cc, t, :], in0=identf[:],
                            scalar1=colt[:, 32 + cc * 4 + t:33 + cc * 4 + t], scalar2=None, op0=ALU.mult)
            g_bc = rowt[:, 0:256]; bv_bc = rowt[:, 256:512]; gn_bc = rowt[:, 512:1024]; dtb_bc = rowt[:, 1024:1032]

            def halo_project():
                act(xhb, xh[:], AF.Copy, [B_xh], [B_xhb])
                p, Bp = pDn()
                for kc in range(8):
                    mm(p[:, 0:128], winb[:, kc, 512:640], xhb[:, kc, :], kc == 0, [B_win, B_xhb], [Bp], stop=(kc == 7))
                act(kT[:, 0:128], p[:, 0:128], AF.Copy, [Bp], [B_kT])
                p, Bp = pDn()
                for kc in range(8):
                    mm(p[:, 0:128], xhb[:, kc, :], winb[:, kc, 1920:2048], kc == 0, [B_win, B_xhb], [Bp], stop=(kc == 7))
                vop("dve", "tensor_copy", [Bp], [B_va[0]], out=vaug[:, 0, :, 0:64], in_=p[:, 0:128].rearrange("p (a b) -> p a b", a=2))
                for cc in range(8):
                    p, Bp = pDn()
                    for kc in range(8):
                        mm(p[:, 0:4], winb[:, kc, 640 + cc * 128:768 + cc * 128], xhb[:, kc, 124:128], kc == 0, [B_win, B_xhb], [Bp], stop=(kc == 7))
                    vop("dve", "tensor_copy", [Bp], [B_halo3], out=halo3[:, cc, :], in_=p[:, 1:4])

            def cast_tile(i, par=0):
                act(xb2[par], X[:, :, i * TA:(i + 1) * TA], AF.Copy, [B_X[i]], [B_xb2[par]])

            def conv_chunk(cc, ri, par=0):
                p, Bp = pDn()
                j = 5 + cc
                for kc in range(8):
                    mm(p[:, 0:TA], winb[:, kc, j * 128:(j + 1) * 128], xb2[par][:, kc, :], kc == 0, [B_win, B_xb2[par]], [Bp], stop=(kc == 7))
                rw = raw[ri]; Br = B_raw[ri]
                act(rw[:, 3:3 + TA], p[:, 0:TA], AF.Copy, [Bp], [Br])
                vop("dve", "tensor_copy", [B_halo3], [Br], out=rw[:, 0:3], in_=halo3[:, cc, :])
                vop("dve", "tensor_copy", [Br], [B_halo3], out=halo3[:, cc, :], in_=rw[:, TA:TA + 3])

                def cw(t):
                    return colt[:, 32 + cc * 4 + t:33 + cc * 4 + t]
                vop("dve", "tensor_scalar", [Br, B_par], [B_acc], out=acc, in0=rw[:, 0:TA], scalar1=cw(0), scalar2=None, op0=ALU.mult)
                for t in range(1, 4):
                    vop("dve", "scalar_tensor_tensor", [Br, B_par, B_acc], [B_acc], out=acc, in0=rw[:, t:t + TA], scalar=cw(t), in1=acc, op0=ALU.mult, op1=ALU.add)
                act(xact2[par][:, cc, :], acc, AF.Silu, [B_acc, B_par], [B_xact2[par]], bias=colt[:, 64 + cc:65 + cc])

            pp_banks = [(pD[0], B_pD[0]), (pD[1], B_pD[1]), (pT0, B_pT0), (pT1, B_pT1)]
            pp_i = [0]

            def pp_bank():
                pp_i[0] = (pp_i[0] + 1) % 4
                return pp_banks[pp_i[0]]

            def conv_pe_a(cc, ri):
                p, Bp = pp_bank()
                j = 5 + cc
                for kc in range(8):
                    mm(p[:, 0:TA], winb[:, kc, j * 128:(j + 1) * 128], xb[:, kc, :], kc == 0, [B_win, B_xb], [Bp], stop=(kc == 7))
                rw = rawb[ri]; Br = B_rawb[ri]
                act(rw[:, 3:3 + TA], p[:, 0:TA], AF.Copy, [Bp, B_g16], [Br])
                vop("pool", "tensor_copy", [B_halo3], [Br], out=rw[:, 0:3], in_=halo3[:, cc, :])
                vop("pool", "tensor_copy", [Br], [B_halo3], out=halo3[:, cc, :], in_=rw[:, TA:TA + 3])

            def conv_pe_b(cc, ri):
                rw = rawb[ri]; Br = B_rawb[ri]
                p2, Bp2 = pp_bank()
                for t in range(4):
                    mm(p2[:, 0:TA], Dcv[:, cc, t, :], rw[:, t:t + TA], t == 0, [B_g16, Br], [Bp2], stop=(t == 3))
                act(xact[:, cc, :], p2[:, 0:TA], AF.Silu, [Bp2, B_par], [B_xact], bias=colt[:, 64 + cc:65 + cc])

            def dt_chain(nch):
                n = nch * 8

                def v3(s_):
                    return sm[:, s_, 0:n].rearrange("p (c h) -> p c h", c=nch)
                vop("dve", "tensor_tensor", [B_sm, B_par], [B_sm], out=v3(0), in0=v3(0), in1=dtb_bc.unsqueeze(1).broadcast_to([128, nch, 8]), op=ALU.add)
                vop("dve", "scalar_tensor_tensor", [B_sm], [B_sm], out=smt(1, n), in0=smt(0, n), scalar=-1.0, in1=smt(0, n), op0=ALU.mult, op1=ALU.min)
                act(smt(1, n), smt(1, n), AF.Exp, [B_sm], [B_sm])
                act(smt(1, n), smt(1, n), AF.Ln, [B_sm, B_c], [B_sm], bias=epsc[:, 2:3])
                vop("dve", "scalar_tensor_tensor", [B_sm], [B_sm], out=smt(1, n), in0=smt(0, n), scalar=0.0, in1=smt(1, n), op0=ALU.max, op1=ALU.add)
                vop("dve", "tensor_tensor", [B_sm, B_der], [B_sm], out=v3(2), in0=v3(1), in1=a_bc[:].unsqueeze(1).broadcast_to([128, nch, 8]), op=ALU.mult)
                act(smt(3, n), smt(1, n), AF.Ln, [B_sm], [B_sm])
                mm(pY[:, 0:n], Uf[:], smt(2, n), True, [B_sm, B_c], [B_pY])
                mm(pY[:, 16:16 + n], onesf[:], smt(2, n), False, [B_sm, B_c], [B_pY])
                mm(pY[:, 0:32], zerosb[:, 0:128], zerosb[:, 0:32], False, [B_c], [B_pY])
                act(smt(13, n), pY[:, 0:n], AF.Copy, [B_pY], [B_sm])
                act(smt(14, n), pY[:, 16:16 + n], AF.Copy, [B_pY], [B_sm])
                act(smt(4, n), smt(13, n), AF.Exp, [B_sm], [B_sm])
                vop("dve", "tensor_tensor", [B_sm], [B_sm], out=smt(5, n), in0=smt(3, n), in1=smt(13, n), op=ALU.subtract)
                vop("dve", "tensor_tensor", [B_sm], [B_sm], out=smt(6, n), in0=smt(5, n), in1=smt(14, n), op=ALU.add)
                act(smt(6, n), smt(6, n), AF.Exp, [B_sm], [B_sm])
                act(smt(7, n), smt(14, n), AF.Exp, [B_sm], [B_sm])

            def tok_transposes(c, par=0):
                cs = slice(c * 128, (c + 1) * 128)
                for cc in range(4):
                    tr(pTP[:, cc * 128:(cc + 1) * 128], xact2[par][:, cc, cs], identb[:], [B_xact2[par], B_c], [B_pTP])
                act(xstok, pTP[:, 0:512], AF.Copy, [B_pTP], [B_xstok])
                for g in range(2):
                    tr(pTP[:, g * 128:(g + 1) * 128], xact2[par][:, 4 + g, cs], identb[:], [B_xact2[par], B_c], [B_pTP])
                vop("dve", "tensor_copy", [B_pTP], [B_Btok], out=Btok, in_=pTP[:, 0:256])

            def state_update(c, want_hb=True):
                k8 = slice(c * 8, c * 8 + 8)
                vop("pool", "tensor_tensor", [B_xstok, B_sm], [B_xw], out=v8(xw), in0=v8(xstok), in1=sm[:, 6, k8].unsqueeze(2).broadcast_to([128, 8, 64]), op=ALU.mult)
                for g in range(2):
                    mm(pY[:, g * 256:(g + 1) * 256], Btok[:, g * 128:(g + 1) * 128], xw[:, g * 256:(g + 1) * 256], g == 0, [B_Btok, B_xw], [B_pY], stop=(g == 1))
                vop("dve", "tensor_tensor", [B_h, B_sm], [B_h], out=v8(hst[:]), in0=v8(hst[:]), in1=sm[:, 7, k8].unsqueeze(2).broadcast_to([128, 8, 64]), op=ALU.mult)
                vop("dve", "tensor_tensor", [B_h, B_pY], [B_h], out=hst[:], in0=hst[:], in1=pY[:], op=ALU.add)
                if want_hb:
                    act(hb[:], hst[:], AF.Copy, [B_h], [B_hb])

            ck(1)
            halo_project()
            vop("dve", "memset", [], [B_h], ap=hst[:], constant=0.0)
            if use_cc:
                for i in range(NTA):
                    cast_tile(i)
                    conv_pe_a(0, 0)
                    for cc in range(6):
                        if cc + 1 < 6:
                            conv_pe_a(cc + 1, (cc + 1) % 2)
                        conv_pe_b(cc, cc % 2)
                    nl_ = len(P.live_dma)
                    dma("sp", xsc[:, :, i * TA:(i + 1) * TA], xact[:, 0:6, :], [B_xact], [B_xsc[i]])
                    del P.live_dma[nl_:]
                    for c in range(2):
                        p, Bp = pDn()
                        for kc in range(8):
                            mm(p[:, 0:8], xb[:, kc, c * 128:(c + 1) * 128], winb[:, kc, 2048:2056], kc == 0, [B_win, B_xb], [Bp], stop=(kc == 7))
                        vop("dve", "tensor_copy", [Bp], [B_sm], out=sm[:, 0, c * 8:c * 8 + 8], in_=p[:, 0:8])
                    dt_chain(2)
                    for c in range(2):
                        tok_transposes(c)
                        state_update(c, want_hb=False)
                    if i == 0:
                        vop("dve", "tensor_tensor", [B_sm], [B_sm], out=sm[:, 8, 0:8], in0=sm[:, 7, 0:8], in1=sm[:, 7, 8:16], op=ALU.mult)
                    else:
                        vop("dve", "tensor_tensor", [B_sm], [B_sm], out=sm[:, 8, 0:8], in0=sm[:, 8, 0:8], in1=sm[:, 7, 0:8], op=ALU.mult)
                        vop("dve", "tensor_tensor", [B_sm], [B_sm], out=sm[:, 8, 0:8], in0=sm[:, 8, 0:8], in1=sm[:, 7, 8:16], op=ALU.mult)
                ck(2)
                dma("sp", cc_st_in[:, 0:512], hst[:], [B_h], [B_ccst])
                dma("sp", cc_st_in[:, 512:520], sm[:, 8, 0:8], [B_sm], [B_ccst])
                P.op("pool", lambda e: e.collective_compute("AllGather", ALU.bypass, replica_groups=[[0, 1, 2, 3], [4, 5, 6, 7]],
                                                            ins=[cc_st_in], outs=[cc_st_out]), r=[B_ccst], w=[B_ccst_o], dma=True, inc=1)
                dma("sp", gst, cc_st_out.rearrange("(r p) f -> p r f", p=128), [B_ccst_o], [B_gst])
                vop("dve", "memset", [], [B_h], ap=hst[:], constant=0.0)
                for r_ in range(3):
                    vop("dve", "tensor_tensor", [B_h, B_gst], [B_yt1], out=v8(yt1), in0=v8(hst[:]), in1=gst[:, r_, 512:520].unsqueeze(2).broadcast_to([128, 8, 64]), op=ALU.mult)
                    vop("dve", "tensor_tensor", [B_yt1, B_gst], [B_yt1], out=yt1, in0=yt1, in1=gst[:, r_, 0:512], op=ALU.add)
                    vop("dve", "tensor_tensor", [B_yt1, B_h], [B_yt1], out=yt1, in0=yt1, in1=hst[:], op=ALU.subtract)
                    vop("dve", "scalar_tensor_tensor", [B_yt1, B_h, B_c], [B_h], out=hst[:], in0=yt1, scalar=sel[:, r_:r_ + 1], in1=hst[:], op0=ALU.mult, op1=ALU.add)
                halo_project()
            act(hb[:], hst[:], AF.Copy, [B_h], [B_hb])

            ck(3)
            P.barrier()
            vslot = [0]

            def stage_proj(i, par):
                pieces = []
                pieces.append(lambda: cast_tile(i, par))

                def p_gau(j):
                    p, Bp = pDn()
                    for kc in range(8):
                        mm(p[:, 0:TA], winb[:, kc, j * 128:(j + 1) * 128], xb2[par][:, kc, :], kc == 0, [B_win, B_xb2[par]], [Bp], stop=(kc == 7))
                    act(gau2[par][:, j, :], p[:, 0:TA], AF.Gelu_apprx_tanh, [Bp], [B_gau2[par]])

                def p_q(j):
                    p, Bp = pDn()
                    for kc in range(8):
                        mm(p[:, 0:TA], winb[:, kc, (2 + j) * 128:(3 + j) * 128], xb2[par][:, kc, :], kc == 0, [B_win, B_xb2[par]], [Bp], stop=(kc == 7))
                    act(qT2[par][:, j, :], p[:, 0:TA], AF.Copy, [Bp], [B_qT2[par]], scale=0.125)

                def p_k():
                    p, Bp = pDn()
                    for kc in range(8):
                        mm(p[:, 0:TA], winb[:, kc, 512:640], xb2[par][:, kc, :], kc == 0, [B_win, B_xb2[par]], [Bp], stop=(kc == 7))
                    if i > 0:
                        vop("pool", "tensor_copy", [B_kT], [B_kT], out=kT[:, 0:128], in_=kT[:, TA:TA + 128])
                    act(kT[:, 128:128 + TA], p[:, 0:TA], AF.Copy, [Bp], [B_kT])
                for j in range(2):
                    pieces.append(lambda j=j: p_gau(j))
                for j in range(2):
                    pieces.append(lambda j=j: p_q(j))
                pieces.append(p_k)
                if use_cc:
                    def p_load():
                        nl_ = len(P.live_dma)
                        dma("sp", xact2[par][:, 0:6, :], xsc[:, :, i * TA:(i + 1) * TA], [B_xsc[i]], [B_xact2[par]])
                        del P.live_dma[nl_:]
                    pieces.insert(1, p_load)
                    for cc in range(6, 8):
                        pieces.append(lambda cc=cc: conv_chunk(cc, cc % 2, par))
                else:
                    for cc in range(8):
                        pieces.append(lambda cc=cc: conv_chunk(cc, cc % 2, par))
                return pieces

            pending = []

            def fill(n=1):
                for _ in range(n):
                    if pending:
                        pending.pop(0)()

            def stage_attn(i, par):
                for c in range(2):
                    cs = slice(c * 128, (c + 1) * 128)
                    gtok = i * 2 + c
                    for kc in range(8):
                        mm(pT0[:, 0:392], xb2[par][:, kc, cs], winb[:, kc, 1664:2056], kc == 0, [B_win, B_xb2[par]], [B_pT0], stop=(kc == 7))
                    for kc in range(8):
                        mm(pT1[:], xb2[par][:, kc, cs], winb[:, kc, 2056:2568], kc == 0, [B_win, B_xb2[par]], [B_pT1], stop=(kc == 7))
                    act(gav, pT0[:, 0:256], AF.Gelu_apprx_tanh, [B_pT0], [B_gav, B_sm], accum_out=sm[:, 9, 0:1])
                    prv = vslot[0]
                    cur = (vslot[0] + 1) % 3
                    vslot[0] = cur
                    vop("dve", "tensor_copy", [B_pT0], [B_va[cur]], out=vaug[:, cur, :, 0:64], in_=pT0[:, 256:384].rearrange("p (a b) -> p a b", a=2))
                    vop("dve", "tensor_copy", [B_pT0], [B_sm], out=sm[:, 0, c * 8:c * 8 + 8], in_=pT0[:, 384:392])
                    act(sz[:, c, :], pT1[:], AF.Silu, [B_pT1], [B_sz[c]])
                    if gtok == NT - 1:
                        vop("dve", "tensor_copy", [B_pT0], [B_os], out=ostage[:, 0:128], in_=pT0[:, 256:384])
                        odma(vP[l], ostage[:, 0:128], [B_os])
                        p, Bp = pDn()
                        for kc in range(8):
                            mm(p[:, 0:128], xb2[par][:, kc, cs], winb[:, kc, 512:640], kc == 0, [B_win, B_xb2[par]], [Bp], stop=(kc == 7))
                        vop("dve", "tensor_copy", [Bp], [B_os], out=ostage[:, 128:256], in_=p[:, 0:128])
                        odma(kP[l], ostage[:, 128:256], [B_os])
                    act(junk[:, 0:256], gav, AF.Square, [B_gav], [B_junk, B_sm], accum_out=sm[:, 9, 1:2])
                    vop("dve", "tensor_scalar", [B_sm], [B_sm], out=sm[:, 9, 2:3], in0=sm[:, 9, 0:1], scalar1=-1.0 / 256, scalar2=None, op0=ALU.mult)
                    vop("dve", "tensor_tensor", [B_sm], [B_sm], out=sm[:, 9, 3:4], in0=sm[:, 9, 2:3], in1=sm[:, 9, 2:3], op=ALU.mult)
                    vop("dve", "scalar_tensor_tensor", [B_sm], [B_sm], out=sm[:, 9, 3:4], in0=sm[:, 9, 1:2], scalar=1.0 / 256, in1=sm[:, 9, 3:4], op0=ALU.mult, op1=ALU.subtract)
                    act(sm[:, 9, 3:4], sm[:, 9, 3:4], AF.Ln, [B_sm, B_c], [B_sm], bias=epsc[:, 0:1])
                    act(sm[:, 9, 3:4], sm[:, 9, 3:4], AF.Exp, [B_sm], [B_sm], scale=-0.5)
                    vop("dve", "tensor_scalar", [B_gav, B_sm], [B_vnf], out=vnf, in0=gav, scalar1=sm[:, 9, 2:3], scalar2=sm[:, 9, 3:4], op0=ALU.add, op1=ALU.mult)
                    vop("dve", "tensor_tensor", [B_vnf, B_par], [B_vnf], out=vnf, in0=vnf, in1=g_bc, op=ALU.mult)
                    vop("dve", "tensor_tensor", [B_vnf, B_par], [B_vn], out=vn, in0=vnf, in1=bv_bc, op=ALU.add)
                    for h in range(4):
                        mm(pM[(h % 2) * 64:(h % 2) * 64 + 64, (h // 2) * 128:(h // 2) * 128 + 128], vn[:, h * 64:(h + 1) * 64], wsTb[:, h, :], True, [B_vn, B_der], [B_pM])
                    yv = yt2[:, 0:256].rearrange("p (a b) -> p a b", a=2)
                    vop("dve", "tensor_tensor", [B_pM, B_par], [B_yt2], out=yv, in0=pM.rearrange("p (a b) -> p a b", a=2), in1=bsb[:], op=ALU.add)
                    vop("dve", "tensor_tensor", [B_yt2, B_gau2[par]], [B_cat], out=catT[:, 0:2, cs], in0=yv, in1=gau2[par][:, :, cs], op=ALU.mult)
                    ngA = negA1 if gtok == 0 else negA
                    for kv in range(2):
                        S4 = pS[kv][:].rearrange("p (g b t) -> p g b t", g=2, b=2)
                        mm(pS[kv][:], identb[:], ngA[:], True, [B_c], [B_pS[kv]], stop=False)
                        for gi in range(2):
                            for blk in range(2):
                                kc0 = c * 128 + blk * 128
                                mm(S4[:, gi, blk, :], kT[kv * 64:kv * 64 + 64, kc0:kc0 + 128], qT2[par][kv * 64:kv * 64 + 64, gi, cs], False,
                                   [B_kT, B_qT2[par]], [B_pS[kv]], stop=(gi == 1 and blk == 1))
                        act(Eb[kv], pS[kv][:], AF.Exp, [B_pS[kv]], [B_E[kv]])
                    AO = pY[:, 0:260].rearrange("p (h d) -> p h d", h=4)
                    first = True
                    for kv in range(2):
                        E4 = Eb[kv].rearrange("p (g b t) -> p g b t", g=2, b=2)
                        for gi in range(2):
                            hidx = gi * 2 + kv
                            for blk in range(2):
                                vs = prv if blk == 0 else cur
                                mm(AO[:, hidx, :], E4[:, gi, blk, :], vaug[:, vs, kv, :], first, [B_E[kv], B_va[vs]], [B_pY], stop=(kv == 1 and gi == 1 and blk == 1))
                                first = False
                    vop("dve", "tensor_tensor", [B_pY, B_der], [B_sm], out=sm[:, 10, 0:4], in0=AO[:, :, 64], in1=esink[:], op=ALU.add)
                    vop("dve", "reciprocal", [B_sm], [B_sm], out=sm[:, 10, 4:8], in_=sm[:, 10, 0:4])
                    vop("dve", "tensor_tensor", [B_pY, B_sm], [B_ob], out=ob.rearrange("p (h d) -> p h d", h=4), in0=AO[:, :, 0:64],
                        in1=sm[:, 10, 4:8].unsqueeze(2).broadcast_to([128, 4, 64]), op=ALU.mult)
                    for j in range(2):
                        tr(pTP[:, j * 128:(j + 1) * 128], ob[:, j * 128:(j + 1) * 128], identb[:], [B_ob, B_c], [B_pTP])
                    act(catT[:, 2:4, cs], pTP[:, 0:256].rearrange("p (a b) -> p a b", a=2), AF.Copy, [B_pTP], [B_cat])

            def stage_ssd(i, par):
                dt_chain(2)
                for c in range(2):
                    cs = slice(c * 128, (c + 1) * 128)
                    k8 = slice(c * 8, c * 8 + 8)
                    tok_transposes(c, par)
                    for g in range(2):
                        mm(pT1[:, g * 256:(g + 1) * 256], xact2[par][:, 6 + g, cs], hb[:, g * 256:(g + 1) * 256], g == 0, [B_xact2[par], B_hb], [B_pT1], stop=(g == 1))
                    state_update(c)
                    vop("pool", "tensor_tensor", [B_sm, B_c], [B_R], out=v8(Rt), in0=sm[:, 2, k8].unsqueeze(2).broadcast_to([128, 8, 128]),
                        in1=Uf[:].unsqueeze(1).broadcast_to([128, 8, 128]), op=ALU.mult)
                    for hf in range(2):
                        mm(pS[hf][:], onesf[:], Rt[:, hf * 512:(hf + 1) * 512], True, [B_R, B_c], [B_pS[hf]], stop=False)
                        mm(pS[hf][:], identb[:], negS[:], False, [B_c], [B_pS[hf]])
                        for hh in range(4):
                            h = hf * 4 + hh
                            act(LT[:, h, :], pS[hf][:, hh * 128:(hh + 1) * 128], AF.Exp, [B_pS[hf], B_sm], [B_LT], bias=sm[:, 5, c * 8 + h:c * 8 + h + 1])
                    fill(1)
                    for g in range(2):
                        mm(pY[:, 64 + g * 128:192 + g * 128], xact2[par][:, 4 + g, cs], xact2[par][:, 6 + g, cs], True, [B_xact2[par]], [B_pY])
                    for g in range(2):
                        vop("dve", "tensor_tensor", [B_LT, B_pY], [B_G], out=Gt[:, g * 4:g * 4 + 4, :], in0=LT[:, g * 4:g * 4 + 4, :],
                            in1=pY[:, 64 + g * 128:192 + g * 128].unsqueeze(1).broadcast_to([128, 4, 128]), op=ALU.mult)
                    for h in range(8):
                        mm(pT0[:, h * 64:(h + 1) * 64], Gt[:, h, :], xstok[:, h * 64:(h + 1) * 64], h == 0, [B_G, B_xstok], [B_pT0], stop=False)
                    for cc in range(4):
                        mm(pT0[:, cc * 128:(cc + 1) * 128], xact2[par][:, cc, cs], Dg[:, cc, :], False, [B_xact2[par], B_der], [B_pT0], stop=(cc == 3))
                    fill(1)
                    vop("dve", "tensor_tensor", [B_pT1, B_sm], [B_yt1], out=v8(yt1), in0=v8(pT1[:]), in1=sm[:, 4, k8].unsqueeze(2).broadcast_to([128, 8, 64]), op=ALU.mult)
                    vop("dve", "tensor_tensor", [B_yt1, B_pT0], [B_yt1], out=yt1, in0=yt1, in1=pT0[:], op=ALU.add)
                    vop("dve", "tensor_tensor", [B_yt1, B_sz[c]], [B_yt2], out=yt2, in0=yt1, in1=sz[:, c, :], op=ALU.mult)
                    act(junk, yt2, AF.Square, [B_yt2], [B_junk, B_sm], accum_out=sm[:, 11, 0:1])
                    act(sm[:, 11, 1:2], sm[:, 11, 0:1], AF.Ln, [B_sm, B_c], [B_sm], bias=epsc[:, 1:2], scale=1.0 / 512)
                    act(sm[:, 11, 1:2], sm[:, 11, 1:2], AF.Exp, [B_sm], [B_sm], scale=-0.5)
                    vop("dve", "scalar_tensor_tensor", [B_yt2, B_sm, B_par], [B_oc], out=ocb, in0=yt2, scalar=sm[:, 11, 1:2], in1=gn_bc, op0=ALU.mult, op1=ALU.mult)
                    fill(1)
                    for cc in range(4):
                        tr(pTP[:, cc * 128:(cc + 1) * 128], ocb[:, cc * 128:(cc + 1) * 128], identb[:], [B_oc, B_c], [B_pTP])
                    act(catT[:, 4:8, cs], pTP[:, 0:512].rearrange("p (a b) -> p a b", a=4), AF.Copy, [B_pTP], [B_cat])
                    fill(1)

            def stage_out(i, par):
                cols = slice(i * TA, (i + 1) * TA)
                for dc in range(8):
                    p, Bp = pDn()
                    for ec in range(8):
                        mm(p[:, 0:TA], woutb[:, ec, dc * 128:(dc + 1) * 128], catT[:, ec, :], ec == 0, [B_wout, B_cat], [Bp], stop=(ec == 7))
                    vop("dve", "scalar_tensor_tensor", [B_X[i], Bp], [B_X[i]], out=X[:, dc, cols], in0=X[:, dc, cols], scalar=ALPHA, in1=p[:, 0:TA], op0=ALU.mult, op1=ALU.add)
                if i == NTA - 1:
                    for half in range(2):
                        p, Bp = pDn()
                        for kc in range(8):
                            mm(p[0:3, :], xb2[par][:, kc, TA - 3:TA], winb[:, kc, 640 + half * 512:1152 + half * 512], kc == 0, [B_win, B_xb2[par]], [Bp], stop=(kc == 7))
                        vop("dve", "tensor_copy", [Bp], B_sz, out=cpst[0:3, half * 512:(half + 1) * 512], in_=p[0:3, :])
                    odma(cP[l], cpst[0:3, :], B_sz)
                layer_norm_fm(cols, TA, [B_X[i]], 0, 8, xb2[par], xact2[par], [B_xb2[par]], [B_xact2[par]], nmean, lvar, B_nm, B_lv)

            for pc in stage_proj(0, 0):
                pc()
            for i in range(NTA):
                par = i % 2
                stage_attn(i, par)
                if i + 1 < NTA:
                    pending.extend(stage_proj(i + 1, 1 - par))
                    fill(1)
                stage_ssd(i, par)
                stage_out(i, par)
                fill(len(pending))
            vop("dve", "tensor_copy", [B_h], [B_yt1], out=yt1, in_=hst[:])
            odma(hP[l], yt1, [B_yt1])

            dbg("sm", sm[:], [B_sm]); dbg("hst", hst[:], [B_h]); dbg("gst", gst, [B_gst]); dbg("yt1", yt1, [B_yt1]); dbg("yt2", yt2, [B_yt2])
            dbg("catT", catT, [B_cat], BF16); dbg("xact", xact, [B_xact], BF16); dbg("LT", LT, [B_LT], BF16); dbg("Gt", Gt, [B_G], BF16)
            dbg("xstok", xstok, [B_xstok], BF16); dbg("sz", sz, B_sz); dbg("Rt", Rt, [B_R]); dbg("gau", gau, [B_gau]); dbg("kT", kT[:], [B_kT], BF16)
            dbg("vaug", vaug[:], B_va, BF16); dbg("Eb0", Eb[0], [B_E[0]], BF16); dbg("ob", ob, [B_ob], BF16); dbg("qT", qT, [B_qT], BF16)
            ck(4)
            P.barrier()
            SI = NTA
            scols = slice(TP, TX)
            xbs = xb[:, :, 0:NS]
            act(xbs, X[:, :, scols], AF.Copy, [B_X[SI]], [B_xb])
            ck(401)
            for (c0, c1) in ((0, 512), (512, 1024), (1024, 1536), (1536, 2048), (2048, 2560), (2560, 2568)):
                p, Bp = pDn()
                for kc in range(8):
                    mm(p[0:NS, 0:c1 - c0], xb[:, kc, 0:NS], winb[:, kc, c0:c1], kc == 0, [B_win, B_xb], [Bp], stop=(kc == 7))
                vop("dve", "tensor_copy", [Bp], [B_stok], out=s_tok[:, c0:c1], in_=p[0:NS, 0:c1 - c0])
            ck(402)
            dma("sp", kS[l], s_tok[:, 512:640], [B_stok], [B_kS])
            dma("sp", vS[l], s_tok[:, 1920:2048], [B_stok], [B_vS])
            ck(403)
            for b in range(NS):
                dma("pool", s_kb[:, b, :], st_k[l, b], [], [B_skb])
                dma("pool", s_vb[:, b, :], st_v[l, b], [], [B_svb])
            ck(41)
            dma("pool", s_kb[0:1, :, :], kS[l:l + 1], [B_kS], [B_skb])
            dma("pool", s_vb[0:1, :, :], vS[l:l + 1], [B_vS], [B_svb])
            for b0 in range(0, NS, 4):
                for b in range(b0, b0 + 4):
                    tr(pTP[:, (b - b0) * 128:(b - b0 + 1) * 128], s_kb[:, b, :], identb[:], [B_skb, B_c], [B_pTP])
                act(s_kT[:, b0:b0 + 4, :], pTP[:, 0:512].rearrange("p (a b) -> p a b", a=4), AF.Copy, [B_pTP], [B_skT])
            for j in range(2):
                p, Bp = pDn()
                for kc in range(8):
                    mm(p[:, 0:NS], winb[:, kc, (2 + j) * 128:(3 + j) * 128], xb[:, kc, 0:NS], kc == 0, [B_win, B_xb], [Bp], stop=(kc == 7))
                act(s_q[:, j, :], p[:, 0:NS], AF.Copy, [Bp], [B_sq], scale=0.125)
            for kv in range(2):
                for gi in range(2):
                    for b in range(NS):
                        col = gi * NS + b
                        mm(pS[kv][:, col:col + 1], s_kT[kv * 64:kv * 64 + 64, b, :], s_q[kv * 64:kv * 64 + 64, gi, b:b + 1], True, [B_skT, B_sq], [B_pS[kv]])
                act(s_E[:, kv * 32:(kv + 1) * 32], pS[kv][:, 0:32], AF.Exp, [B_pS[kv]], [B_sE])
            ck(42)
            mm(pY[:, 0:64], onesb[:], s_E, True, [B_sE, B_c], [B_pY])
            for kv in range(2):
                for gi in range(2):
                    hidx = gi * 2 + kv
                    c0 = kv * 32 + gi * NS
                    vop("dve", "tensor_scalar", [B_pY, B_der], [B_srd], out=s_rd[:, c0:c0 + NS], in0=pY[:, c0:c0 + NS], scalar1=esink[:, hidx:hidx + 1], scalar2=None, op0=ALU.add)
            vop("dve", "reciprocal", [B_srd], [B_srd], out=s_rd, in_=s_rd)
            pO = pT0[:, 0:2 * NS].rearrange("p (g b) -> p g b", g=2)
            for kv in range(2):
                for gi in range(2):
                    for b in range(NS):
                        col = kv * 32 + gi * NS + b
                        mm(pO[kv * 64:kv * 64 + 64, gi, b:b + 1], s_vb[:, b, kv * 64:kv * 64 + 64], s_E[:, col:col + 1], True, [B_svb, B_sE], [B_pT0])
            for kv in range(2):
                vop("dve", "tensor_tensor", [B_pT0, B_srd], [B_scatT], out=s_catT[kv * 64:kv * 64 + 64, 2:4, :], in0=pO[kv * 64:kv * 64 + 64, :, :],
                    in1=s_rd[kv * 64:kv * 64 + 64, kv * 32:kv * 32 + 32].rearrange("p (g b) -> p g b", g=2), op=ALU.mult)
            ck(43)
            sN = lambda a, b: sm[0:NS, 12, a:b]
            act(s_misc[:, 0:256], s_tok[:, 0:256], AF.Gelu_apprx_tanh, [B_stok], [B_smisc])
            act(s_misc[:, 256:512], s_tok[:, 1664:1920], AF.Gelu_apprx_tanh, [B_stok], [B_smisc, B_sm], accum_out=sN(0, 1))
            act(junk[0:NS, 0:256], s_misc[:, 256:512], AF.Square, [B_smisc], [B_junk, B_sm], accum_out=sN(1, 2))
            vop("dve", "tensor_scalar", [B_sm], [B_sm], out=sN(2, 3), in0=sN(0, 1), scalar1=-1.0 / 256, scalar2=None, op0=ALU.mult)
            vop("dve", "tensor_tensor", [B_sm], [B_sm], out=sN(3, 4), in0=sN(2, 3), in1=sN(2, 3), op=ALU.mult)
            vop("dve", "scalar_tensor_tensor", [B_sm], [B_sm], out=sN(3, 4), in0=sN(1, 2), scalar=1.0 / 256, in1=sN(3, 4), op0=ALU.mult, op1=ALU.subtract)
            act(sN(3, 4), sN(3, 4), AF.Ln, [B_sm, B_c], [B_sm], bias=epsc[0:NS, 0:1])
            act(sN(3, 4), sN(3, 4), AF.Exp, [B_sm], [B_sm], scale=-0.5)
            vnS_ = s_misc[:, 256:512]
            vop("dve", "tensor_scalar", [B_smisc, B_sm], [B_smisc], out=vnS_, in0=vnS_, scalar1=sN(2, 3), scalar2=sN(3, 4), op0=ALU.add, op1=ALU.mult)
            vop("dve", "tensor_tensor", [B_smisc, B_par], [B_smisc], out=vnS_, in0=vnS_, in1=rowt[0:NS, 0:256], op=ALU.mult)
            vop("dve", "tensor_tensor", [B_smisc, B_par], [B_smisc], out=vnS_, in0=vnS_, in1=rowt[0:NS, 256:512], op=ALU.add)
            odma(vnS[l], vnS_, [B_smisc])
            m4 = s_misc[:, 512:768].rearrange("p (h d) -> p h d", h=4)
            vop("dve", "tensor_tensor", [B_smisc, B_par], [B_smisc], out=m4, in0=vnS_.rearrange("p (h d) -> p h d", h=4),
                in1=rowt[0:NS, 1052:1056].unsqueeze(2).broadcast_to([NS, 4, 64]), op=ALU.mult)
            vop("dve", "tensor_tensor", [B_smisc, B_par], [B_smisc], out=m4, in0=m4, in1=rowt[0:NS, 1056:1060].unsqueeze(2).broadcast_to([NS, 4, 64]), op=ALU.add)
            vop("dve", "tensor_tensor", [B_smisc], [B_scat], out=s_cat[:, 0:256], in0=s_misc[:, 512:768], in1=s_misc[:, 0:256], op=ALU.mult)
            ck(44)
            dma("sp", s_hc, st_c[l], [], [B_shc])
            odma(cS[l, :, 0:2, :], st_c[l].rearrange("(b i) c -> b i c", i=3)[:, 1:3, :], [])
            odma(cS[l, :, 2, :], s_tok[:, 640:1664], [B_stok])
            for cc in range(8):
                mm(pT1[:, cc * 48:(cc + 1) * 48], s_hc[:, cc * 128:(cc + 1) * 128], identf[0:NS * 3, 0:NS * 3], cc == 0, [B_shc, B_c], [B_pT1], stop=(cc == 7))
            mm(pT1[:, 0:384], zerosb[:, 0:128], zerosb[:, 0:384], False, [B_c], [B_pT1])
            vop("dve", "tensor_copy", [B_pT1], [B_shist], out=s_hist, in_=pT1[:, 0:384].rearrange("p (c j) -> p c j", c=8))
            for cc in range(8):
                p, Bp = pDn()
                for kc in range(8):
                    mm(p[:, 0:NS], winb[:, kc, (5 + cc) * 128:(6 + cc) * 128], xb[:, kc, 0:NS], kc == 0, [B_win, B_xb], [Bp], stop=(kc == 7))
                vop("dve", "tensor_copy", [Bp], [B_sxfm], out=s_xfm[:, cc, :], in_=p[:, 0:NS])
            cw4 = colt[:, 32:64].rearrange("p (c t) -> p c t", c=8)
            hist4 = s_hist.rearrange("p c (b i) -> p c b i", i=3)
            vop("dve", "tensor_tensor", [B_sxfm, B_par], [B_scv], out=s_cv, in0=s_xfm, in1=cw4[:, :, 3:4].broadcast_to([128, 8, NS]), op=ALU.mult)
            for t in range(3):
                vop("dve", "tensor_tensor", [B_shist, B_par], [B_scv2], out=s_cv2, in0=hist4[:, :, :, t], in1=cw4[:, :, t:t + 1].broadcast_to([128, 8, NS]), op=ALU.mult)
                vop("dve", "tensor_tensor", [B_scv, B_scv2], [B_scv], out=s_cv, in0=s_cv, in1=s_cv2, op=ALU.add)
            vop("dve", "tensor_tensor", [B_scv, B_par], [B_scv], out=s_cv, in0=s_cv, in1=colt[:, 64:72].unsqueeze(2).broadcast_to([128, 8, NS]), op=ALU.add)
            act(s_cv, s_cv, AF.Silu, [B_scv], [B_scv])
            for cc in range(8):
                bank, Bb = (pT1, B_pT1) if cc < 4 else (pS[0], B_pS[0])
                mm(bank[0:NS, (cc % 4) * 128:(cc % 4 + 1) * 128], s_cv[:, cc, :], identf[:], cc % 4 == 0, [B_scv, B_c], [Bb], stop=(cc % 4 == 3))
            mm(pT1[0:NS, :], zerosb[:, 0:NS], zerosb[:], False, [B_c], [B_pT1])
            mm(pS[0][0:NS, :], zerosb[:, 0:NS], zerosb[:], False, [B_c], [B_pS[0]])
            vop("dve", "tensor_copy", [B_pT1], [B_spk], out=s_pk[:, 0:512], in_=pT1[0:NS, :])
            vop("dve", "tensor_copy", [B_pS[0]], [B_spk], out=s_pk[:, 512:1024], in_=pS[0][0:NS, :])
            dtr = s_pk[:, 1024:1032]; dtt = s_pk[:, 1032:1040]; dAe = s_pk[:, 1040:1048]
            vop("dve", "tensor_tensor", [B_stok, B_par], [B_spk], out=dtr, in0=s_tok[:, 2048:2056], in1=rowt[0:NS, 1024:1032], op=ALU.add)
            vop("dve", "scalar_tensor_tensor", [B_spk], [B_spk], out=dtt, in0=dtr, scalar=-1.0, in1=dtr, op0=ALU.mult, op1=ALU.min)
            act(dtt, dtt, AF.Exp, [B_spk], [B_spk])
            act(dtt, dtt, AF.Ln, [B_spk, B_c], [B_spk], bias=epsc[0:NS, 2:3])
            vop("dve", "scalar_tensor_tensor", [B_spk], [B_spk], out=dtt, in0=dtr, scalar=0.0, in1=dtt, op0=ALU.max, op1=ALU.add)
            vop("dve", "tensor_tensor", [B_spk, B_der], [B_spk], out=dAe, in0=dtt, in1=a_bc[0:NS, :], op=ALU.mult)
            act(dAe, dAe, AF.Exp, [B_spk], [B_spk])
            vop("dve", "tensor_tensor", [B_spk], [B_sact], out=v8(s_act[:, 0:512]), in0=v8(s_pk[:, 0:512]), in1=dtt.unsqueeze(2).broadcast_to([NS, 8, 64]), op=ALU.mult)
            ck(5)
            dma("sp", sbn[l], s_pk, [B_spk], [B_sbn])
            dma("sp", sbx[l], s_act[:, 0:512], [B_sact], [B_sbx])
            dma("sp", sbd[l], s_pk[:, 1040:1048], [B_spk], [B_sbd])
            for hh in range(4):
                dma("sp", sbB[l, :, :, hh, :], s_pk[:, 512:768].rearrange("p (g n) -> p g n", g=2), [B_spk], [B_sbB])
                dma("sp", sbC[l, :, :, hh, :], s_pk[:, 768:1024].rearrange("p (g n) -> p g n", g=2), [B_spk], [B_sbC])
            P.barrier()
            dma("sp", s_x8[:, 0:64], sbx[l].rearrange("b (h d) -> (b h) d", h=8), [B_sbx], [B_sx8])
            dma("sp", s_x8[:, 64:192], sbB[l].rearrange("b g h n -> (b g h) n"), [B_sbB], [B_sx8])
            dma("sp", s_x8[:, 192:320], sbC[l].rearrange("b g h n -> (b g h) n"), [B_sbC], [B_sx8])
            dma("sp", s_x8[:, 320:321], sbd[l].rearrange("b (h o) -> (b h) o", o=1), [B_sbd], [B_sx8])
            sth = st_h[l].rearrange("p (d n) -> p d n", n=128)
            hSl = hS[l].rearrange("p (d n) -> p d n", n=128)
            for half in range(2):
                dma("sp", s_h, sth[:, half * 32:(half + 1) * 32, :], [], [B_sh])
                act(s_h, s_h, AF.Copy, [B_sh, B_sx8], [B_sh], scale=s_x8[:, 320:321])
                for q2 in range(2):
                    d0 = half * 32 + q2 * 16
                    vop("dve", "tensor_tensor", [B_sx8], [B_stmp], out=s_tmp, in0=s_x8[:, d0:d0 + 16].unsqueeze(2).broadcast_to([128, 16, 128]),
                        in1=s_x8[:, 64:192].unsqueeze(1).broadcast_to([128, 16, 128]), op=ALU.mult)
                    vop("pool", "tensor_tensor", [B_sh, B_stmp], [B_sh], out=s_h[:, q2 * 16:(q2 + 1) * 16, :], in0=s_h[:, q2 * 16:(q2 + 1) * 16, :], in1=s_tmp, op=ALU.add)
                odma(hSl[:, half * 32:(half + 1) * 32, :], s_h, [B_sh])
                for q2 in range(2):
                    d0 = half * 32 + q2 * 16
                    vop("dve", "tensor_tensor", [B_sh, B_sx8], [B_stmp], out=s_tmp, in0=s_h[:, q2 * 16:(q2 + 1) * 16, :],
                        in1=s_x8[:, 192:320].unsqueeze(1).broadcast_to([128, 16, 128]), op=ALU.mult)
                    vop("dve", "tensor_reduce", [B_stmp], [B_sy8], out=s_y8[:, d0:d0 + 16], in_=s_tmp, axis=AX.X, op=ALU.add)
            dma("sp", sby[l], s_y8, [B_sy8], [B_sby])
            dma("sp", s_act[:, 512:1024], sby[l].rearrange("(b h) d -> b (h d)", h=8), [B_sby], [B_sact])
            vop("dve", "tensor_tensor", [B_spk, B_par], [B_smisc], out=v8(s_misc[:, 0:512]), in0=v8(s_pk[:, 0:512]), in1=rowt[0:NS, 1044:1052].unsqueeze(2).broadcast_to([NS, 8, 64]), op=ALU.mult)
            vop("dve", "tensor_tensor", [B_smisc, B_sact], [B_smisc], out=s_misc[:, 0:512], in0=s_misc[:, 0:512], in1=s_act[:, 512:1024], op=ALU.add)
            act(s_misc[:, 512:1024], s_tok[:, 2056:2568], AF.Silu, [B_stok], [B_smisc])
            vop("dve", "tensor_tensor", [B_smisc], [B_smisc], out=s_misc[:, 0:512], in0=s_misc[:, 0:512], in1=s_misc[:, 512:1024], op=ALU.mult)
            act(junk[0:NS, :], s_misc[:, 0:512], AF.Square, [B_smisc], [B_junk, B_sm], accum_out=sN(5, 6))
            act(sN(6, 7), sN(5, 6), AF.Ln, [B_sm, B_c], [B_sm], bias=epsc[0:NS, 1:2], scale=1.0 / 512)
            act(sN(6, 7), sN(6, 7), AF.Exp, [B_sm], [B_sm], scale=-0.5)
            vop("dve", "scalar_tensor_tensor", [B_smisc, B_sm, B_par], [B_scat], out=s_cat[:, 512:1024], in0=s_misc[:, 0:512], scalar=sN(6, 7), in1=rowt[0:NS, 512:1024], op0=ALU.mult, op1=ALU.mult)
            for idx, c0 in enumerate((0, 128, 512, 640, 768, 896)):
                tr(pTP[:, idx * NS:(idx + 1) * NS], s_cat[:, c0:c0 + 128], identb[0:NS, 0:NS], [B_scat, B_c], [B_pTP])
            act(s_catT[:, 0:2, :], pTP[:, 0:2 * NS].rearrange("p (a b) -> p a b", a=2), AF.Copy, [B_pTP], [B_scatT])
            act(s_catT[:, 4:8, :], pTP[:, 2 * NS:6 * NS].rearrange("p (a b) -> p a b", a=4), AF.Copy, [B_pTP], [B_scatT])
            for dc in range(8):
                p, Bp = pDn()
                for ec in range(8):
                    mm(p[:, 0:NS], woutb[:, ec, dc * 128:(dc + 1) * 128], s_catT[:, ec, :], ec == 0, [B_wout, B_scatT], [Bp], stop=(ec == 7))
                vop("dve", "scalar_tensor_tensor", [B_X[SI], Bp], [B_X[SI]], out=X[:, dc, scols], in0=X[:, dc, scols], scalar=ALPHA, in1=p[:, 0:NS], op0=ALU.mult, op1=ALU.add)
            layer_norm_fm(scols, NS, [B_X[SI]], 0, 8, xb[:, :, 0:NS], s_sq, [B_xb], [B_ssq], s_nm, s_lv, B_snm, B_slv)

            ck(6)
            P.barrier()
            banks = [(pD[0], B_pD[0]), (pD[1], B_pD[1]), (pT0, B_pT0), (pT1, B_pT1), (pS[0], B_pS[0]), (pS[1], B_pS[1]), (pY, B_pY)]
            pendB = []
            for ft in range(NFT):
                f0 = ft * FT
                blocks = [(f0 + b0, min(512, FT - b0), b0) for b0 in range(0, FT, 512)]
                xbufs = [B_X[j] for j in range(ft * (FT // TA), (ft + 1) * (FT // TA))]
                if ft == NFT - 1:
                    blocks.append((TP, NS, FT))
                    xbufs = xbufs + [B_X[NTA]]
                for (c0, n, o0) in blocks:
                    act(x1b[:, :, o0:o0 + n], X[:, :, c0:c0 + n], AF.Copy, xbufs, [B_x1b])
                bi = 0
                for g in range(16):
                    if g >= 2 and pendB:
                        pendB.pop(0)()
                    wb = w1b[g % 2]; Bw = B_w1b[g % 2]
                    dma("pool", wb, w1[l, g // 2, :, :, (g % 2) * 256:(g % 2) * 256 + 256], [], [Bw])
                    for f2 in range(2):
                        fc = g * 2 + f2
                        for (c0, n, o0) in blocks:
                            p, Bp = banks[bi % 5]; bi += 1
                            for kc in range(8):
                                mm(p[:, 0:n], wb[:, kc, f2 * 128:(f2 + 1) * 128], x1b[:, kc, o0:o0 + n], kc == 0, [Bw, B_x1b], [Bp], stop=(kc == 7))
                            r_ = rt[bi % 2]; Br = B_rt[bi % 2]
                            act(r_[:, 0:n], p[:, 0:n], AF.Relu, [Bp], [Br])
                            vop("dve", "tensor_tensor", [Br], [B_hT], out=hT[:, fc, o0:o0 + n], in0=r_[:, 0:n], in1=r_[:, 0:n], op=ALU.mult)
                while pendB:
                    pendB.pop(0)()
                for dc in range(8):
                    wb = w2b[dc % 2]; Bw = B_w2b[dc % 2]
                    dma("pool", wb, w2[l, dc], [], [Bw])
                    for (c0, n, o0) in blocks:
                        p, Bp = banks[bi % 5]; bi += 1
                        for fc in range(32):
                            mm(p[:, 0:n], wb[:, fc, :], hT[:, fc, o0:o0 + n], fc == 0, [Bw, B_hT], [Bp], stop=(fc == 31))
                        vop("dve", "scalar_tensor_tensor", xbufs + [Bp], xbufs, out=X[:, dc, c0:c0 + n], in0=X[:, dc, c0:c0 + n], scalar=ALPHA, in1=p[:, 0:n], op0=ALU.mult, op1=ALU.add)
                if ft == NFT - 1 and not last and can_prefetch:
                    nl = len(P.live_dma)
                    for kc in range(8):
                        dma("pool", winb[:, kc, :], win[l + 1, :, kc, :], [], [B_win, B_hT])
                    dma("pool", woutb, wout[l + 1], [], [B_wout, B_hT])
                    del P.live_dma[nl:]
                for (c0, n, o0) in blocks:
                    pcs = layer_norm_pieces(slice(c0, c0 + n), n, xbufs, 16, 24, lnx[:, :, 0:n], lnq[:, :, 0:n], [B_w2b[1]], [B_w2b[0]], nmeanB, lvarB, B_nmB, B_lvB,
                                            bk0=(pS[1], B_pS[1]), bk1=(pY, B_pY))
                    if ft < NFT - 1:
                        pendB.extend(pcs)
                    else:
                        for pc in pcs:
                            pc()
            P.barrier()
            ck(7)
            if not last and use_cc:
                dma("sp", cc_x_in.rearrange("p (k t) -> p k t", k=8), X[:, :, TP - 128:TP], [B_X[NTA - 1]], [B_ccx])
                P.op("pool", lambda e: e.collective_compute("AllGather", ALU.bypass, replica_groups=[[0, 1, 2, 3], [4, 5, 6, 7]],
                                                            ins=[cc_x_in], outs=[cc_x_out]), r=[B_ccx], w=[B_ccx_o], dma=True, inc=1)
                dma("sp", gx, cc_x_out.rearrange("(r p) f -> p r f", p=128), [B_ccx_o], [B_gx])
                xh2 = xh[:].rearrange("p k t -> p (k t)")
                vop("dve", "tensor_scalar", [B_gx, B_c], [B_xh], out=xh2, in0=gx[:, 0, :], scalar1=sel[:, 4:5], scalar2=None, op0=ALU.mult)
                for r_ in range(1, 4):
                    vop("dve", "scalar_tensor_tensor", [B_gx, B_c, B_xh], [B_xh], out=xh2, in0=gx[:, r_, :], scalar=sel[:, 4 + r_:5 + r_], in1=xh2, op0=ALU.mult, op1=ALU.add)

          except _Stop:
            break
        for i in range(NTA):
            odma(yT[:, :, i * TA:(i + 1) * TA], X[:, :, i * TA:(i + 1) * TA], [B_X[i]])
        odma(yT[:, :, TP:TX], X[:, :, TP:TX], [B_X[NTA]])
        P.op("sp", None, w=outbufs + [B_kS, B_vS])
        P.emit(st)
        P.stats.update(carve_stats)
    return nc, P


def _perm_cols():
    r = np.arange
    return np.concatenate([r(0, 256), r(512, 576), r(640, 704), r(576, 640), r(704, 768), r(768, 896), r(1536, 2560),
                           r(256, 512), r(896, 1024), r(2560, 2568), r(1024, 1536)])


def _perm_rows():
    r = np.arange
    return np.concatenate([r(0, 256), 256 + r(0, 64), 256 + r(128, 192), 256 + r(64, 128), 256 + r(192, 256), r(512, 1024)])


def prep_shared(inp, L):
    f = np.float32
    w_in = np.asarray(inp["w_in"], f)[:L]
    win = np.ascontiguousarray(w_in[:, :, _perm_cols()].reshape(L, 8, 128, DIN).transpose(0, 2, 1, 3))
    w_out = np.asarray(inp["w_out"], f)[:L]
    wout = np.ascontiguousarray(w_out[:, _perm_rows(), :].reshape(L, 8, 128, D).transpose(0, 2, 1, 3))
    w1 = np.ascontiguousarray(np.asarray(inp["w1"], f)[:L].reshape(L, 8, 128, 8, 512).transpose(0, 3, 2, 1, 4))
    w2 = np.ascontiguousarray(np.asarray(inp["w2"], f)[:L].reshape(L, 32, 128, 8, 128).transpose(0, 3, 2, 1, 4))
    rowp = np.zeros((L, 1, NROW), f)
    rowp[:, 0, 0:256] = inp["ln_v_g"][:L]
    rowp[:, 0, 256:512] = inp["ln_v_b"][:L]
    rowp[:, 0, 512:1024] = inp["gn_w"][:L]
    rowp[:, 0, 1024:1032] = inp["dt_bias"][:L]
    rowp[:, 0, 1032:1040] = inp["a_log"][:L]
    rowp[:, 0, 1040:1044] = np.asarray(inp["sinks"])[:L][:, [0, 2, 1, 3]]
    rowp[:, 0, 1044:1052] = inp["d_skip"][:L]
    rowp[:, 0, 1052:1056] = np.asarray(inp["w_s"])[:L, :, 0, 0]
    rowp[:, 0, 1056:1060] = np.asarray(inp["b_s"])[:L, :, 0]
    colp = np.zeros((L, 128, NCOLP), f)
    for k, name in enumerate(("ln1_g", "ln1_b", "ln2_g", "ln2_b")):
        colp[:, :, 8 * k:8 * k + 8] = np.asarray(inp[name], f)[:L].reshape(L, 8, 128).transpose(0, 2, 1)
    cw = np.asarray(inp["conv_w"], f)[:L].reshape(L, 4, 8, 128)
    colp[:, :, 32:64] = cw.transpose(0, 3, 2, 1).reshape(L, 128, 32)
    colp[:, :, 64:72] = np.asarray(inp["conv_b"], f)[:L].reshape(L, 8, 128).transpose(0, 2, 1)
    dsk = np.repeat(np.asarray(inp["d_skip"], f)[:L], 64, axis=1)
    colp[:, :, 72:76] = dsk.reshape(L, 4, 128).transpose(0, 2, 1)
    wsT = np.ascontiguousarray(np.asarray(inp["w_s"], f)[:L].transpose(0, 3, 1, 2))
    bsd = np.ascontiguousarray(np.asarray(inp["b_s"], f)[:L])
    s_ = np.arange(128)[:, None]
    t_ = np.arange(128)[None, :]
    cst_f = np.stack([np.eye(128, dtype=f), (s_ <= t_).astype(f), np.ones((128, 128), f)], axis=1)
    prev = np.where(s_ > t_, 0.0, NEG).astype(f)
    cur = np.where(s_ <= t_, 0.0, NEG).astype(f)
    negA = np.concatenate([prev, cur, prev, cur], axis=1)
    allneg = np.full((128, 128), NEG, f)
    negA_first = np.concatenate([allneg, cur, allneg, cur], axis=1)
    negS = np.concatenate([cur] * 4, axis=1)
    return dict(win=win, wout=wout, w1=w1, w2=w2, rowp=rowp, colp=colp, wsT=wsT, bsd=bsd, cst_f=cst_f), (negA, negA_first, negS)


def prep_core(inp, shared, masks, c, L, NT):
    f = np.float32
    TP = NT * 128
    b, p = c // 4, c % 4
    t0 = p * TP
    xp = np.asarray(inp["x_prompt"], f)
    xs = np.asarray(inp["x_sample"], f)
    tok = np.concatenate([xp[b, t0:t0 + TP], xs[c * NS:(c + 1) * NS, 0]], axis=0)
    xT0 = np.ascontiguousarray(tok.T.reshape(8, 128, TP + NS).transpose(1, 0, 2))
    if p == 0:
        xh0 = np.zeros((128, 8, 128), f)
    else:
        xh0 = np.ascontiguousarray(xp[b, t0 - 128:t0].T.reshape(8, 128, 128).transpose(1, 0, 2))
    negA, negA_first, negS = masks
    cst_m = np.stack([negA, negA_first if p == 0 else negA, negS], axis=1)
    selp = np.zeros((128, 8), f)
    for r in range(3):
        selp[:, r] = 1.0 if r < p else 0.0
    for r in range(4):
        selp[:, 4 + r] = 1.0 if r == p - 1 else 0.0
    sl = slice(c * NS, (c + 1) * NS)
    d = dict(shared)
    d.update(xT0=xT0, xh0=xh0, cst_m=np.ascontiguousarray(cst_m), selp=selp,
             st_k=np.ascontiguousarray(np.asarray(inp["state_attn_k"], f)[:L, sl].reshape(L, NS, 128, 128)),
             st_v=np.ascontiguousarray(np.asarray(inp["state_attn_v"], f)[:L, sl].reshape(L, NS, 128, 128)),
             st_c=np.ascontiguousarray(np.asarray(inp["state_conv"], f)[:L, sl].reshape(L, NS * 3, 1024)),
             st_h=np.ascontiguousarray(np.asarray(inp["state_ssm"], f)[:L, sl].reshape(L, NS * 8, 64 * 128)))
    return d


def assemble(res, L, NT):
    f = np.float32
    TP = NT * 128
    S = 4 * TP
    yp = np.zeros((2, S, D), f); ys = np.zeros((128, 1, D), f)
    kp = np.zeros((L, 2, 128, 2, 64), f); vp = np.zeros_like(kp)
    cp = np.zeros((L, 2, 3, 1024), f); hp = np.zeros((L, 2, 8, 64, 128), f)
    ks = np.zeros((L, 128, 1, 2, 64), f); vs = np.zeros_like(ks)
    cs = np.zeros((L, 128, 3, 1024), f); hs = np.zeros((L, 128, 8, 64, 128), f); vns = np.zeros((L, 128, 1, 256), f)
    for c in range(NCORES):
        r = res[c]
        b, p = c // 4, c % 4
        tok = np.asarray(r["yT"]).transpose(1, 0, 2).reshape(D, TP + NS).T
        yp[b, p * TP:(p + 1) * TP] = tok[:TP]
        sl = slice(c * NS, (c + 1) * NS)
        ys[sl, 0] = tok[TP:]
        if p == 3:
            kp[:, b] = np.asarray(r["kP"]).reshape(L, 128, 2, 64)
            vp[:, b] = np.asarray(r["vP"]).reshape(L, 128, 2, 64)
            cp[:, b] = np.asarray(r["cP"])
            hp[:, b] = np.asarray(r["hP"]).reshape(L, 128, 8, 64).transpose(0, 2, 3, 1)
        ks[:, sl, 0] = np.asarray(r["kS"]).reshape(L, NS, 2, 64)
        vs[:, sl, 0] = np.asarray(r["vS"]).reshape(L, NS, 2, 64)
        cs[:, sl] = np.asarray(r["cS"])
        hs[:, sl] = np.asarray(r["hS"]).reshape(L, NS, 8, 64, 128)
        vns[:, sl, 0] = np.asarray(r["vnS"])
    return (yp, ys, kp, vp, cp, hp, ks, vs, cs, hs, vns)


_CACHE = {}


def run(inp, L, NT, use_cc=True, trace=False):
    key = (L, NT, use_cc)
    if key not in _CACHE:
        _CACHE[key] = build(L, NT, use_cc)
    nc, P = _CACHE[key]
    shared, masks = prep_shared(inp, L)
    in_maps = [prep_core(inp, shared, masks, c, L, NT) for c in range(NCORES)]
    res = run_bass_kernel_spmd(nc, in_maps, core_ids=list(range(NCORES)), trace=trace)
    return assemble(res.results, L, NT), res


def kernel(**inputs):
    out, _ = run(inputs, DEPTH, SEQ // (4 * 128))
    return out
```
